# Optimizing a Trainium2 kernel written in Bass

```python
import jax, jax.numpy as jnp
from jax import lax
import numpy as np

D_MODEL = 2048
BATCH = 4
SEQ = 4096
DEPTH = 2

A_HEADS = 8
A_HEAD_DIM = 128
A_ROT_DIM = A_HEAD_DIM // 4
IDX_HEADS = 16
IDX_DIM = 64
IDX_ROT_DIM = IDX_DIM // 4
TOPK_MAX = 256
MLA_HEADS = 8
MLA_NOPE = 128
MLA_ROPE = 64
MLA_V = 128
Q_LORA = 512
KV_LORA = 256
FFN_DIM = -(-8 * D_MODEL // (3 * 256)) * 256
ROPE_THETA = 500000.0
MLA_ROPE_THETA = 10000.0
ALPHA = (2 * DEPTH) ** 0.25
BETA = (8 * DEPTH) ** -0.25
LN_EPS = 1e-5
RMS_EPS = 1e-6
Q_BLOCK = 128
SPARSE_Q_BLOCK = 64
A_WIDTH = A_HEADS * A_HEAD_DIM
IN_SIZES = (A_WIDTH, A_WIDTH, A_WIDTH, IDX_HEADS * IDX_DIM, IDX_HEADS, IDX_DIM, Q_LORA, KV_LORA, MLA_ROPE)
IN_COLS = sum(IN_SIZES)

kernel_name = "hybrid_dsa_mla_deepnorm_block"


def _layer_norm(x, g, b):
    xf = x.astype(jnp.float32)
    mu = jnp.mean(xf, axis=-1, keepdims=True)
    var = jnp.mean(jnp.square(xf - mu), axis=-1, keepdims=True)
    y = (xf - mu) * lax.rsqrt(var + LN_EPS) * g.astype(jnp.float32) + b.astype(jnp.float32)
    return y.astype(x.dtype)


def _rms_norm(x, g):
    xf = x.astype(jnp.float32)
    y = xf * lax.rsqrt(jnp.mean(jnp.square(xf), axis=-1, keepdims=True) + RMS_EPS) * g.astype(jnp.float32)
    return y.astype(x.dtype)


def _rope(x, pos, rot_dim, theta):
    half = rot_dim // 2
    inv_freq = theta ** (-2.0 * jnp.arange(half, dtype=jnp.float32) / rot_dim)
    ang = pos.astype(jnp.float32)[..., None] * inv_freq
    cos = jnp.cos(ang)[:, :, None, :].astype(x.dtype)
    sin = jnp.sin(ang)[:, :, None, :].astype(x.dtype)
    x1, x2 = x[..., :half], x[..., half:rot_dim]
    return jnp.concatenate([x1 * cos - x2 * sin, x2 * cos + x1 * sin, x[..., rot_dim:]], axis=-1)


def _to_blocks(a, blk):
    b, s = a.shape[:2]
    return jnp.moveaxis(a.reshape((b, s // blk, blk) + a.shape[2:]), 1, 0)


def _from_blocks(a):
    n, b, blk = a.shape[:3]
    return jnp.moveaxis(a, 0, 1).reshape((b, n * blk) + a.shape[3:])


def _dsa_attention(q, k, v, q_idx, k_idx, w_idx):
    s_len = q.shape[1]
    topk = min(TOPK_MAX, s_len // 4)
    pos_s = jnp.arange(s_len, dtype=jnp.int32)
    scale = A_HEAD_DIM ** -0.5

    def block(args):
        qb, qib, wib, tb = args
        rel = jax.nn.relu(jnp.einsum('bthd,bsd->bths', qib, k_idx).astype(jnp.float32))
        score = jnp.einsum('bths,bth->bts', rel, wib.astype(jnp.float32))
        causal = tb[:, None] >= pos_s[None, :]
        score = jnp.where(causal[None], score, -jnp.inf)
        _, sel = lax.top_k(score, topk)
        k_sel = jax.vmap(lambda kb, ib: kb[ib])(k, sel)
        v_sel = jax.vmap(lambda vb, ib: vb[ib])(v, sel)
        logits = jnp.einsum('bthd,btkhd->bthk', qb, k_sel).astype(jnp.float32) * scale
        valid = (sel <= tb[None, :, None])[:, :, None, :]
        p = jax.nn.softmax(jnp.where(valid, logits, -jnp.inf), axis=-1)
        return jnp.einsum('bthk,btkhd->bthd', p.astype(v.dtype), v_sel)

    out = lax.map(block, (_to_blocks(q, SPARSE_Q_BLOCK), _to_blocks(q_idx, SPARSE_Q_BLOCK),
                          _to_blocks(w_idx, SPARSE_Q_BLOCK), pos_s.reshape(-1, SPARSE_Q_BLOCK)))
    return _from_blocks(out)


def _causal_attention(q, k, v, scale):
    s_len = q.shape[1]
    pos_s = jnp.arange(s_len, dtype=jnp.int32)

    def block(args):
        qb, tb = args
        logits = jnp.einsum('bthd,bshd->bhts', qb, k).astype(jnp.float32) * scale
        mask = tb[:, None] >= pos_s[None, :]
        p = jax.nn.softmax(jnp.where(mask[None, None], logits, -jnp.inf), axis=-1)
        return jnp.einsum('bhts,bshd->bthd', p.astype(v.dtype), v)

    out = lax.map(block, (_to_blocks(q, Q_BLOCK), pos_s.reshape(-1, Q_BLOCK)))
    return _from_blocks(out)


def _hybrid_mixer(x, positions, w_in, g_cq, g_ckv, w_uq, w_ukv, w_o):
    b, s, _ = x.shape
    proj = jnp.einsum('bsd,de->bse', x, w_in)
    offsets = np.cumsum(IN_SIZES)[:-1].tolist()
    qa, ka, va, qi, wi, ki, cq, ckv, kr = jnp.split(proj, offsets, axis=-1)
    qa = _rope(qa.reshape(b, s, A_HEADS, A_HEAD_DIM), positions, A_ROT_DIM, ROPE_THETA)
    ka = _rope(ka.reshape(b, s, A_HEADS, A_HEAD_DIM), positions, A_ROT_DIM, ROPE_THETA)
    va = va.reshape(b, s, A_HEADS, A_HEAD_DIM)
    qi = _rope(qi.reshape(b, s, IDX_HEADS, IDX_DIM), positions, IDX_ROT_DIM, ROPE_THETA)
    ki = _rope(ki[:, :, None, :], positions, IDX_ROT_DIM, ROPE_THETA)[:, :, 0, :]
    wi = wi * (IDX_HEADS ** -0.5 * IDX_DIM ** -0.5)
    out_a = _dsa_attention(qa, ka, va, qi, ki, wi).reshape(b, s, A_WIDTH)
    q_b = jnp.einsum('bsr,re->bse', _rms_norm(cq, g_cq), w_uq).reshape(b, s, MLA_HEADS, MLA_NOPE + MLA_ROPE)
    q_nope = q_b[..., :MLA_NOPE]
    q_pe = _rope(q_b[..., MLA_NOPE:], positions, MLA_ROPE, MLA_ROPE_THETA)
    kv = jnp.einsum('bsr,re->bse', _rms_norm(ckv, g_ckv), w_ukv).reshape(b, s, MLA_HEADS, MLA_NOPE + MLA_V)
    k_nope, v_b = kv[..., :MLA_NOPE], kv[..., MLA_NOPE:]
    k_pe = _rope(kr[:, :, None, :], positions, MLA_ROPE, MLA_ROPE_THETA)
    k_pe = jnp.broadcast_to(k_pe, (b, s, MLA_HEADS, MLA_ROPE))
    q_mla = jnp.concatenate([q_nope, q_pe], axis=-1)
    k_mla = jnp.concatenate([k_nope, k_pe], axis=-1)
    out_b = _causal_attention(q_mla, k_mla, v_b, (MLA_NOPE + MLA_ROPE) ** -0.5).reshape(b, s, MLA_HEADS * MLA_V)
    return jnp.einsum('bse,ed->bsd', jnp.concatenate([out_a, out_b], axis=-1), w_o)


def _swiglu(x, w_gate, w_up, w_down):
    h = jax.nn.silu(jnp.einsum('bsd,df->bsf', x, w_gate)) * jnp.einsum('bsd,df->bsf', x, w_up)
    return jnp.einsum('bsf,fd->bsd', h, w_down)


def setup_inputs(seed: int = 0) -> dict:
    key = jax.random.key(seed)
    ks = jax.random.split(key, 16)
    f32 = jnp.float32
    nrm = lambda k, shape, scale: jax.random.normal(k, shape, f32) * scale
    mix_width = A_WIDTH + MLA_HEADS * MLA_V
    return {
        "x": jax.random.normal(ks[0], (BATCH, SEQ, D_MODEL), f32),
        "positions": jnp.broadcast_to(jnp.arange(SEQ, dtype=jnp.int32), (BATCH, SEQ)),
        "w_in": nrm(ks[1], (DEPTH, D_MODEL, IN_COLS), D_MODEL ** -0.5),
        "g_cq": 1.0 + nrm(ks[2], (DEPTH, Q_LORA), 0.02),
        "g_ckv": 1.0 + nrm(ks[3], (DEPTH, KV_LORA), 0.02),
        "w_uq": nrm(ks[4], (DEPTH, Q_LORA, MLA_HEADS * (MLA_NOPE + MLA_ROPE)), Q_LORA ** -0.5),
        "w_ukv": nrm(ks[5], (DEPTH, KV_LORA, MLA_HEADS * (MLA_NOPE + MLA_V)), KV_LORA ** -0.5),
        "w_o": nrm(ks[6], (DEPTH, mix_width, D_MODEL), BETA * mix_width ** -0.5),
        "ln1_g": 1.0 + nrm(ks[7], (DEPTH, D_MODEL), 0.02),
        "ln1_b": nrm(ks[8], (DEPTH, D_MODEL), 0.02),
        "w_gate": nrm(ks[9], (DEPTH, D_MODEL, FFN_DIM), D_MODEL ** -0.5),
        "w_up": nrm(ks[10], (DEPTH, D_MODEL, FFN_DIM), D_MODEL ** -0.5),
        "w_down": nrm(ks[11], (DEPTH, FFN_DIM, D_MODEL), BETA * FFN_DIM ** -0.5),
        "ln2_g": 1.0 + nrm(ks[12], (DEPTH, D_MODEL), 0.02),
        "ln2_b": nrm(ks[13], (DEPTH, D_MODEL), 0.02),
    }


def reference(x, positions, w_in, g_cq, g_ckv, w_uq, w_ukv, w_o, ln1_g, ln1_b,
              w_gate, w_up, w_down, ln2_g, ln2_b):
    for i in range(DEPTH):
        mix = _hybrid_mixer(x, positions, w_in[i], g_cq[i], g_ckv[i], w_uq[i], w_ukv[i], w_o[i])
        x = _layer_norm(ALPHA * x + mix, ln1_g[i], ln1_b[i])
        ffn = _swiglu(x, w_gate[i], w_up[i], w_down[i])
        x = _layer_norm(ALPHA * x + ffn, ln2_g[i], ln2_b[i])
    return x
```

```python
import numpy as np
from contextlib import ExitStack
import concourse.bass as bass
import concourse.mybir as mybir
from concourse.bass_utils import run_bass_kernel_spmd

F32 = mybir.dt.float32
BF16 = mybir.dt.bfloat16
I32 = mybir.dt.int32
ALU = mybir.AluOpType
AF = mybir.ActivationFunctionType

S = 4096
D = 2048
DEPTH = 2
FF = 5632
NFC = FF // 128
G = 512
NG = S // G
ALPHA = float((2 * DEPTH) ** 0.25)
LN_EPS = 1e-5
RMS_EPS = 1e-6
NEG = -1.0e30
MBIG = 30000.0
NFM = 44
ARENA = 50 * 1024


def _dsize(dt):
    return 4 if dt in (F32, I32) else 2


class TR:
    NQ = 8

    def __init__(self, nc, es):
        self.nc = nc
        self.E = {'pe': nc.tensor, 'act': nc.scalar, 'dve': nc.vector, 'pool': nc.gpsimd, 'sp': nc.sync}
        self.sem = {k: es.enter_context(nc.semaphore('s_' + k)) for k in ('pe', 'act', 'dve', 'pool')}
        self.cnt = {k: 0 for k in self.sem}
        self.dq = {}
        for q in ('sp', 'pool', 'act'):
            self.dq[q] = {'sems': [es.enter_context(nc.semaphore('d_%s%d' % (q, i))) for i in range(self.NQ)], 'j': 0}
        self.seen = {e: {} for e in self.E}
        self.lastw = {}
        self.readers = {}
        self.arena = es.enter_context(nc.sbuf_tensor('arena', [128, ARENA], F32))
        self.off = 0
        self.ps = [es.enter_context(nc.psum_tensor('ps%d' % i, [128, 512], F32)) for i in range(8)]

    def reset(self):
        self.off = 0

    def alloc(self, n, dt):
        nb = n * _dsize(dt)
        n4 = (nb + 31) // 32 * 8
        ap = self.arena[:, self.off:self.off + n4]
        self.off += n4
        assert self.off <= ARENA, ('arena overflow', self.off)
        if dt != F32:
            ap = ap.bitcast(dt)
        return ap[:, 0:n]

    def _handle(self, key):
        if key[0] == 'c':
            return self.sem[key[1]]
        return self.dq[key[1]]['sems'][key[2]]

    def _wait(self, e, ev):
        if ev is None:
            return
        key, val = ev
        if key == ('c', e) and e == 'pe':
            return
        if self.seen[e].get(key, 0) >= val:
            return
        self.E[e].wait_ge(self._handle(key), val)
        self.seen[e][key] = val

    def _deps(self, reads, writes):
        out = []
        for t in reads:
            ev = self.lastw.get(t)
            if ev is not None:
                out.append(ev)
        for t in writes:
            ev = self.lastw.get(t)
            if ev is not None:
                out.append(ev)
            for k, v in self.readers.get(t, {}).items():
                out.append((k, v))
        return out

    def _record(self, ev, reads, writes):
        for t in reads:
            r = self.readers.setdefault(t, {})
            if r.get(ev[0], 0) < ev[1]:
                r[ev[0]] = ev[1]
        for t in writes:
            self.lastw[t] = ev
            self.readers[t] = {}

    def op(self, e, fn, reads=(), writes=(), inc=True):
        for ev in self._deps(reads, writes):
            self._wait(e, ev)
        ins = fn(self.E[e])
        if inc:
            self.cnt[e] += 1
            ins.then_inc(self.sem[e], 1)
            ev = (('c', e), self.cnt[e])
        else:
            ev = (('c', e), self.cnt[e] + 1)
        self._record(ev, reads, writes)

    def dma(self, q, out, in_, reads=(), writes=()):
        dq = self.dq[q]
        j = dq['j']
        i = j % self.NQ
        if j >= self.NQ:
            self._wait(q, (('d', q, i), 16 * (j // self.NQ)))
        for ev in self._deps(reads, writes):
            self._wait(q, ev)
        self.E[q].dma_start(out=out, in_=in_).then_inc(dq['sems'][i], 16)
        dq['j'] += 1
        ev = (('d', q, i), 16 * (j // self.NQ + 1))
        self._record(ev, reads, writes)

    def barrier(self):
        evs = [(('c', k), self.cnt[k]) for k in self.cnt if self.cnt[k] > 0]
        for q, dq in self.dq.items():
            for i in range(self.NQ):
                n = (dq['j'] - i + self.NQ - 1) // self.NQ
                if n > 0:
                    evs.append((('d', q, i), 16 * n))
        for e in self.E:
            for ev in evs:
                self._wait(e, ev)
        self.lastw = {}
        self.readers = {}


A_OFF, K_OFF, V_OFF, QI_OFF, WI_OFF, KI_OFF, CQ_OFF, CKV_OFF, KR_OFF = 0, 1024, 2048, 3072, 4096, 4112, 4176, 4688, 4944


def _fm_chunk_cols():
    chunks = []

    def rot_pair(base, hd, rot, heads):
        half = rot // 2
        a, b = [], []
        for h in heads:
            o = base + h * hd
            a += list(range(o, o + rot))
            b += list(range(o + half, o + rot)) + list(range(o, o + half))
        return a, b

    for base in (A_OFF, K_OFF):
        for c in range(2):
            a, b = rot_pair(base, 128, 32, range(4 * c, 4 * c + 4))
            chunks += [a, b]
        for h in range(8):
            chunks.append(list(range(base + h * 128 + 32, base + h * 128 + 128)))
    for c in range(2):
        a, b = rot_pair(QI_OFF, 64, 16, range(8 * c, 8 * c + 8))
        chunks += [a, b]
    for j in range(8):
        cols = []
        for h in (2 * j, 2 * j + 1):
            cols += list(range(QI_OFF + h * 64 + 16, QI_OFF + h * 64 + 64))
        chunks.append(cols)
    for c in range(4):
        chunks.append(list(range(CQ_OFF + c * 128, CQ_OFF + (c + 1) * 128)))
    for c in range(2):
        chunks.append(list(range(CKV_OFF + c * 128, CKV_OFF + (c + 1) * 128)))
    s1 = list(range(KR_OFF, KR_OFF + 64)) + list(range(KI_OFF, KI_OFF + 64))
    s2 = list(range(KR_OFF + 32, KR_OFF + 64)) + list(range(KR_OFF, KR_OFF + 32)) + \
        list(range(KI_OFF + 8, KI_OFF + 16)) + list(range(KI_OFF, KI_OFF + 8))
    chunks += [s1, s2]
    assert len(chunks) == NFM
    return chunks


def _lhsT_tiles(w, col_chunks):
    K = w.shape[0]
    KC = K // 128
    out = np.zeros((len(col_chunks), 128, KC, 128), np.float32)
    wr = w.reshape(KC, 128, w.shape[1])
    for c, cols in enumerate(col_chunks):
        cols = np.asarray(cols)
        out[c, :, :, :len(cols)] = wr[:, :, cols].transpose(1, 0, 2)
    return out.reshape(len(col_chunks) * 128, KC * 128)


def _rhs_tiles(w, blocks):
    K = w.shape[0]
    KC = K // 128
    wr = w.reshape(KC, 128, w.shape[1])
    outs = []
    for cols in blocks:
        cols = np.asarray(cols)
        outs.append(np.ascontiguousarray(wr[:, :, cols].transpose(1, 0, 2)).reshape(128, KC * len(cols)))
    return np.concatenate(outs, axis=0)


WSHAPES = {
    'win_fm': (NFM * 128, 2048),
    'win_tm': (128, 16 * 1040),
    'wuq': (16 * 128, 512),
    'wukv_k': (8 * 128, 256),
    'wukv_v': (128, 2048),
    'wo': (4 * 128, 16 * 512),
    'wg': (NFC * 128, 2048),
    'wu': (NFC * 128, 2048),
    'wd': (8 * 128, NFC * 256),
}


def _prep_layer(w_in, w_uq, w_ukv, w_o, w_gate, w_up, w_down):
    o = {}
    o['win_fm'] = _lhsT_tiles(w_in, _fm_chunk_cols())
    tm_cols = list(range(V_OFF, V_OFF + 1024)) + list(range(WI_OFF, WI_OFF + 16))
    o['win_tm'] = _rhs_tiles(w_in, [tm_cols])
    uq = []
    for h in range(8):
        uq.append(list(range(h * 192, h * 192 + 128)))
    for c in range(4):
        a, b = [], []
        for h in (2 * c, 2 * c + 1):
            a += list(range(h * 192 + 128, h * 192 + 192))
            b += list(range(h * 192 + 160, h * 192 + 192)) + list(range(h * 192 + 128, h * 192 + 160))
        uq += [a, b]
    o['wuq'] = _lhsT_tiles(w_uq, uq)
    o['wukv_k'] = _lhsT_tiles(w_ukv, [list(range(h * 256, h * 256 + 128)) for h in range(8)])
    vcols = []
    for h in range(8):
        vcols += list(range(h * 256 + 128, h * 256 + 256))
    o['wukv_v'] = _rhs_tiles(w_ukv, [vcols])
    o['wo'] = _rhs_tiles(w_o, [list(range(b * 512, (b + 1) * 512)) for b in range(4)])
    o['wg'] = _lhsT_tiles(w_gate, [list(range(c * 128, (c + 1) * 128)) for c in range(NFC)])
    o['wu'] = _lhsT_tiles(w_up, [list(range(c * 128, (c + 1) * 128)) for c in range(NFC)])
    o['wd'] = _rhs_tiles(w_down, [list(range(b * 256, (b + 1) * 256)) for b in range(8)])
    for k, v in o.items():
        assert v.shape == WSHAPES[k], (k, v.shape)
    return o


def _consts():
    p = np.arange(128)
    cst = np.zeros((128, 8), np.float32)
    j = (p % 32) % 16
    cst[:, 0] = (np.float32(500000.0) ** (-2.0 * j.astype(np.float32) / 32)).astype(np.float32)
    cst[:, 1] = np.where((p % 32) < 16, -1.0, 1.0)
    j = (p % 16) % 8
    cst[:, 2] = (np.float32(500000.0) ** (-2.0 * j.astype(np.float32) / 16)).astype(np.float32)
    cst[:, 3] = np.where((p % 16) < 8, -1.0, 1.0)
    j = (p % 64) % 32
    cst[:, 4] = (np.float32(10000.0) ** (-2.0 * j.astype(np.float32) / 64)).astype(np.float32)
    cst[:, 5] = np.where((p % 64) < 32, -1.0, 1.0)
    ident = np.eye(128, dtype=np.float32)
    t = np.arange(128)[:, None]
    s = np.arange(128)[None, :]
    tri = np.where(s <= t, 0.0, NEG).astype(np.float32)
    cm = np.zeros((128, 4, 512), np.float32)
    for jj in range(4):
        cm[:, jj, :] = ((128 * jj + np.arange(128)[:, None]) <= np.arange(512)[None, :]).astype(np.float32)
    return {'cst': cst, 'ident': ident, 'tri': tri, 'cmask': cm.reshape(128, 2048)}


def build_program(dbg=()):
    nc = bass.Bass("TRN2", target_bir_lowering=False)
    es = ExitStack()
    tr = TR(nc, es)
    op, dma = tr.op, tr.dma

    def din(name, shape, dt=F32):
        return nc.dram_tensor(name, list(shape), dt, kind="ExternalInput").ap()

    def dscr(name, shape, dt):
        kind = "ExternalOutput" if name in dbg else "Internal"
        return nc.dram_tensor(name, list(shape), dt, kind=kind).ap()

    x_in = din('x', [S, D])
    pos_in = din('positions', [1, S], I32)
    out_d = nc.dram_tensor('out', [S, D], F32, kind="ExternalOutput").ap()
    w_in32 = {k: din(k, [DEPTH * v[0], v[1]]) for k, v in WSHAPES.items()}
    gq_in = din('g_cq_t', [DEPTH * 128, 4])
    gkv_in = din('g_ckv_t', [DEPTH * 128, 2])
    ln_in = {k: din(k, [DEPTH, D]) for k in ('ln1_g', 'ln1_b', 'ln2_g', 'ln2_b')}
    cst_in = din('cst', [128, 8])
    ident_in = din('ident', [128, 128])
    tri_in = din('tri', [128, 128])
    cmask_in = din('cmask', [128, 2048])

    wb = {k: dscr('b_' + k, [DEPTH * v[0], v[1]], BF16) for k, v in WSHAPES.items()}
    tab = dscr('tab', [6 * 128, S], F32)
    xa = dscr('xa', [S, D], F32)
    xb_d = dscr('xb', [S, D], F32)
    qaT = dscr('qaT', [8 * 128, S], BF16)
    kaT = dscr('kaT', [8 * 128, S], BF16)
    va_hp = dscr('va_hp', [8 * 128, 32 * 128], BF16)
    qiT = dscr('qiT', [16 * 64, S], BF16)
    kiT = dscr('kiT', [64, S], BF16)
    wi_d = dscr('wi', [S, 16], F32)
    qnT = dscr('qnT', [8 * 128, S], BF16)
    qpT = dscr('qpT', [8 * 64, S], BF16)
    knT = dscr('knT', [8 * 128, S], BF16)
    kpT = dscr('kpT', [64, S], BF16)
    vb_hp = dscr('vb_hp', [8 * 128, 32 * 128], BF16)
    aoT = dscr('aoT', [D, S], BF16)
    if 'dbg_p2' in dbg:
        dacc = dscr('dacc', [256, S], F32)
        dwork = dscr('dwork', [256, S], F32)
        dm8 = dscr('dm8', [256, 8], F32)
        dthr = dscr('dthr', [256, 1], F32)
        dmask = dscr('dmask', [256, S], BF16)

    def sb(name, shape, dt):
        return es.enter_context(nc.sbuf_tensor(name, shape, dt))

    identb = sb('identb', [128, 128], BF16)
    onesb = sb('onesb', [128, 128], BF16)
    onesf = sb('onesf', [128, 128], F32)
    cst = sb('cstt', [128, 8], F32)
    trineg = sb('trineg', [128, 128], F32)
    cmaskb = sb('cmaskb', [128, 2048], BF16)
    negbig = sb('negbig', [128, 1], F32)
    negm = sb('negm', [128, 1], F32)

    PS = [p[:] for p in tr.ps]
    PSB = [p[:].bitcast(BF16) for p in tr.ps]

    with es:
        tr.reset()
        tmpf = tr.alloc(2048, F32)
        dma('sp', tmpf[:, 0:128], ident_in, writes=['tmpf'])
        op('dve', lambda e: e.tensor_copy(out=identb[:], in_=tmpf[:, 0:128]), reads=['tmpf'], writes=['identb'])
        dma('sp', tmpf, cmask_in, writes=['tmpf'])
        op('dve', lambda e: e.tensor_scalar(out=cmaskb[:], in0=tmpf, scalar1=MBIG, scalar2=-MBIG, op0=ALU.mult, op1=ALU.add),
           reads=['tmpf'], writes=['cmaskb'])
        dma('sp', cst[:], cst_in, writes=['cst'])
        dma('sp', trineg[:], tri_in, writes=['trineg'])
        op('dve', lambda e: e.memset(onesb[:], 1.0), writes=['onesb'])
        op('dve', lambda e: e.memset(onesf[:], 1.0), writes=['onesf'])
        op('dve', lambda e: e.memset(negbig[:], -1.0e29), writes=['negbig'])
        op('dve', lambda e: e.memset(negm[:], -MBIG), writes=['negm'])

        def cast_weights(l):
            for k, (R, C) in WSHAPES.items():
                rows = max(128, (1 << 21) // C // 128 * 128)
                r = 0
                while r < R:
                    n = min(rows, R - r)
                    dma('pool', wb[k][l * R + r:l * R + r + n, :], w_in32[k][l * R + r:l * R + r + n, :],
                        writes=['W_%s_%d' % (k, l)])
                    r += n

        cast_weights(0)

        posi = tr.alloc(S, I32)
        posf = tr.alloc(S, F32)
        ang = tr.alloc(S, F32)
        tmp = tr.alloc(S, F32)
        tmpi = tr.alloc(S, I32)
        res = [tr.alloc(S, F32), tr.alloc(S, F32)]
        dma('sp', posi, pos_in[0:1, :].to_broadcast([128, S]), writes=['posi'])
        op('dve', lambda e: e.tensor_copy(out=posf, in_=posi), reads=['posi'], writes=['posf'])
        PI = float(np.pi)
        for ty in range(3):
            for which in range(2):
                rb = res[(2 * ty + which) % 2]
                rt = 'res%d' % ((2 * ty + which) % 2)
                shift = PI / 2 if which == 0 else 0.0
                op('dve', lambda e: e.tensor_scalar(out=ang, in0=posf, scalar1=cst[:, 2 * ty:2 * ty + 1], scalar2=shift,
                                                    op0=ALU.mult, op1=ALU.add), reads=['posf', 'cst'], writes=['ang'])
                op('dve', lambda e: e.tensor_scalar(out=tmp, in0=ang, scalar1=float(1 / (2 * PI)), scalar2=None, op0=ALU.mult),
                   reads=['ang'], writes=['tmp'])
                op('dve', lambda e: e.tensor_copy(out=tmpi, in_=tmp), reads=['tmp'], writes=['tmpi'])
                op('dve', lambda e: e.tensor_copy(out=tmp, in_=tmpi), reads=['tmpi'], writes=['tmp'])
                op('dve', lambda e: e.scalar_tensor_tensor(out=ang, in0=tmp, scalar=float(-2 * PI), in1=ang, op0=ALU.mult, op1=ALU.add),
                   reads=['tmp', 'ang'], writes=['ang'])
                op('dve', lambda e: e.tensor_scalar(out=tmp, in0=ang, scalar1=PI, scalar2=float(-2 * PI), op0=ALU.is_gt, op1=ALU.mult),
                   reads=['ang'], writes=['tmp'])
                op('dve', lambda e: e.tensor_tensor(out=ang, in0=ang, in1=tmp, op=ALU.add), reads=['ang', 'tmp'], writes=['ang'])
                op('dve', lambda e: e.tensor_scalar(out=tmp, in0=ang, scalar1=-PI, scalar2=float(2 * PI), op0=ALU.is_lt, op1=ALU.mult),
                   reads=['ang'], writes=['tmp'])
                op('dve', lambda e: e.tensor_tensor(out=ang, in0=ang, in1=tmp, op=ALU.add), reads=['ang', 'tmp'], writes=['ang'])
                op('dve', lambda e: e.tensor_scalar(out=ang, in0=ang, scalar1=PI, scalar2=-PI, op0=ALU.min, op1=ALU.max),
                   reads=['ang'], writes=['ang'])
                op('act', lambda e: e.activation(out=rb, in_=ang, func=AF.Sin), reads=['ang'], writes=[rt])
                if which == 1:
                    op('dve', lambda e: e.tensor_scalar(out=rb, in0=rb, scalar1=cst[:, 2 * ty + 1:2 * ty + 2], scalar2=None, op0=ALU.mult),
                       reads=[rt, 'cst'], writes=[rt])
                k = 2 * ty + which
                dma('sp', tab[k * 128:(k + 1) * 128, :], rb, reads=[rt])
        tr.barrier()
        cast_weights(1)

        def layer(l, x_src, x_dst):
            def W(k, r0, r1):
                R = WSHAPES[k][0]
                return wb[k][l * R + r0:l * R + r1, :]

            tr.reset()
            wtm = tr.alloc(16 * 1040, BF16).rearrange("p (k n) -> p k n", k=16)
            xf = [tr.alloc(D, F32) for _ in range(2)]
            xbf = [tr.alloc(D, BF16) for _ in range(2)]
            xT = [tr.alloc(16 * G, BF16).rearrange("p (k n) -> p k n", k=16) for _ in range(2)]
            wfm = [tr.alloc(16 * 128, BF16).rearrange("p (k n) -> p k n", k=16) for _ in range(3)]
            tabs = {k: tr.alloc(G, F32) for k in ('a_c', 'a_s', 'i_c', 'i_s', 'm_c', 'm_s', 's_c', 's_s')}
            t1 = [tr.alloc(G, F32) for _ in range(2)]
            t2 = [tr.alloc(G, F32) for _ in range(2)]
            ob = [tr.alloc(G, BF16) for _ in range(4)]
            cqf = tr.alloc(4 * G, F32).rearrange("p (k n) -> p k n", k=4)
            ckvf = tr.alloc(2 * G, F32).rearrange("p (k n) -> p k n", k=2)
            sq = [tr.alloc(G, F32) for _ in range(2)]
            rstd = {'q': tr.alloc(G, F32), 'kv': tr.alloc(G, F32)}
            cqn = tr.alloc(4 * G, BF16).rearrange("p (k n) -> p k n", k=4)
            ckvn = tr.alloc(2 * G, BF16).rearrange("p (k n) -> p k n", k=2)
            wuq = tr.alloc(16 * 512, BF16).rearrange("p (c k n) -> p c k n", c=16, k=4)
            wukk = tr.alloc(8 * 256, BF16).rearrange("p (c k n) -> p c k n", c=8, k=2)
            wukv = tr.alloc(2 * 1024, BF16).rearrange("p (k n) -> p k n", k=2)
            gq = tr.alloc(4, F32)
            gkv = tr.alloc(2, F32)
            vtm = [tr.alloc(1024, BF16) for _ in range(2)]
            wio = [tr.alloc(16, F32) for _ in range(2)]

            dma('sp', wtm, W('win_tm', 0, 128).rearrange("p (k n) -> p k n", k=16), reads=['W_win_tm_%d' % l], writes=['wtm'])
            for c in range(16):
                dma('sp', wuq[:, c], W('wuq', c * 128, (c + 1) * 128).rearrange("p (k n) -> p k n", k=4),
                    reads=['W_wuq_%d' % l], writes=['wuq'])
            for c in range(8):
                dma('sp', wukk[:, c], W('wukv_k', c * 128, (c + 1) * 128).rearrange("p (k n) -> p k n", k=2),
                    reads=['W_wukv_k_%d' % l], writes=['wukk'])
            dma('sp', wukv, W('wukv_v', 0, 128).rearrange("p (k n) -> p k n", k=2), reads=['W_wukv_v_%d' % l], writes=['wukv'])
            dma('sp', gq, gq_in[l * 128:(l + 1) * 128, :], writes=['gq'])
            dma('sp', gkv, gkv_in[l * 128:(l + 1) * 128, :], writes=['gkv'])
            op('pool', lambda e: e.memset(tabs['s_c'][64:128, :], 1.0), writes=['s_c'])
            op('pool', lambda e: e.memset(tabs['s_s'][64:128, :], 0.0), writes=['s_s'])

            bankrr = [0]

            def nbank(lo=2, n=5):
                b = lo + bankrr[0] % n
                bankrr[0] += 1
                return b

            wcnt = [0]

            def load_wfm(ci):
                b = wcnt[0] % 3
                wcnt[0] += 1
                dma('sp', wfm[b], W('win_fm', ci * 128, (ci + 1) * 128).rearrange("p (k n) -> p k n", k=16),
                    reads=['W_win_fm_%d' % l], writes=['wfm%d' % b])
                return b

            cnt = {'ob': 0, 't': 0, 'sq': 0, 'v': 0}

            def store_rows(src, srct, dsts, ts):
                for (p0, p1, dr) in dsts:
                    dma('pool', dr[:, ts], src[p0:p1, :], reads=[srct])

            def evac_copy(bank, M, dsts, ts):
                k = cnt['ob'] % 4
                cnt['ob'] += 1
                op('act', lambda e: e.copy(out=ob[k][0:M, :], in_=PS[bank][0:M, :]), reads=['ps%d' % bank], writes=['ob%d' % k])
                store_rows(ob[k], 'ob%d' % k, dsts, ts)

            def evac_rope(bankA, bankB, ty, dsts, ts):
                k = cnt['ob'] % 4
                cnt['ob'] += 1
                j = cnt['t'] % 2
                cnt['t'] += 1
                op('dve', lambda e: e.tensor_tensor(out=t1[j], in0=PS[bankA], in1=tabs[ty + '_c'], op=ALU.mult),
                   reads=['ps%d' % bankA, ty + '_c'], writes=['t1%d' % j])
                op('dve', lambda e: e.tensor_tensor(out=t2[j], in0=PS[bankB], in1=tabs[ty + '_s'], op=ALU.mult),
                   reads=['ps%d' % bankB, ty + '_s'], writes=['t2%d' % j])
                op('pool', lambda e: e.tensor_tensor(out=ob[k], in0=t1[j], in1=t2[j], op=ALU.add),
                   reads=['t1%d' % j, 't2%d' % j], writes=['ob%d' % k])
                store_rows(ob[k], 'ob%d' % k, dsts, ts)

            def mm_group(bank, M, N, lhs_fn, rhs_fn, KC, reads):
                for kc in range(KC):
                    last = kc == KC - 1
                    op('pe', lambda e: e.matmul(PS[bank][0:M, 0:N], lhsT=lhs_fn(kc), rhs=rhs_fn(kc), start=(kc == 0), stop=last),
                       reads=reads, writes=['ps%d' % bank], inc=last)

            def rows(t, r0, n):
                return t[r0:r0 + n, :]

            for g in range(NG):
                ts = slice(g * G, (g + 1) * G)
                xt = xT[g % 2]
                xtt = 'xT%d' % (g % 2)
                for nm, k in (('a_c', 0), ('a_s', 1), ('i_c', 2), ('i_s', 3), ('m_c', 4), ('m_s', 5)):
                    dma('sp', tabs[nm], tab[k * 128:(k + 1) * 128, ts], writes=[nm])
                dma('sp', tabs['s_c'][0:64, :], tab[4 * 128:4 * 128 + 64, ts], writes=['s_c'])
                dma('sp', tabs['s_s'][0:64, :], tab[5 * 128:5 * 128 + 64, ts], writes=['s_s'])
                dma('sp', tabs['s_c'][64:80, :], tab[2 * 128 + 64:2 * 128 + 80, ts], writes=['s_c'])
                dma('sp', tabs['s_s'][64:80, :], tab[3 * 128 + 64:3 * 128 + 80, ts], writes=['s_s'])
                for i in range(4):
                    b = i % 2
                    r0 = g * G + i * 128
                    dma('sp', xf[b], x_src[r0:r0 + 128, :], writes=['xf%d' % b])
                    op('pool', lambda e: e.tensor_copy(out=xbf[b], in_=xf[b]), reads=['xf%d' % b], writes=['xbf%d' % b])
                    for hb in range(2):
                        for c in range(8):
                            kc = hb * 8 + c
                            op('pe', lambda e: e.transpose(out=PSB[hb][:, c * 128:(c + 1) * 128], in_=xbf[b][:, kc * 128:(kc + 1) * 128],
                                                           identity=identb[:]),
                               reads=['xbf%d' % b, 'identb'], writes=['ps%d' % hb], inc=(c == 7))
                        eng = 'act' if hb == 0 else 'dve'
                        op(eng, lambda e: (e.copy if eng == 'act' else e.tensor_copy)(
                            out=xt[:, hb * 8:(hb + 1) * 8, i * 128:(i + 1) * 128],
                            in_=PSB[hb].rearrange("p (c n) -> p c n", c=8)), reads=['ps%d' % hb], writes=[xtt])
                for i in range(4):
                    r0 = g * G + i * 128
                    vb_ = cnt['v'] % 2
                    cnt['v'] += 1
                    for cb in range(2):
                        bank = nbank()
                        mm_group(bank, 128, 512, lambda kc: xt[:, kc, i * 128:(i + 1) * 128], lambda kc: wtm[:, kc, cb * 512:(cb + 1) * 512],
                                 16, [xtt, 'wtm'])
                        op('act', lambda e: e.copy(out=vtm[vb_][:, cb * 512:(cb + 1) * 512], in_=PS[bank]), reads=['ps%d' % bank],
                           writes=['vtm%d' % vb_])
                    blk = (g * G) // 128 + i
                    for h in range(8):
                        dma('pool', va_hp[h * 128:(h + 1) * 128, blk * 128:(blk + 1) * 128], vtm[vb_][:, h * 128:(h + 1) * 128],
                            reads=['vtm%d' % vb_])
                    bank = nbank()
                    mm_group(bank, 128, 16, lambda kc: xt[:, kc, i * 128:(i + 1) * 128], lambda kc: wtm[:, kc, 1024:1040], 16, [xtt, 'wtm'])
                    op('act', lambda e: e.activation(out=wio[vb_], in_=PS[bank][:, 0:16], func=AF.Copy, scale=1.0 / 32.0),
                       reads=['ps%d' % bank], writes=['wio%d' % vb_])
                    dma('pool', wi_d[r0:r0 + 128, :], wio[vb_], reads=['wio%d' % vb_])
                ci = 0
                nxt = load_wfm(0)

                def fm(M=128):
                    nonlocal ci, nxt
                    b = nxt
                    if ci + 1 < NFM:
                        nxt = load_wfm(ci + 1)
                    bank = nbank()
                    mm_group(bank, M, G, lambda kc: wfm[b][:, kc, 0:M], lambda kc: xt[:, kc, :], 16, ['wfm%d' % b, xtt])
                    ci += 1
                    return bank

                for (dst, nm) in ((qaT, 'a'), (kaT, 'a')):
                    for c in range(2):
                        ba = fm()
                        bb = fm()
                        evac_rope(ba, bb, 'a', [(32 * k, 32 * k + 32, rows(dst, (4 * c + k) * 128, 32)) for k in range(4)], ts)
                    for h in range(8):
                        ba = fm(96)
                        evac_copy(ba, 96, [(0, 96, rows(dst, h * 128 + 32, 96))], ts)
                for c in range(2):
                    ba = fm()
                    bb = fm()
                    evac_rope(ba, bb, 'i', [(16 * k, 16 * k + 16, rows(qiT, (8 * c + k) * 64, 16)) for k in range(8)], ts)
                for j in range(8):
                    ba = fm(96)
                    evac_copy(ba, 96, [(0, 48, rows(qiT, (2 * j) * 64 + 16, 48)), (48, 96, rows(qiT, (2 * j + 1) * 64 + 16, 48))], ts)
                for (nchunk, dstf, nm) in ((4, cqf, 'q'), (2, ckvf, 'kv')):
                    sbank = 7
                    for c in range(nchunk):
                        ba = fm()
                        op('act', lambda e: e.copy(out=dstf[:, c, :], in_=PS[ba]), reads=['ps%d' % ba], writes=['cf' + nm])
                        k = cnt['sq'] % 2
                        cnt['sq'] += 1
                        op('act', lambda e: e.activation(out=sq[k], in_=PS[ba], func=AF.Square), reads=['ps%d' % ba], writes=['sq%d' % k])
                        op('pe', lambda e: e.matmul(PS[sbank], lhsT=onesf[:], rhs=sq[k], start=(c == 0), stop=(c == nchunk - 1)),
                           reads=['sq%d' % k, 'onesf'], writes=['ps%d' % sbank])
                    rs = rstd[nm]
                    op('dve', lambda e: e.tensor_scalar(out=rs, in0=PS[sbank], scalar1=1.0 / (128 * nchunk), scalar2=RMS_EPS,
                                                        op0=ALU.mult, op1=ALU.add), reads=['ps%d' % sbank], writes=['rs' + nm])
                    op('act', lambda e: e.activation(out=rs, in_=rs, func=AF.Sqrt), reads=['rs' + nm], writes=['rs' + nm])
                    op('dve', lambda e: e.reciprocal(out=rs, in_=rs), reads=['rs' + nm], writes=['rs' + nm])
                    gg = gq if nm == 'q' else gkv
                    dn = cqn if nm == 'q' else ckvn
                    for c in range(nchunk):
                        op('dve', lambda e: e.scalar_tensor_tensor(out=dn[:, c, :], in0=dstf[:, c, :], scalar=gg[:, c:c + 1], in1=rs,
                                                                   op0=ALU.mult, op1=ALU.mult),
                           reads=['cf' + nm, 'rs' + nm, 'gq', 'gkv'], writes=['n' + nm])
                ba = fm()
                bb = fm()
                evac_rope(ba, bb, 's', [(0, 64, kpT), (64, 128, kiT)], ts)
                assert ci == NFM
                for h in range(8):
                    bank = nbank()
                    mm_group(bank, 128, G, lambda kc: wuq[:, h, kc, :], lambda kc: cqn[:, kc, :], 4, ['wuq', 'nq'])
                    evac_copy(bank, 128, [(0, 128, rows(qnT, h * 128, 128))], ts)
                for c in range(4):
                    ba = nbank()
                    mm_group(ba, 128, G, lambda kc: wuq[:, 8 + 2 * c, kc, :], lambda kc: cqn[:, kc, :], 4, ['wuq', 'nq'])
                    bb = nbank()
                    mm_group(bb, 128, G, lambda kc: wuq[:, 9 + 2 * c, kc, :], lambda kc: cqn[:, kc, :], 4, ['wuq', 'nq'])
                    evac_rope(ba, bb, 'm', [(0, 64, rows(qpT, (2 * c) * 64, 64)), (64, 128, rows(qpT, (2 * c + 1) * 64, 64))], ts)
                for h in range(8):
                    bank = nbank()
                    mm_group(bank, 128, G, lambda kc: wukk[:, h, kc, :], lambda kc: ckvn[:, kc, :], 2, ['wukk', 'nkv'])
                    evac_copy(bank, 128, [(0, 128, rows(knT, h * 128, 128))], ts)
                for i in range(4):
                    vb_ = cnt['v'] % 2
                    cnt['v'] += 1
                    for cb in range(2):
                        bank = nbank()
                        mm_group(bank, 128, 512, lambda kc: ckvn[:, kc, i * 128:(i + 1) * 128], lambda kc: wukv[:, kc, cb * 512:(cb + 1) * 512],
                                 2, ['nkv', 'wukv'])
                        op('act', lambda e: e.copy(out=vtm[vb_][:, cb * 512:(cb + 1) * 512], in_=PS[bank]), reads=['ps%d' % bank],
                           writes=['vtm%d' % vb_])
                    blk = (g * G) // 128 + i
                    for h in range(8):
                        dma('pool', vb_hp[h * 128:(h + 1) * 128, blk * 128:(blk + 1) * 128], vtm[vb_][:, h * 128:(h + 1) * 128],
                            reads=['vtm%d' % vb_])
            tr.barrier()
            if 'stop_p1' in dbg:
                return

            tr.reset()
            ki2 = tr.alloc(S, BF16)
            qit = [tr.alloc(8 * 128, BF16).rearrange("p (j n) -> p j n", j=8) for _ in range(2)]
            wit = [tr.alloc(16, F32) for _ in range(2)]
            acc = [tr.alloc(S, F32) for _ in range(2)]
            work = tr.alloc(S, F32)
            rl = [tr.alloc(512, BF16) for _ in range(4)]
            diag = [tr.alloc(16 * 128, BF16).rearrange("p (h n) -> p h n", h=16) for _ in range(2)]
            m8 = [tr.alloc(8, F32) for _ in range(2)]
            thr = [tr.alloc(1, F32) for _ in range(2)]
            maskb = [tr.alloc(S, BF16) for _ in range(4)]
            maskT = tr.alloc(32 * 512, BF16).rearrange("p (b n) -> p b n", b=32)
            kT = [tr.alloc(S, BF16) for _ in range(2)]
            vv = [tr.alloc(32 * 128, BF16).rearrange("p (b n) -> p b n", b=32) for _ in range(2)]
            qq = [tr.alloc(512, BF16) for _ in range(2)]
            kpe = tr.alloc(S, BF16)
            qpe = [tr.alloc(512, BF16) for _ in range(2)]
            Pb = [tr.alloc(512, BF16) for _ in range(3)]
            rc = [tr.alloc(512, F32) for _ in range(2)]
            oo = [tr.alloc(512, BF16) for _ in range(2)]
            pos_ = [tr.alloc(512, F32) for _ in range(2)]

            dma('sp', ki2[0:64, :], kiT, writes=['ki2'])
            dma('sp', ki2[64:128, :], kiT, writes=['ki2'])
            dma('sp', kpe[0:64, :], kpT, writes=['kpe'])
            c2 = {'q': 0, 'rl': 0, 'ib': 0, 'tb': 0, 'kv': 0, 'P': 0, 'qk': 0, 'o': 0, 'm8': 0}
            qi_v = qiT.rearrange("(j two d) s -> (two d) j s", two=2, d=64)
            cm4 = cmaskb[:].rearrange("p (j n) -> p j n", j=4)

            def score_tile(qt, q4):
                L = (qt + 1) * 128
                b = c2['q'] % 2
                c2['q'] += 1
                a = acc[b]
                at = 'acc%d' % b
                dma('sp', qit[b], qi_v[:, :, qt * 128:(qt + 1) * 128], writes=['qit%d' % b])
                dma('sp', wit[b], wi_d[qt * 128:(qt + 1) * 128, :], writes=['wit%d' % b])
                dg = diag[b]
                for h in range(16):
                    op('pool', lambda e: e.tensor_scalar(out=dg[:, h, :], in0=identb[:], scalar1=wit[b][:, h:h + 1], scalar2=None, op0=ALU.mult),
                       reads=['identb', 'wit%d' % b], writes=['diag%d' % b])
                nsb = (L + 511) // 512
                for sbk in range(nsb):
                    n = min(512, L - sbk * 512)
                    ks = []

                    def a_mm(h):
                        bank = (0, 1, 3)[c2['ib'] % 3]
                        c2['ib'] += 1
                        p0 = (h % 2) * 64
                        op('pe', lambda e: e.matmul(PS[bank][:, 0:n], lhsT=qit[b][p0:p0 + 64, h // 2, :], rhs=ki2[p0:p0 + 64, sbk * 512:sbk * 512 + n],
                                                    start=True, stop=True), reads=['qit%d' % b, 'ki2'], writes=['ps%d' % bank])
                        k = c2['rl'] % 4
                        c2['rl'] += 1
                        op('act', lambda e: e.activation(out=rl[k][:, 0:n], in_=PS[bank][:, 0:n], func=AF.Relu), reads=['ps%d' % bank],
                           writes=['rl%d' % k])
                        ks.append(k)

                    def d_mm(h):
                        k = ks[h]
                        op('pe', lambda e: e.matmul(PS[2][:, 0:n], lhsT=dg[:, h, :], rhs=rl[k][:, 0:n], start=(h == 0), stop=(h == 15)),
                           reads=['diag%d' % b, 'rl%d' % k], writes=['ps2'])

                    a_mm(0)
                    a_mm(1)
                    for h in range(16):
                        if h + 2 < 16:
                            a_mm(h + 2)
                        d_mm(h)
                    op('act', lambda e: e.copy(out=a[:, sbk * 512:sbk * 512 + n], in_=PS[2][:, 0:n]), reads=['ps2'], writes=[at])
                op('dve', lambda e: e.tensor_tensor(out=a[:, qt * 128:L], in0=a[:, qt * 128:L], in1=trineg[:], op=ALU.add),
                   reads=[at, 'trineg'], writes=[at])
                mk = maskb[q4]
                mt = 'maskb%d' % q4
                if L > 256:
                    for r in range(32):
                        mi = c2['m8'] % 2
                        c2['m8'] += 1
                        src = a[:, 0:L] if r == 0 else work[:, 0:L]
                        op('dve', lambda e: e.max(out=m8[mi], in_=src), reads=[at, 'work'], writes=['m8%d' % mi])
                        if r < 31:
                            op('dve', lambda e: e.match_replace(out=work[:, 0:L], in_to_replace=m8[mi], in_values=src, imm_value=NEG),
                               reads=[at, 'work', 'm8%d' % mi], writes=['work'])
                    tb_ = thr[b]
                    op('dve', lambda e: e.tensor_scalar(out=tb_, in0=m8[mi][:, 7:8], scalar1=-1.0e29, scalar2=None, op0=ALU.max),
                       reads=['m8%d' % mi], writes=['thr%d' % b])
                    op('dve', lambda e: e.tensor_scalar(out=mk[:, 0:L], in0=a[:, 0:L], scalar1=tb_, scalar2=None, op0=ALU.is_ge),
                       reads=[at, 'thr%d' % b], writes=[mt])
                else:
                    op('dve', lambda e: e.tensor_scalar(out=mk[:, 0:L], in0=a[:, 0:L], scalar1=negbig[:], scalar2=None, op0=ALU.is_ge),
                       reads=[at, 'negbig'], writes=[mt])
                if 'dbg_p2' in dbg and l == 0 and qt in (2, 9):
                    di = 0 if qt == 2 else 1
                    dma('sp', dacc[di * 128:(di + 1) * 128, 0:L], a[:, 0:L], reads=[at])
                    dma('sp', dwork[di * 128:(di + 1) * 128, 0:L], work[:, 0:L], reads=['work'])
                    dma('sp', dm8[di * 128:(di + 1) * 128, :], m8[mi], reads=['m8%d' % mi])
                    dma('sp', dthr[di * 128:(di + 1) * 128, :], thr[b], reads=['thr%d' % b])
                    dma('sp', dmask[di * 128:(di + 1) * 128, 0:L], mk[:, 0:L], reads=[mt])
                Lmax = (qt // 4 + 1) * 512
                if L < Lmax:
                    op('dve', lambda e: e.memset(mk[:, L:Lmax], 0.0), writes=[mt])

            def transposes_tile(j):
                tbk, q4 = j // 4, j % 4
                nblk = 4 * tbk + 4
                blk0 = 0
                while blk0 < nblk:
                    nb = min(8, nblk - blk0)
                    for i in range(nb):
                        blk = blk0 + i
                        op('pe', lambda e: e.transpose(out=PSB[3][:, i * 128:(i + 1) * 128], in_=maskb[q4][:, blk * 128:(blk + 1) * 128],
                                                       identity=identb[:]),
                           reads=['maskb%d' % q4, 'identb'], writes=['ps3'], inc=(i == nb - 1))
                    op('act', lambda e: e.activation(out=maskT[:, blk0:blk0 + nb, q4 * 128:(q4 + 1) * 128],
                                                     in_=PSB[3][:, 0:nb * 128].rearrange("p (b n) -> p b n", b=nb),
                                                     func=AF.Identity, scale=MBIG, bias=negm[:]),
                       reads=['ps3', 'negm'], writes=['maskT'])
                    blk0 += nb

            def attn_head(tbk, h, mla):
                nblk = 4 * tbk + 4
                Lmax = nblk * 128
                tsl = slice(tbk * 512, (tbk + 1) * 512)
                b = c2['kv'] % 2
                c2['kv'] += 1
                ksrc, vsrc, qsrc = (knT, vb_hp, qnT) if mla else (kaT, va_hp, qaT)
                dma('sp', kT[b][:, 0:Lmax], ksrc[h * 128:(h + 1) * 128, 0:Lmax], writes=['kT%d' % b])
                dma('sp', vv[b][:, 0:nblk, :], vsrc[h * 128:(h + 1) * 128, 0:Lmax].rearrange("p (b n) -> p b n", n=128), writes=['vv%d' % b])
                dma('sp', qq[b], qsrc[h * 128:(h + 1) * 128, tsl], writes=['qq%d' % b])
                if mla:
                    dma('sp', qpe[b][0:64, :], qpT[h * 64:(h + 1) * 64, tsl], writes=['qpe%d' % b])
                scale = float((192.0 if mla else 128.0) ** -0.5)
                qkb = {}

                def qk(sc):
                    bank = 4 + c2['qk'] % 2
                    c2['qk'] += 1
                    qkb[sc] = bank
                    diag_blk = mla and sc >= 4 * tbk
                    op('pe', lambda e: e.matmul(PS[bank], lhsT=kT[b][:, sc * 128:(sc + 1) * 128], rhs=qq[b], start=True, stop=False),
                       reads=['kT%d' % b, 'qq%d' % b], writes=['ps%d' % bank], inc=False)
                    if mla:
                        op('pe', lambda e: e.matmul(PS[bank], lhsT=kpe[0:64, sc * 128:(sc + 1) * 128], rhs=qpe[b][0:64, :], start=False,
                                                    stop=not diag_blk),
                           reads=['kpe', 'qpe%d' % b], writes=['ps%d' % bank], inc=not diag_blk)
                        if diag_blk:
                            op('pe', lambda e: e.matmul(PS[bank], lhsT=identb[:], rhs=cm4[:, sc - 4 * tbk, :], start=False, stop=True),
                               reads=['identb', 'cmaskb'], writes=['ps%d' % bank])
                    else:
                        op('pe', lambda e: e.matmul(PS[bank], lhsT=identb[:], rhs=maskT[:, sc, :], start=False, stop=True),
                           reads=['identb', 'maskT'], writes=['ps%d' % bank])

                qk(0)
                for sc in range(nblk):
                    if sc + 1 < nblk:
                        qk(sc + 1)
                    bank = qkb[sc]
                    k = c2['P'] % 3
                    c2['P'] += 1
                    op('act', lambda e: e.activation(out=Pb[k], in_=PS[bank], func=AF.Exp, scale=scale), reads=['ps%d' % bank], writes=['P%d' % k])
                    op('pe', lambda e: e.matmul(PS[6], lhsT=vv[b][:, sc, :], rhs=Pb[k], start=(sc == 0), stop=(sc == nblk - 1)),
                       reads=['vv%d' % b, 'P%d' % k], writes=['ps6'], inc=False)
                    op('pe', lambda e: e.matmul(PS[7], lhsT=onesb[:], rhs=Pb[k], start=(sc == 0), stop=(sc == nblk - 1)),
                       reads=['onesb', 'P%d' % k], writes=['ps7'])
                j = c2['o'] % 2
                c2['o'] += 1
                op('act', lambda e: e.activation(out=rc[j], in_=PS[7], func=AF.Ln), reads=['ps7'], writes=['rc%d' % j])
                op('act', lambda e: e.copy(out=pos_[j], in_=PS[6]), reads=['ps6'], writes=['pos%d' % j])
                op('act', lambda e: e.activation(out=rc[j], in_=rc[j], func=AF.Exp, scale=-1.0), reads=['rc%d' % j], writes=['rc%d' % j])
                op('pool', lambda e: e.tensor_tensor(out=oo[j], in0=pos_[j], in1=rc[j], op=ALU.mult),
                   reads=['pos%d' % j, 'rc%d' % j], writes=['oo%d' % j])
                r0 = (8 + h if mla else h) * 128
                dma('pool', aoT[r0:r0 + 128, tsl], oo[j], reads=['oo%d' % j])

            NTB = 3 if 'dbg_p2' in dbg else 8
            NT = 4 * NTB
            st = {'scored': 0, 'T': 0}
            dsa_done = [0] * NTB
            mla_done = [0] * NTB

            def ensure_dsa(tb_, n):
                ensure_T(4 * tb_ + 4)
                while dsa_done[tb_] < n:
                    attn_head(tb_, dsa_done[tb_], False)
                    dsa_done[tb_] += 1

            def ensure_T(upto):
                while st['T'] < upto:
                    j = st['T']
                    if j % 4 == 0 and j >= 4:
                        ensure_dsa(j // 4 - 1, 8)
                    assert st['scored'] > j
                    transposes_tile(j)
                    st['T'] += 1

            k = 0
            while True:
                if k < NT:
                    if k >= 4:
                        ensure_T(k - 3)
                    score_tile(k, k % 4)
                    st['scored'] += 1
                njobs = 0
                while njobs < 4:
                    cand = [t for t in range(NTB) if dsa_done[t] < 8 and st['T'] >= 4 * t + 4]
                    if cand:
                        t = cand[0]
                        attn_head(t, dsa_done[t], False)
                        dsa_done[t] += 1
                        njobs += 1
                        continue
                    cand = [t for t in range(NTB) if mla_done[t] < 8 and t <= k // 4 + 2]
                    if cand:
                        t = cand[0]
                        attn_head(t, mla_done[t], True)
                        mla_done[t] += 1
                        njobs += 1
                        continue
                    break
                ensure_T(max(0, min(k - 1, NT, st['scored'])))
                k += 1
                if k >= NT and st['T'] >= NT and all(d == 8 for d in dsa_done) and all(d == 8 for d in mla_done):
                    break
                assert k < NT + 64
            tr.barrier()
            if 'stop_p2' in dbg:
                return

            def layer_norm(xt_, xtt_, gt, bt, stt, mv, sd):
                xv = xt_.rearrange("p (c n) -> p c n", c=4)
                for c in range(4):
                    op('dve', lambda e: e.bn_stats(out=stt[:, c, :], in_=xv[:, c, :]), reads=[xtt_], writes=['stt'])
                op('dve', lambda e: e.bn_aggr(out=mv, in_=stt), reads=['stt'], writes=['mv'])
                op('dve', lambda e: e.tensor_scalar(out=sd, in0=mv[:, 1:2], scalar1=LN_EPS, scalar2=None, op0=ALU.add), reads=['mv'], writes=['sd'])
                op('act', lambda e: e.activation(out=sd, in_=sd, func=AF.Sqrt), reads=['sd'], writes=['sd'])
                op('dve', lambda e: e.reciprocal(out=sd, in_=sd), reads=['sd'], writes=['sd'])
                op('dve', lambda e: e.tensor_scalar(out=xt_, in0=xt_, scalar1=mv[:, 0:1], scalar2=sd, op0=ALU.subtract, op1=ALU.mult),
                   reads=[xtt_, 'mv', 'sd'], writes=[xtt_])
                op('pool', lambda e: e.tensor_tensor(out=xt_, in0=xt_, in1=gt, op=ALU.mult), reads=[xtt_, 'lng'], writes=[xtt_])
                op('pool', lambda e: e.tensor_tensor(out=xt_, in0=xt_, in1=bt, op=ALU.add), reads=[xtt_, 'lnb'], writes=[xtt_])

            tr.reset()
            aT = [tr.alloc(16 * G, BF16).rearrange("p (k n) -> p k n", k=16) for _ in range(2)]
            wo = [tr.alloc(16 * 512, BF16).rearrange("p (k n) -> p k n", k=16) for _ in range(2)]
            xs = [tr.alloc(D, F32) for _ in range(8)]
            lng = tr.alloc(D, F32)
            lnb = tr.alloc(D, F32)
            stt = tr.alloc(4 * 6, F32).rearrange("p (c n) -> p c n", c=4)
            mv = tr.alloc(2, F32)
            sd = tr.alloc(1, F32)
            dma('sp', lng, ln_in['ln1_g'][l:l + 1, :].to_broadcast([128, D]), writes=['lng'])
            dma('sp', lnb, ln_in['ln1_b'][l:l + 1, :].to_broadcast([128, D]), writes=['lnb'])
            aoT_v = aoT.rearrange("(k p) s -> p k s", p=128)
            wc = 0
            bk = 0
            for g in range(NG):
                ab = g % 2
                dma('sp', aT[ab], aoT_v[:, :, g * G:(g + 1) * G], writes=['aT%d' % ab])
                for i in range(4):
                    xi = (g % 2) * 4 + i
                    r0 = g * G + i * 128
                    dma('sp', xs[xi], x_src[r0:r0 + 128, :], writes=['xs%d' % xi])
                for cb in range(4):
                    wbf = wc % 2
                    wc += 1
                    dma('sp', wo[wbf], W('wo', cb * 128, (cb + 1) * 128).rearrange("p (k n) -> p k n", k=16), writes=['wo%d' % wbf])
                    for i in range(4):
                        xi = (g % 2) * 4 + i
                        bank = bk % 6
                        bk += 1
                        mm_group(bank, 128, 512, lambda kc: aT[ab][:, kc, i * 128:(i + 1) * 128], lambda kc: wo[wbf][:, kc, :], 16,
                                 ['aT%d' % ab, 'wo%d' % wbf])
                        xsl = xs[xi][:, cb * 512:(cb + 1) * 512]
                        op('dve', lambda e: e.scalar_tensor_tensor(out=xsl, in0=xsl, scalar=ALPHA, in1=PS[bank], op0=ALU.mult, op1=ALU.add),
                           reads=['ps%d' % bank, 'xs%d' % xi], writes=['xs%d' % xi])
                for i in range(4):
                    xi = (g % 2) * 4 + i
                    r0 = g * G + i * 128
                    layer_norm(xs[xi], 'xs%d' % xi, lng, lnb, stt, mv, sd)
                    dma('pool', xa[r0:r0 + 128, :], xs[xi], reads=['xs%d' % xi])
            tr.barrier()
            if 'stop_p3' in dbg:
                return

            tr.reset()
            x1 = [tr.alloc(D, F32) for _ in range(4)]
            xbf = [tr.alloc(D, BF16) for _ in range(2)]
            x1T = tr.alloc(16 * G, BF16).rearrange("p (k n) -> p k n", k=16)
            hT = tr.alloc(NFC * G, BF16).rearrange("p (k n) -> p k n", k=NFC)
            wgb = [tr.alloc(16 * 128, BF16).rearrange("p (k n) -> p k n", k=16) for _ in range(2)]
            wub = [tr.alloc(16 * 128, BF16).rearrange("p (k n) -> p k n", k=16) for _ in range(2)]
            wdb = [tr.alloc(NFC * 256, BF16).rearrange("p (k n) -> p k n", k=NFC) for _ in range(2)]
            sg = [tr.alloc(G, F32) for _ in range(2)]
            lng = tr.alloc(D, F32)
            lnb = tr.alloc(D, F32)
            stt = tr.alloc(4 * 6, F32).rearrange("p (c n) -> p c n", c=4)
            mv = tr.alloc(2, F32)
            sd = tr.alloc(1, F32)
            dma('sp', lng, ln_in['ln2_g'][l:l + 1, :].to_broadcast([128, D]), writes=['lng'])
            dma('sp', lnb, ln_in['ln2_b'][l:l + 1, :].to_broadcast([128, D]), writes=['lnb'])
            wc = 0
            wdc = 0
            bk = 0
            for g in range(NG):
                for i in range(4):
                    r0 = g * G + i * 128
                    b = i % 2
                    dma('sp', x1[i], xa[r0:r0 + 128, :], writes=['x1%d' % i])
                    op('pool', lambda e: e.tensor_copy(out=xbf[b], in_=x1[i]), reads=['x1%d' % i], writes=['xbf%d' % b])
                    for hb in range(2):
                        for c in range(8):
                            kc = hb * 8 + c
                            op('pe', lambda e: e.transpose(out=PSB[6 + hb][:, c * 128:(c + 1) * 128], in_=xbf[b][:, kc * 128:(kc + 1) * 128],
                                                           identity=identb[:]),
                               reads=['xbf%d' % b, 'identb'], writes=['ps%d' % (6 + hb)], inc=(c == 7))
                        eng = 'act' if hb == 0 else 'dve'
                        op(eng, lambda e: (e.copy if eng == 'act' else e.tensor_copy)(
                            out=x1T[:, hb * 8:(hb + 1) * 8, i * 128:(i + 1) * 128],
                            in_=PSB[6 + hb].rearrange("p (c n) -> p c n", c=8)), reads=['ps%d' % (6 + hb)], writes=['x1T'])
                for fc in range(NFC):
                    wbf = wc % 2
                    wc += 1
                    dma('sp', wgb[wbf], W('wg', fc * 128, (fc + 1) * 128).rearrange("p (k n) -> p k n", k=16), writes=['wg%d' % wbf])
                    dma('sp', wub[wbf], W('wu', fc * 128, (fc + 1) * 128).rearrange("p (k n) -> p k n", k=16), writes=['wu%d' % wbf])
                    bg = (bk % 3) * 2
                    bu = bg + 1
                    bk += 1
                    mm_group(bg, 128, G, lambda kc: wgb[wbf][:, kc, :], lambda kc: x1T[:, kc, :], 16, ['wg%d' % wbf, 'x1T'])
                    mm_group(bu, 128, G, lambda kc: wub[wbf][:, kc, :], lambda kc: x1T[:, kc, :], 16, ['wu%d' % wbf, 'x1T'])
                    k = fc % 2
                    op('act', lambda e: e.activation(out=sg[k], in_=PS[bg], func=AF.Silu), reads=['ps%d' % bg], writes=['sg%d' % k])
                    op('dve', lambda e: e.tensor_tensor(out=hT[:, fc, :], in0=sg[k], in1=PS[bu], op=ALU.mult),
                       reads=['sg%d' % k, 'ps%d' % bu], writes=['hT'])
                for cbh in range(8):
                    wbf = wdc % 2
                    wdc += 1
                    dma('sp', wdb[wbf], W('wd', cbh * 128, (cbh + 1) * 128).rearrange("p (k n) -> p k n", k=NFC), writes=['wd%d' % wbf])
                    for i in range(4):
                        bank = bk % 6
                        bk += 1
                        mm_group(bank, 128, 256, lambda kc: hT[:, kc, i * 128:(i + 1) * 128], lambda kc: wdb[wbf][:, kc, :], NFC,
                                 ['hT', 'wd%d' % wbf])
                        xsl = x1[i][:, cbh * 256:(cbh + 1) * 256]
                        op('dve', lambda e: e.scalar_tensor_tensor(out=xsl, in0=xsl, scalar=ALPHA, in1=PS[bank][:, 0:256], op0=ALU.mult, op1=ALU.add),
                           reads=['ps%d' % bank, 'x1%d' % i], writes=['x1%d' % i])
                for i in range(4):
                    r0 = g * G + i * 128
                    layer_norm(x1[i], 'x1%d' % i, lng, lnb, stt, mv, sd)
                    dma('pool', x_dst[r0:r0 + 128, :], x1[i], reads=['x1%d' % i])
            tr.barrier()

        layer(0, x_in, xb_d)
        if not any(k.startswith('stop') for k in dbg):
            layer(1, xb_d, out_d)
        tr.barrier()
    return nc


_CACHE = {}


def _host_inputs(inputs):
    f = lambda a: np.asarray(a, dtype=np.float32)
    layers = [_prep_layer(f(inputs['w_in'][l]), f(inputs['w_uq'][l]), f(inputs['w_ukv'][l]), f(inputs['w_o'][l]),
                          f(inputs['w_gate'][l]), f(inputs['w_up'][l]), f(inputs['w_down'][l])) for l in range(DEPTH)]
    shared = {k: np.ascontiguousarray(np.concatenate([layers[l][k] for l in range(DEPTH)], axis=0)) for k in WSHAPES}
    shared.update(_consts())
    shared['g_cq_t'] = np.ascontiguousarray(f(inputs['g_cq']).reshape(DEPTH, 4, 128).transpose(0, 2, 1)).reshape(DEPTH * 128, 4)
    shared['g_ckv_t'] = np.ascontiguousarray(f(inputs['g_ckv']).reshape(DEPTH, 2, 128).transpose(0, 2, 1)).reshape(DEPTH * 128, 2)
    for k in ('ln1_g', 'ln1_b', 'ln2_g', 'ln2_b'):
        shared[k] = f(inputs[k])
    return shared


def kernel(**inputs):
    x = np.asarray(inputs['x'], dtype=np.float32)
    pos = np.asarray(inputs['positions']).astype(np.int32)
    shared = _host_inputs(inputs)
    if 'nc' not in _CACHE:
        _CACHE['nc'] = build_program()
    nc = _CACHE['nc']
    in_maps = []
    for b in range(4):
        m = dict(shared)
        m['x'] = np.ascontiguousarray(x[b])
        m['positions'] = np.ascontiguousarray(pos[b:b + 1])
        in_maps.append(m)
    res = run_bass_kernel_spmd(nc, in_maps, core_ids=list(range(4)))
    out = np.stack([np.asarray(res.results[b]['out'], dtype=np.float32) for b in range(4)], axis=0)
    return out
```

```python
import numpy as np
from contextlib import ExitStack
import concourse.bass as bass
import concourse.mybir as mybir
from concourse.bass_utils import run_bass_kernel_spmd

F32 = mybir.dt.float32
BF16 = mybir.dt.bfloat16
I32 = mybir.dt.int32
ALU = mybir.AluOpType
AF = mybir.ActivationFunctionType

S = 4096
D = 2048
DEPTH = 2
FF = 5632
NFC = FF // 128
G = 512
NG = S // G
ALPHA = float((2 * DEPTH) ** 0.25)
LN_EPS = 1e-5
RMS_EPS = 1e-6
NEG = -1.0e30
MBIG = 30000.0
NFM = 44
ARENA = 50 * 1024


def _dsize(dt):
    return 4 if dt in (F32, I32) else 2


class TR:
    NQ = 8

    def __init__(self, nc, es):
        self.nc = nc
        self.E = {'pe': nc.tensor, 'act': nc.scalar, 'dve': nc.vector, 'pool': nc.gpsimd, 'sp': nc.sync}
        self.sem = {k: es.enter_context(nc.semaphore('s_' + k)) for k in ('pe', 'act', 'dve', 'pool')}
        self.cnt = {k: 0 for k in self.sem}
        self.dq = {}
        for q in ('sp', 'pool', 'act'):
            self.dq[q] = {'sems': [es.enter_context(nc.semaphore('d_%s%d' % (q, i))) for i in range(self.NQ)], 'j': 0}
        self.seen = {e: {} for e in self.E}
        self.lastw = {}
        self.readers = {}
        self.arena = es.enter_context(nc.sbuf_tensor('arena', [128, ARENA], F32))
        self.off = 0
        self.ps = [es.enter_context(nc.psum_tensor('ps%d' % i, [128, 512], F32)) for i in range(8)]

    def reset(self):
        self.off = 0

    def alloc(self, n, dt):
        nb = n * _dsize(dt)
        n4 = (nb + 31) // 32 * 8
        ap = self.arena[:, self.off:self.off + n4]
        self.off += n4
        assert self.off <= ARENA, ('arena overflow', self.off)
        if dt != F32:
            ap = ap.bitcast(dt)
        return ap[:, 0:n]

    def _handle(self, key):
        if key[0] == 'c':
            return self.sem[key[1]]
        return self.dq[key[1]]['sems'][key[2]]

    def _wait(self, e, ev):
        if ev is None:
            return
        key, val = ev
        if key == ('c', e) and e == 'pe':
            return
        if self.seen[e].get(key, 0) >= val:
            return
        self.E[e].wait_ge(self._handle(key), val)
        self.seen[e][key] = val

    def _deps(self, reads, writes):
        out = []
        for t in reads:
            ev = self.lastw.get(t)
            if ev is not None:
                out.append(ev)
        for t in writes:
            ev = self.lastw.get(t)
            if ev is not None:
                out.append(ev)
            for k, v in self.readers.get(t, {}).items():
                out.append((k, v))
        return out

    def _record(self, ev, reads, writes):
        for t in reads:
            r = self.readers.setdefault(t, {})
            if r.get(ev[0], 0) < ev[1]:
                r[ev[0]] = ev[1]
        for t in writes:
            self.lastw[t] = ev
            self.readers[t] = {}

    def op(self, e, fn, reads=(), writes=(), inc=True):
        for ev in self._deps(reads, writes):
            self._wait(e, ev)
        ins = fn(self.E[e])
        if inc:
            self.cnt[e] += 1
            ins.then_inc(self.sem[e], 1)
            ev = (('c', e), self.cnt[e])
        else:
            ev = (('c', e), self.cnt[e] + 1)
        self._record(ev, reads, writes)

    def dma(self, q, out, in_, reads=(), writes=()):
        dq = self.dq[q]
        j = dq['j']
        i = j % self.NQ
        if j >= self.NQ:
            self._wait(q, (('d', q, i), 16 * (j // self.NQ)))
        for ev in self._deps(reads, writes):
            self._wait(q, ev)
        self.E[q].dma_start(out=out, in_=in_).then_inc(dq['sems'][i], 16)
        dq['j'] += 1
        ev = (('d', q, i), 16 * (j // self.NQ + 1))
        self._record(ev, reads, writes)

    def barrier(self):
        evs = [(('c', k), self.cnt[k]) for k in self.cnt if self.cnt[k] > 0]
        for q, dq in self.dq.items():
            for i in range(self.NQ):
                n = (dq['j'] - i + self.NQ - 1) // self.NQ
                if n > 0:
                    evs.append((('d', q, i), 16 * n))
        for e in self.E:
            for ev in evs:
                self._wait(e, ev)
        self.lastw = {}
        self.readers = {}


A_OFF, K_OFF, V_OFF, QI_OFF, WI_OFF, KI_OFF, CQ_OFF, CKV_OFF, KR_OFF = 0, 1024, 2048, 3072, 4096, 4112, 4176, 4688, 4944


def _fm_chunk_cols():
    chunks = []

    def rot_pair(base, hd, rot, heads):
        half = rot // 2
        a, b = [], []
        for h in heads:
            o = base + h * hd
            a += list(range(o, o + rot))
            b += list(range(o + half, o + rot)) + list(range(o, o + half))
        return a, b

    for base in (A_OFF, K_OFF):
        for c in range(2):
            a, b = rot_pair(base, 128, 32, range(4 * c, 4 * c + 4))
            chunks += [a, b]
        for h in range(8):
            chunks.append(list(range(base + h * 128 + 32, base + h * 128 + 128)))
    for c in range(2):
        a, b = rot_pair(QI_OFF, 64, 16, range(8 * c, 8 * c + 8))
        chunks += [a, b]
    for j in range(8):
        cols = []
        for h in (2 * j, 2 * j + 1):
            cols += list(range(QI_OFF + h * 64 + 16, QI_OFF + h * 64 + 64))
        chunks.append(cols)
    for c in range(4):
        chunks.append(list(range(CQ_OFF + c * 128, CQ_OFF + (c + 1) * 128)))
    for c in range(2):
        chunks.append(list(range(CKV_OFF + c * 128, CKV_OFF + (c + 1) * 128)))
    s1 = list(range(KR_OFF, KR_OFF + 64)) + list(range(KI_OFF, KI_OFF + 64))
    s2 = list(range(KR_OFF + 32, KR_OFF + 64)) + list(range(KR_OFF, KR_OFF + 32)) + \
        list(range(KI_OFF + 8, KI_OFF + 16)) + list(range(KI_OFF, KI_OFF + 8))
    chunks += [s1, s2]
    assert len(chunks) == NFM
    return chunks


def _lhsT_tiles(w, col_chunks):
    K = w.shape[0]
    KC = K // 128
    out = np.zeros((len(col_chunks), 128, KC, 128), np.float32)
    wr = w.reshape(KC, 128, w.shape[1])
    for c, cols in enumerate(col_chunks):
        cols = np.asarray(cols)
        out[c, :, :, :len(cols)] = wr[:, :, cols].transpose(1, 0, 2)
    return out.reshape(len(col_chunks) * 128, KC * 128)


def _rhs_tiles(w, blocks):
    K = w.shape[0]
    KC = K // 128
    wr = w.reshape(KC, 128, w.shape[1])
    outs = []
    for cols in blocks:
        cols = np.asarray(cols)
        outs.append(np.ascontiguousarray(wr[:, :, cols].transpose(1, 0, 2)).reshape(128, KC * len(cols)))
    return np.concatenate(outs, axis=0)


WSHAPES = {
    'win_fm': (NFM * 128, 2048),
    'win_tm': (128, 16 * 1040),
    'wuq': (16 * 128, 512),
    'wukv_k': (8 * 128, 256),
    'wukv_v': (128, 2048),
    'wo': (4 * 128, 16 * 512),
    'wg': (NFC * 128, 2048),
    'wu': (NFC * 128, 2048),
    'wd': (8 * 128, NFC * 256),
}


def _prep_layer(w_in, w_uq, w_ukv, w_o, w_gate, w_up, w_down):
    o = {}
    o['win_fm'] = _lhsT_tiles(w_in, _fm_chunk_cols())
    tm_cols = list(range(V_OFF, V_OFF + 1024)) + list(range(WI_OFF, WI_OFF + 16))
    o['win_tm'] = _rhs_tiles(w_in, [tm_cols])
    uq = []
    for h in range(8):
        uq.append(list(range(h * 192, h * 192 + 128)))
    for c in range(4):
        a, b = [], []
        for h in (2 * c, 2 * c + 1):
            a += list(range(h * 192 + 128, h * 192 + 192))
            b += list(range(h * 192 + 160, h * 192 + 192)) + list(range(h * 192 + 128, h * 192 + 160))
        uq += [a, b]
    o['wuq'] = _lhsT_tiles(w_uq, uq)
    o['wukv_k'] = _lhsT_tiles(w_ukv, [list(range(h * 256, h * 256 + 128)) for h in range(8)])
    vcols = []
    for h in range(8):
        vcols += list(range(h * 256 + 128, h * 256 + 256))
    o['wukv_v'] = _rhs_tiles(w_ukv, [vcols])
    o['wo'] = _rhs_tiles(w_o, [list(range(b * 512, (b + 1) * 512)) for b in range(4)])
    o['wg'] = _lhsT_tiles(w_gate, [list(range(c * 128, (c + 1) * 128)) for c in range(NFC)])
    o['wu'] = _lhsT_tiles(w_up, [list(range(c * 128, (c + 1) * 128)) for c in range(NFC)])
    o['wd'] = _rhs_tiles(w_down, [list(range(b * 256, (b + 1) * 256)) for b in range(8)])
    for k, v in o.items():
        assert v.shape == WSHAPES[k], (k, v.shape)
    return o


def _consts():
    p = np.arange(128)
    cst = np.zeros((128, 8), np.float32)
    j = (p % 32) % 16
    cst[:, 0] = (np.float32(500000.0) ** (-2.0 * j.astype(np.float32) / 32)).astype(np.float32)
    cst[:, 1] = np.where((p % 32) < 16, -1.0, 1.0)
    j = (p % 16) % 8
    cst[:, 2] = (np.float32(500000.0) ** (-2.0 * j.astype(np.float32) / 16)).astype(np.float32)
    cst[:, 3] = np.where((p % 16) < 8, -1.0, 1.0)
    j = (p % 64) % 32
    cst[:, 4] = (np.float32(10000.0) ** (-2.0 * j.astype(np.float32) / 64)).astype(np.float32)
    cst[:, 5] = np.where((p % 64) < 32, -1.0, 1.0)
    ident = np.eye(128, dtype=np.float32)
    t = np.arange(128)[:, None]
    s = np.arange(128)[None, :]
    tri = np.where(s <= t, 0.0, NEG).astype(np.float32)
    cm = np.zeros((128, 4, 512), np.float32)
    for jj in range(4):
        cm[:, jj, :] = ((128 * jj + np.arange(128)[:, None]) <= np.arange(512)[None, :]).astype(np.float32)
    return {'cst': cst, 'ident': ident, 'tri': tri, 'cmask': cm.reshape(128, 2048)}


def build_program(dbg=()):
    nc = bass.Bass("TRN2", target_bir_lowering=False)
    es = ExitStack()
    tr = TR(nc, es)
    op, dma = tr.op, tr.dma

    def din(name, shape, dt=F32):
        return nc.dram_tensor(name, list(shape), dt, kind="ExternalInput").ap()

    def dscr(name, shape, dt):
        kind = "ExternalOutput" if name in dbg else "Internal"
        return nc.dram_tensor(name, list(shape), dt, kind=kind).ap()

    x_in = din('x', [S, D])
    pos_in = din('positions', [1, S], I32)
    out_d = nc.dram_tensor('out', [S, D], F32, kind="ExternalOutput").ap()
    w_in32 = {k: din(k, [DEPTH * v[0], v[1]]) for k, v in WSHAPES.items()}
    gq_in = din('g_cq_t', [DEPTH * 128, 4])
    gkv_in = din('g_ckv_t', [DEPTH * 128, 2])
    ln_in = {k: din(k, [DEPTH, D]) for k in ('ln1_g', 'ln1_b', 'ln2_g', 'ln2_b')}
    cst_in = din('cst', [128, 8])
    ident_in = din('ident', [128, 128])
    tri_in = din('tri', [128, 128])
    cmask_in = din('cmask', [128, 2048])

    wb = {k: dscr('b_' + k, [DEPTH * v[0], v[1]], BF16) for k, v in WSHAPES.items()}
    tab = dscr('tab', [6 * 128, S], F32)
    xa = dscr('xa', [S, D], F32)
    xb_d = dscr('xb', [S, D], F32)
    qaT = dscr('qaT', [8 * 128, S], BF16)
    kaT = dscr('kaT', [8 * 128, S], BF16)
    va_hp = dscr('va_hp', [8 * 128, 32 * 128], BF16)
    qiT = dscr('qiT', [16 * 64, S], BF16)
    kiT = dscr('kiT', [64, S], BF16)
    wi_d = dscr('wi', [S, 16], F32)
    qnT = dscr('qnT', [8 * 128, S], BF16)
    qpT = dscr('qpT', [8 * 64, S], BF16)
    knT = dscr('knT', [8 * 128, S], BF16)
    kpT = dscr('kpT', [64, S], BF16)
    vb_hp = dscr('vb_hp', [8 * 128, 32 * 128], BF16)
    aoT = dscr('aoT', [D, S], BF16)
    if 'dbg_p2' in dbg:
        dacc = dscr('dacc', [256, S], F32)
        dwork = dscr('dwork', [256, S], F32)
        dm8 = dscr('dm8', [256, 8], F32)
        dthr = dscr('dthr', [256, 1], F32)
        dmask = dscr('dmask', [256, S], BF16)

    def sb(name, shape, dt):
        return es.enter_context(nc.sbuf_tensor(name, shape, dt))

    identb = sb('identb', [128, 128], BF16)
    onesb = sb('onesb', [128, 128], BF16)
    onesf = sb('onesf', [128, 128], F32)
    cst = sb('cstt', [128, 8], F32)
    trineg = sb('trineg', [128, 128], F32)
    cmaskb = sb('cmaskb', [128, 2048], BF16)
    negbig = sb('negbig', [128, 1], F32)
    negm = sb('negm', [128, 1], F32)

    PS = [p[:] for p in tr.ps]
    PSB = [p[:].bitcast(BF16) for p in tr.ps]

    with es:
        tr.reset()
        tmpf = tr.alloc(2048, F32)
        dma('sp', tmpf[:, 0:128], ident_in, writes=['tmpf'])
        op('dve', lambda e: e.tensor_copy(out=identb[:], in_=tmpf[:, 0:128]), reads=['tmpf'], writes=['identb'])
        dma('sp', tmpf, cmask_in, writes=['tmpf'])
        op('dve', lambda e: e.tensor_scalar(out=cmaskb[:], in0=tmpf, scalar1=MBIG, scalar2=-MBIG, op0=ALU.mult, op1=ALU.add),
           reads=['tmpf'], writes=['cmaskb'])
        dma('sp', cst[:], cst_in, writes=['cst'])
        dma('sp', trineg[:], tri_in, writes=['trineg'])
        op('dve', lambda e: e.memset(onesb[:], 1.0), writes=['onesb'])
        op('dve', lambda e: e.memset(onesf[:], 1.0), writes=['onesf'])
        op('dve', lambda e: e.memset(negbig[:], -1.0e29), writes=['negbig'])
        op('dve', lambda e: e.memset(negm[:], -MBIG), writes=['negm'])

        P1_W = ('win_fm', 'win_tm', 'wuq', 'wukv_k', 'wukv_v')
        cast_q = []

        def cast_weights(l, names, defer=False):
            for k in names:
                R, C = WSHAPES[k]
                rows = max(128, (1 << 21) // C // 128 * 128)
                r = 0
                while r < R:
                    n = min(rows, R - r)
                    args = (wb[k][l * R + r:l * R + r + n, :], w_in32[k][l * R + r:l * R + r + n, :])
                    if defer:
                        cast_q.append(args)
                    else:
                        dma('pool', args[0], args[1], writes=['W_%s_%d' % (k, l)])
                    r += n

        def drip_casts(n):
            while n > 0 and cast_q:
                a = cast_q.pop(0)
                dma('pool', a[0], a[1])
                n -= 1

        cast_weights(0, P1_W)

        posi = tr.alloc(S, I32)
        posf = tr.alloc(S, F32)
        ang = tr.alloc(S, F32)
        tmp = tr.alloc(S, F32)
        tmpi = tr.alloc(S, I32)
        res = [tr.alloc(S, F32), tr.alloc(S, F32)]
        dma('sp', posi, pos_in[0:1, :].to_broadcast([128, S]), writes=['posi'])
        op('dve', lambda e: e.tensor_copy(out=posf, in_=posi), reads=['posi'], writes=['posf'])
        PI = float(np.pi)
        for ty in range(3):
            for which in range(2):
                rb = res[(2 * ty + which) % 2]
                rt = 'res%d' % ((2 * ty + which) % 2)
                shift = PI / 2 if which == 0 else 0.0
                op('dve', lambda e: e.tensor_scalar(out=ang, in0=posf, scalar1=cst[:, 2 * ty:2 * ty + 1], scalar2=shift,
                                                    op0=ALU.mult, op1=ALU.add), reads=['posf', 'cst'], writes=['ang'])
                op('dve', lambda e: e.tensor_scalar(out=tmp, in0=ang, scalar1=float(1 / (2 * PI)), scalar2=None, op0=ALU.mult),
                   reads=['ang'], writes=['tmp'])
                op('dve', lambda e: e.tensor_copy(out=tmpi, in_=tmp), reads=['tmp'], writes=['tmpi'])
                op('dve', lambda e: e.tensor_copy(out=tmp, in_=tmpi), reads=['tmpi'], writes=['tmp'])
                op('dve', lambda e: e.scalar_tensor_tensor(out=ang, in0=tmp, scalar=float(-2 * PI), in1=ang, op0=ALU.mult, op1=ALU.add),
                   reads=['tmp', 'ang'], writes=['ang'])
                op('dve', lambda e: e.tensor_scalar(out=tmp, in0=ang, scalar1=PI, scalar2=float(-2 * PI), op0=ALU.is_gt, op1=ALU.mult),
                   reads=['ang'], writes=['tmp'])
                op('dve', lambda e: e.tensor_tensor(out=ang, in0=ang, in1=tmp, op=ALU.add), reads=['ang', 'tmp'], writes=['ang'])
                op('dve', lambda e: e.tensor_scalar(out=tmp, in0=ang, scalar1=-PI, scalar2=float(2 * PI), op0=ALU.is_lt, op1=ALU.mult),
                   reads=['ang'], writes=['tmp'])
                op('dve', lambda e: e.tensor_tensor(out=ang, in0=ang, in1=tmp, op=ALU.add), reads=['ang', 'tmp'], writes=['ang'])
                op('dve', lambda e: e.tensor_scalar(out=ang, in0=ang, scalar1=PI, scalar2=-PI, op0=ALU.min, op1=ALU.max),
                   reads=['ang'], writes=['ang'])
                op('act', lambda e: e.activation(out=rb, in_=ang, func=AF.Sin), reads=['ang'], writes=[rt])
                if which == 1:
                    op('dve', lambda e: e.tensor_scalar(out=rb, in0=rb, scalar1=cst[:, 2 * ty + 1:2 * ty + 2], scalar2=None, op0=ALU.mult),
                       reads=[rt, 'cst'], writes=[rt])
                k = 2 * ty + which
                dma('sp', tab[k * 128:(k + 1) * 128, :], rb, reads=[rt])
        tr.barrier()
        cast_weights(0, [k for k in WSHAPES if k not in P1_W], defer=True)
        cast_weights(1, list(WSHAPES), defer=True)

        def layer(l, x_src, x_dst):
            def W(k, r0, r1):
                R = WSHAPES[k][0]
                return wb[k][l * R + r0:l * R + r1, :]

            tr.reset()
            wtm = tr.alloc(16 * 1040, BF16).rearrange("p (k n) -> p k n", k=16)
            xf = [tr.alloc(D, F32) for _ in range(2)]
            xbf = [tr.alloc(D, BF16) for _ in range(2)]
            xT = [tr.alloc(16 * G, BF16).rearrange("p (k n) -> p k n", k=16) for _ in range(2)]
            wfm = [tr.alloc(16 * 128, BF16).rearrange("p (k n) -> p k n", k=16) for _ in range(3)]
            tabs = {k: tr.alloc(G, F32) for k in ('a_c', 'a_s', 'i_c', 'i_s', 'm_c', 'm_s', 's_c', 's_s')}
            t1 = [tr.alloc(G, F32) for _ in range(2)]
            t2 = [tr.alloc(G, F32) for _ in range(2)]
            ob = [tr.alloc(G, BF16) for _ in range(4)]
            cqf = tr.alloc(4 * G, F32).rearrange("p (k n) -> p k n", k=4)
            ckvf = tr.alloc(2 * G, F32).rearrange("p (k n) -> p k n", k=2)
            sq = [tr.alloc(G, F32) for _ in range(2)]
            rstd = {'q': tr.alloc(G, F32), 'kv': tr.alloc(G, F32)}
            cqn = tr.alloc(4 * G, BF16).rearrange("p (k n) -> p k n", k=4)
            ckvn = tr.alloc(2 * G, BF16).rearrange("p (k n) -> p k n", k=2)
            wuq = tr.alloc(16 * 512, BF16).rearrange("p (c k n) -> p c k n", c=16, k=4)
            wukk = tr.alloc(8 * 256, BF16).rearrange("p (c k n) -> p c k n", c=8, k=2)
            wukv = tr.alloc(2 * 1024, BF16).rearrange("p (k n) -> p k n", k=2)
            gq = tr.alloc(4, F32)
            gkv = tr.alloc(2, F32)
            vtm = [tr.alloc(1024, BF16) for _ in range(2)]
            wio = [tr.alloc(16, F32) for _ in range(2)]

            dma('sp', wtm, W('win_tm', 0, 128).rearrange("p (k n) -> p k n", k=16), reads=['W_win_tm_%d' % l], writes=['wtm'])
            for c in range(16):
                dma('sp', wuq[:, c], W('wuq', c * 128, (c + 1) * 128).rearrange("p (k n) -> p k n", k=4),
                    reads=['W_wuq_%d' % l], writes=['wuq'])
            for c in range(8):
                dma('sp', wukk[:, c], W('wukv_k', c * 128, (c + 1) * 128).rearrange("p (k n) -> p k n", k=2),
                    reads=['W_wukv_k_%d' % l], writes=['wukk'])
            dma('sp', wukv, W('wukv_v', 0, 128).rearrange("p (k n) -> p k n", k=2), reads=['W_wukv_v_%d' % l], writes=['wukv'])
            dma('sp', gq, gq_in[l * 128:(l + 1) * 128, :], writes=['gq'])
            dma('sp', gkv, gkv_in[l * 128:(l + 1) * 128, :], writes=['gkv'])
            op('pool', lambda e: e.memset(tabs['s_c'][64:128, :], 1.0), writes=['s_c'])
            op('pool', lambda e: e.memset(tabs['s_s'][64:128, :], 0.0), writes=['s_s'])

            bankrr = [0]

            def nbank(lo=2, n=5):
                b = lo + bankrr[0] % n
                bankrr[0] += 1
                return b

            wcnt = [0]

            def load_wfm(ci):
                b = wcnt[0] % 3
                wcnt[0] += 1
                dma('sp', wfm[b], W('win_fm', ci * 128, (ci + 1) * 128).rearrange("p (k n) -> p k n", k=16),
                    reads=['W_win_fm_%d' % l], writes=['wfm%d' % b])
                return b

            cnt = {'ob': 0, 't': 0, 'sq': 0, 'v': 0}

            def store_rows(src, srct, dsts, ts):
                for (p0, p1, dr) in dsts:
                    dma('pool', dr[:, ts], src[p0:p1, :], reads=[srct])

            def evac_copy(bank, M, dsts, ts):
                k = cnt['ob'] % 4
                cnt['ob'] += 1
                op('act', lambda e: e.copy(out=ob[k][0:M, :], in_=PS[bank][0:M, :]), reads=['ps%d' % bank], writes=['ob%d' % k])
                store_rows(ob[k], 'ob%d' % k, dsts, ts)

            def evac_rope(bankA, bankB, ty, dsts, ts):
                k = cnt['ob'] % 4
                cnt['ob'] += 1
                j = cnt['t'] % 2
                cnt['t'] += 1
                op('dve', lambda e: e.tensor_tensor(out=t1[j], in0=PS[bankA], in1=tabs[ty + '_c'], op=ALU.mult),
                   reads=['ps%d' % bankA, ty + '_c'], writes=['t1%d' % j])
                op('dve', lambda e: e.tensor_tensor(out=t2[j], in0=PS[bankB], in1=tabs[ty + '_s'], op=ALU.mult),
                   reads=['ps%d' % bankB, ty + '_s'], writes=['t2%d' % j])
                op('pool', lambda e: e.tensor_tensor(out=ob[k], in0=t1[j], in1=t2[j], op=ALU.add),
                   reads=['t1%d' % j, 't2%d' % j], writes=['ob%d' % k])
                store_rows(ob[k], 'ob%d' % k, dsts, ts)

            def mm_group(bank, M, N, lhs_fn, rhs_fn, KC, reads):
                for kc in range(KC):
                    last = kc == KC - 1
                    op('pe', lambda e: e.matmul(PS[bank][0:M, 0:N], lhsT=lhs_fn(kc), rhs=rhs_fn(kc), start=(kc == 0), stop=last),
                       reads=reads, writes=['ps%d' % bank], inc=last)

            def rows(t, r0, n):
                return t[r0:r0 + n, :]

            for g in range(NG):
                ts = slice(g * G, (g + 1) * G)
                xt = xT[g % 2]
                xtt = 'xT%d' % (g % 2)
                for nm, k in (('a_c', 0), ('a_s', 1), ('i_c', 2), ('i_s', 3), ('m_c', 4), ('m_s', 5)):
                    dma('sp', tabs[nm], tab[k * 128:(k + 1) * 128, ts], writes=[nm])
                dma('sp', tabs['s_c'][0:64, :], tab[4 * 128:4 * 128 + 64, ts], writes=['s_c'])
                dma('sp', tabs['s_s'][0:64, :], tab[5 * 128:5 * 128 + 64, ts], writes=['s_s'])
                dma('sp', tabs['s_c'][64:80, :], tab[2 * 128 + 64:2 * 128 + 80, ts], writes=['s_c'])
                dma('sp', tabs['s_s'][64:80, :], tab[3 * 128 + 64:3 * 128 + 80, ts], writes=['s_s'])
                for i in range(4):
                    b = i % 2
                    r0 = g * G + i * 128
                    dma('sp', xf[b], x_src[r0:r0 + 128, :], writes=['xf%d' % b])
                    op('pool', lambda e: e.tensor_copy(out=xbf[b], in_=xf[b]), reads=['xf%d' % b], writes=['xbf%d' % b])
                    for hb in range(2):
                        for c in range(8):
                            kc = hb * 8 + c
                            op('pe', lambda e: e.transpose(out=PSB[hb][:, c * 128:(c + 1) * 128], in_=xbf[b][:, kc * 128:(kc + 1) * 128],
                                                           identity=identb[:]),
                               reads=['xbf%d' % b, 'identb'], writes=['ps%d' % hb], inc=(c == 7))
                        eng = 'act' if hb == 0 else 'dve'
                        op(eng, lambda e: (e.copy if eng == 'act' else e.tensor_copy)(
                            out=xt[:, hb * 8:(hb + 1) * 8, i * 128:(i + 1) * 128],
                            in_=PSB[hb].rearrange("p (c n) -> p c n", c=8)), reads=['ps%d' % hb], writes=[xtt])
                for i in range(4):
                    r0 = g * G + i * 128
                    vb_ = cnt['v'] % 2
                    cnt['v'] += 1
                    for cb in range(2):
                        bank = nbank()
                        mm_group(bank, 128, 512, lambda kc: xt[:, kc, i * 128:(i + 1) * 128], lambda kc: wtm[:, kc, cb * 512:(cb + 1) * 512],
                                 16, [xtt, 'wtm'])
                        op('act', lambda e: e.copy(out=vtm[vb_][:, cb * 512:(cb + 1) * 512], in_=PS[bank]), reads=['ps%d' % bank],
                           writes=['vtm%d' % vb_])
                    blk = (g * G) // 128 + i
                    for h in range(8):
                        dma('pool', va_hp[h * 128:(h + 1) * 128, blk * 128:(blk + 1) * 128], vtm[vb_][:, h * 128:(h + 1) * 128],
                            reads=['vtm%d' % vb_])
                    bank = nbank()
                    mm_group(bank, 128, 16, lambda kc: xt[:, kc, i * 128:(i + 1) * 128], lambda kc: wtm[:, kc, 1024:1040], 16, [xtt, 'wtm'])
                    op('act', lambda e: e.activation(out=wio[vb_], in_=PS[bank][:, 0:16], func=AF.Copy, scale=1.0 / 32.0),
                       reads=['ps%d' % bank], writes=['wio%d' % vb_])
                    dma('pool', wi_d[r0:r0 + 128, :], wio[vb_], reads=['wio%d' % vb_])
                ci = 0
                nxt = load_wfm(0)

                def fm(M=128):
                    nonlocal ci, nxt
                    b = nxt
                    if ci + 1 < NFM:
                        nxt = load_wfm(ci + 1)
                    bank = nbank()
                    mm_group(bank, M, G, lambda kc: wfm[b][:, kc, 0:M], lambda kc: xt[:, kc, :], 16, ['wfm%d' % b, xtt])
                    ci += 1
                    return bank

                for (dst, nm) in ((qaT, 'a'), (kaT, 'a')):
                    for c in range(2):
                        ba = fm()
                        bb = fm()
                        evac_rope(ba, bb, 'a', [(32 * k, 32 * k + 32, rows(dst, (4 * c + k) * 128, 32)) for k in range(4)], ts)
                    for h in range(8):
                        ba = fm(96)
                        evac_copy(ba, 96, [(0, 96, rows(dst, h * 128 + 32, 96))], ts)
                for c in range(2):
                    ba = fm()
                    bb = fm()
                    evac_rope(ba, bb, 'i', [(16 * k, 16 * k + 16, rows(qiT, (8 * c + k) * 64, 16)) for k in range(8)], ts)
                for j in range(8):
                    ba = fm(96)
                    evac_copy(ba, 96, [(0, 48, rows(qiT, (2 * j) * 64 + 16, 48)), (48, 96, rows(qiT, (2 * j + 1) * 64 + 16, 48))], ts)
                for (nchunk, dstf, nm) in ((4, cqf, 'q'), (2, ckvf, 'kv')):
                    sbank = 7
                    for c in range(nchunk):
                        ba = fm()
                        op('act', lambda e: e.copy(out=dstf[:, c, :], in_=PS[ba]), reads=['ps%d' % ba], writes=['cf' + nm])
                        k = cnt['sq'] % 2
                        cnt['sq'] += 1
                        op('act', lambda e: e.activation(out=sq[k], in_=PS[ba], func=AF.Square), reads=['ps%d' % ba], writes=['sq%d' % k])
                        op('pe', lambda e: e.matmul(PS[sbank], lhsT=onesf[:], rhs=sq[k], start=(c == 0), stop=(c == nchunk - 1)),
                           reads=['sq%d' % k, 'onesf'], writes=['ps%d' % sbank])
                    rs = rstd[nm]
                    op('dve', lambda e: e.tensor_scalar(out=rs, in0=PS[sbank], scalar1=1.0 / (128 * nchunk), scalar2=RMS_EPS,
                                                        op0=ALU.mult, op1=ALU.add), reads=['ps%d' % sbank], writes=['rs' + nm])
                    op('act', lambda e: e.activation(out=rs, in_=rs, func=AF.Sqrt), reads=['rs' + nm], writes=['rs' + nm])
                    op('dve', lambda e: e.reciprocal(out=rs, in_=rs), reads=['rs' + nm], writes=['rs' + nm])
                    gg = gq if nm == 'q' else gkv
                    dn = cqn if nm == 'q' else ckvn
                    for c in range(nchunk):
                        op('dve', lambda e: e.scalar_tensor_tensor(out=dn[:, c, :], in0=dstf[:, c, :], scalar=gg[:, c:c + 1], in1=rs,
                                                                   op0=ALU.mult, op1=ALU.mult),
                           reads=['cf' + nm, 'rs' + nm, 'gq', 'gkv'], writes=['n' + nm])
                ba = fm()
                bb = fm()
                evac_rope(ba, bb, 's', [(0, 64, kpT), (64, 128, kiT)], ts)
                assert ci == NFM
                for h in range(8):
                    bank = nbank()
                    mm_group(bank, 128, G, lambda kc: wuq[:, h, kc, :], lambda kc: cqn[:, kc, :], 4, ['wuq', 'nq'])
                    evac_copy(bank, 128, [(0, 128, rows(qnT, h * 128, 128))], ts)
                for c in range(4):
                    ba = nbank()
                    mm_group(ba, 128, G, lambda kc: wuq[:, 8 + 2 * c, kc, :], lambda kc: cqn[:, kc, :], 4, ['wuq', 'nq'])
                    bb = nbank()
                    mm_group(bb, 128, G, lambda kc: wuq[:, 9 + 2 * c, kc, :], lambda kc: cqn[:, kc, :], 4, ['wuq', 'nq'])
                    evac_rope(ba, bb, 'm', [(0, 64, rows(qpT, (2 * c) * 64, 64)), (64, 128, rows(qpT, (2 * c + 1) * 64, 64))], ts)
                for h in range(8):
                    bank = nbank()
                    mm_group(bank, 128, G, lambda kc: wukk[:, h, kc, :], lambda kc: ckvn[:, kc, :], 2, ['wukk', 'nkv'])
                    evac_copy(bank, 128, [(0, 128, rows(knT, h * 128, 128))], ts)
                for i in range(4):
                    vb_ = cnt['v'] % 2
                    cnt['v'] += 1
                    for cb in range(2):
                        bank = nbank()
                        mm_group(bank, 128, 512, lambda kc: ckvn[:, kc, i * 128:(i + 1) * 128], lambda kc: wukv[:, kc, cb * 512:(cb + 1) * 512],
                                 2, ['nkv', 'wukv'])
                        op('act', lambda e: e.copy(out=vtm[vb_][:, cb * 512:(cb + 1) * 512], in_=PS[bank]), reads=['ps%d' % bank],
                           writes=['vtm%d' % vb_])
                    blk = (g * G) // 128 + i
                    for h in range(8):
                        dma('pool', vb_hp[h * 128:(h + 1) * 128, blk * 128:(blk + 1) * 128], vtm[vb_][:, h * 128:(h + 1) * 128],
                            reads=['vtm%d' % vb_])
            tr.barrier()
            if 'stop_p1' in dbg:
                return

            tr.reset()
            ki2 = tr.alloc(S, BF16)
            qit = [tr.alloc(8 * 128, BF16).rearrange("p (j n) -> p j n", j=8) for _ in range(2)]
            wit = [tr.alloc(16, F32) for _ in range(2)]
            acc = [tr.alloc(S, F32) for _ in range(2)]
            work = tr.alloc(S, F32)
            rl = [tr.alloc(512, BF16) for _ in range(4)]
            diag = [tr.alloc(16 * 128, BF16).rearrange("p (h n) -> p h n", h=16) for _ in range(2)]
            m8 = [tr.alloc(8, F32) for _ in range(2)]
            thr = [tr.alloc(1, F32) for _ in range(2)]
            maskb = [tr.alloc(S, BF16) for _ in range(4)]
            maskT = tr.alloc(32 * 512, BF16).rearrange("p (b n) -> p b n", b=32)
            kT = [tr.alloc(S, BF16) for _ in range(2)]
            vv = [tr.alloc(32 * 128, BF16).rearrange("p (b n) -> p b n", b=32) for _ in range(2)]
            qq = [tr.alloc(512, BF16) for _ in range(2)]
            kpe = tr.alloc(S, BF16)
            qpe = [tr.alloc(512, BF16) for _ in range(2)]
            Pb = [tr.alloc(512, BF16) for _ in range(3)]
            rc = [tr.alloc(512, F32) for _ in range(2)]
            oo = [tr.alloc(512, BF16) for _ in range(2)]
            pos_ = [tr.alloc(512, F32) for _ in range(2)]

            dma('sp', ki2[0:64, :], kiT, writes=['ki2'])
            dma('sp', ki2[64:128, :], kiT, writes=['ki2'])
            dma('sp', kpe[0:64, :], kpT, writes=['kpe'])
            c2 = {'q': 0, 'rl': 0, 'ib': 0, 'tb': 0, 'kv': 0, 'P': 0, 'qk': 0, 'o': 0, 'm8': 0}
            qi_v = qiT.rearrange("(j two d) s -> (two d) j s", two=2, d=64)
            cm4 = cmaskb[:].rearrange("p (j n) -> p j n", j=4)

            def score_tile(qt, q4):
                L = (qt + 1) * 128
                b = c2['q'] % 2
                c2['q'] += 1
                a = acc[b]
                at = 'acc%d' % b
                dma('sp', qit[b], qi_v[:, :, qt * 128:(qt + 1) * 128], writes=['qit%d' % b])
                dma('sp', wit[b], wi_d[qt * 128:(qt + 1) * 128, :], writes=['wit%d' % b])
                dg = diag[b]
                for h in range(16):
                    op('pool', lambda e: e.tensor_scalar(out=dg[:, h, :], in0=identb[:], scalar1=wit[b][:, h:h + 1], scalar2=None, op0=ALU.mult),
                       reads=['identb', 'wit%d' % b], writes=['diag%d' % b])
                nsb = (L + 511) // 512
                for sbk in range(nsb):
                    n = min(512, L - sbk * 512)
                    ks = []

                    def a_mm(h):
                        bank = (0, 1, 3)[c2['ib'] % 3]
                        c2['ib'] += 1
                        p0 = (h % 2) * 64
                        op('pe', lambda e: e.matmul(PS[bank][:, 0:n], lhsT=qit[b][p0:p0 + 64, h // 2, :], rhs=ki2[p0:p0 + 64, sbk * 512:sbk * 512 + n],
                                                    start=True, stop=True), reads=['qit%d' % b, 'ki2'], writes=['ps%d' % bank])
                        k = c2['rl'] % 4
                        c2['rl'] += 1
                        op('act', lambda e: e.activation(out=rl[k][:, 0:n], in_=PS[bank][:, 0:n], func=AF.Relu), reads=['ps%d' % bank],
                           writes=['rl%d' % k])
                        ks.append(k)

                    def d_mm(h):
                        k = ks[h]
                        op('pe', lambda e: e.matmul(PS[2][:, 0:n], lhsT=dg[:, h, :], rhs=rl[k][:, 0:n], start=(h == 0), stop=(h == 15)),
                           reads=['diag%d' % b, 'rl%d' % k], writes=['ps2'])

                    a_mm(0)
                    a_mm(1)
                    for h in range(16):
                        if h + 2 < 16:
                            a_mm(h + 2)
                        d_mm(h)
                    op('act', lambda e: e.copy(out=a[:, sbk * 512:sbk * 512 + n], in_=PS[2][:, 0:n]), reads=['ps2'], writes=[at])
                op('dve', lambda e: e.tensor_tensor(out=a[:, qt * 128:L], in0=a[:, qt * 128:L], in1=trineg[:], op=ALU.add),
                   reads=[at, 'trineg'], writes=[at])
                mk = maskb[q4]
                mt = 'maskb%d' % q4
                if L > 256:
                    for r in range(32):
                        mi = c2['m8'] % 2
                        c2['m8'] += 1
                        src = a[:, 0:L] if r == 0 else work[:, 0:L]
                        op('dve', lambda e: e.max(out=m8[mi], in_=src), reads=[at, 'work'], writes=['m8%d' % mi])
                        if r < 31:
                            op('dve', lambda e: e.match_replace(out=work[:, 0:L], in_to_replace=m8[mi], in_values=src, imm_value=NEG),
                               reads=[at, 'work', 'm8%d' % mi], writes=['work'])
                    tb_ = thr[b]
                    op('dve', lambda e: e.tensor_scalar(out=tb_, in0=m8[mi][:, 7:8], scalar1=-1.0e29, scalar2=None, op0=ALU.max),
                       reads=['m8%d' % mi], writes=['thr%d' % b])
                    op('dve', lambda e: e.tensor_scalar(out=mk[:, 0:L], in0=a[:, 0:L], scalar1=tb_, scalar2=None, op0=ALU.is_ge),
                       reads=[at, 'thr%d' % b], writes=[mt])
                else:
                    op('dve', lambda e: e.tensor_scalar(out=mk[:, 0:L], in0=a[:, 0:L], scalar1=negbig[:], scalar2=None, op0=ALU.is_ge),
                       reads=[at, 'negbig'], writes=[mt])
                if 'dbg_p2' in dbg and l == 0 and qt in (2, 9):
                    di = 0 if qt == 2 else 1
                    dma('sp', dacc[di * 128:(di + 1) * 128, 0:L], a[:, 0:L], reads=[at])
                    dma('sp', dwork[di * 128:(di + 1) * 128, 0:L], work[:, 0:L], reads=['work'])
                    dma('sp', dm8[di * 128:(di + 1) * 128, :], m8[mi], reads=['m8%d' % mi])
                    dma('sp', dthr[di * 128:(di + 1) * 128, :], thr[b], reads=['thr%d' % b])
                    dma('sp', dmask[di * 128:(di + 1) * 128, 0:L], mk[:, 0:L], reads=[mt])
                Lmax = (qt // 4 + 1) * 512
                if L < Lmax:
                    op('dve', lambda e: e.memset(mk[:, L:Lmax], 0.0), writes=[mt])

            def transposes_tile(j):
                tbk, q4 = j // 4, j % 4
                nblk = 4 * tbk + 4
                blk0 = 0
                while blk0 < nblk:
                    nb = min(8, nblk - blk0)
                    for i in range(nb):
                        blk = blk0 + i
                        op('pe', lambda e: e.transpose(out=PSB[3][:, i * 128:(i + 1) * 128], in_=maskb[q4][:, blk * 128:(blk + 1) * 128],
                                                       identity=identb[:]),
                           reads=['maskb%d' % q4, 'identb'], writes=['ps3'], inc=(i == nb - 1))
                    op('act', lambda e: e.activation(out=maskT[:, blk0:blk0 + nb, q4 * 128:(q4 + 1) * 128],
                                                     in_=PSB[3][:, 0:nb * 128].rearrange("p (b n) -> p b n", b=nb),
                                                     func=AF.Identity, scale=MBIG, bias=negm[:]),
                       reads=['ps3', 'negm'], writes=['maskT'])
                    blk0 += nb

            def attn_head(tbk, h, mla):
                nblk = 4 * tbk + 4
                Lmax = nblk * 128
                tsl = slice(tbk * 512, (tbk + 1) * 512)
                b = c2['kv'] % 2
                c2['kv'] += 1
                ksrc, vsrc, qsrc = (knT, vb_hp, qnT) if mla else (kaT, va_hp, qaT)
                dma('sp', kT[b][:, 0:Lmax], ksrc[h * 128:(h + 1) * 128, 0:Lmax], writes=['kT%d' % b])
                dma('sp', vv[b][:, 0:nblk, :], vsrc[h * 128:(h + 1) * 128, 0:Lmax].rearrange("p (b n) -> p b n", n=128), writes=['vv%d' % b])
                dma('sp', qq[b], qsrc[h * 128:(h + 1) * 128, tsl], writes=['qq%d' % b])
                if mla:
                    dma('sp', qpe[b][0:64, :], qpT[h * 64:(h + 1) * 64, tsl], writes=['qpe%d' % b])
                scale = float((192.0 if mla else 128.0) ** -0.5)
                qkb = {}

                def qk(sc):
                    bank = 4 + c2['qk'] % 2
                    c2['qk'] += 1
                    qkb[sc] = bank
                    diag_blk = mla and sc >= 4 * tbk
                    op('pe', lambda e: e.matmul(PS[bank], lhsT=kT[b][:, sc * 128:(sc + 1) * 128], rhs=qq[b], start=True, stop=False),
                       reads=['kT%d' % b, 'qq%d' % b], writes=['ps%d' % bank], inc=False)
                    if mla:
                        op('pe', lambda e: e.matmul(PS[bank], lhsT=kpe[0:64, sc * 128:(sc + 1) * 128], rhs=qpe[b][0:64, :], start=False,
                                                    stop=not diag_blk),
                           reads=['kpe', 'qpe%d' % b], writes=['ps%d' % bank], inc=not diag_blk)
                        if diag_blk:
                            op('pe', lambda e: e.matmul(PS[bank], lhsT=identb[:], rhs=cm4[:, sc - 4 * tbk, :], start=False, stop=True),
                               reads=['identb', 'cmaskb'], writes=['ps%d' % bank])
                    else:
                        op('pe', lambda e: e.matmul(PS[bank], lhsT=identb[:], rhs=maskT[:, sc, :], start=False, stop=True),
                           reads=['identb', 'maskT'], writes=['ps%d' % bank])

                qk(0)
                for sc in range(nblk):
                    if sc + 1 < nblk:
                        qk(sc + 1)
                    bank = qkb[sc]
                    k = c2['P'] % 3
                    c2['P'] += 1
                    op('act', lambda e: e.activation(out=Pb[k], in_=PS[bank], func=AF.Exp, scale=scale), reads=['ps%d' % bank], writes=['P%d' % k])
                    op('pe', lambda e: e.matmul(PS[6], lhsT=vv[b][:, sc, :], rhs=Pb[k], start=(sc == 0), stop=(sc == nblk - 1)),
                       reads=['vv%d' % b, 'P%d' % k], writes=['ps6'], inc=False)
                    op('pe', lambda e: e.matmul(PS[7], lhsT=onesb[:], rhs=Pb[k], start=(sc == 0), stop=(sc == nblk - 1)),
                       reads=['onesb', 'P%d' % k], writes=['ps7'])
                j = c2['o'] % 2
                c2['o'] += 1
                op('act', lambda e: e.activation(out=rc[j], in_=PS[7], func=AF.Ln), reads=['ps7'], writes=['rc%d' % j])
                op('act', lambda e: e.copy(out=pos_[j], in_=PS[6]), reads=['ps6'], writes=['pos%d' % j])
                op('act', lambda e: e.activation(out=rc[j], in_=rc[j], func=AF.Exp, scale=-1.0), reads=['rc%d' % j], writes=['rc%d' % j])
                op('pool', lambda e: e.tensor_tensor(out=oo[j], in0=pos_[j], in1=rc[j], op=ALU.mult),
                   reads=['pos%d' % j, 'rc%d' % j], writes=['oo%d' % j])
                r0 = (8 + h if mla else h) * 128
                dma('pool', aoT[r0:r0 + 128, tsl], oo[j], reads=['oo%d' % j])

            NTB = 3 if 'dbg_p2' in dbg else 8
            NT = 4 * NTB
            st = {'scored': 0, 'T': 0}
            dsa_done = [0] * NTB
            mla_done = [0] * NTB

            def ensure_dsa(tb_, n):
                ensure_T(4 * tb_ + 4)
                while dsa_done[tb_] < n:
                    attn_head(tb_, dsa_done[tb_], False)
                    dsa_done[tb_] += 1

            def ensure_T(upto):
                while st['T'] < upto:
                    j = st['T']
                    if j % 4 == 0 and j >= 4:
                        ensure_dsa(j // 4 - 1, 8)
                    assert st['scored'] > j
                    transposes_tile(j)
                    st['T'] += 1

            k = 0
            while True:
                if k < NT:
                    if k >= 4:
                        ensure_T(k - 3)
                    score_tile(k, k % 4)
                    st['scored'] += 1
                njobs = 0
                while njobs < 4:
                    cand = [t for t in range(NTB) if dsa_done[t] < 8 and st['T'] >= 4 * t + 4]
                    if cand:
                        t = cand[0]
                        attn_head(t, dsa_done[t], False)
                        dsa_done[t] += 1
                        njobs += 1
                        continue
                    cand = [t for t in range(NTB) if mla_done[t] < 8 and t <= k // 4 + 2]
                    if cand:
                        t = cand[0]
                        attn_head(t, mla_done[t], True)
                        mla_done[t] += 1
                        njobs += 1
                        continue
                    break
                ensure_T(max(0, min(k - 1, NT, st['scored'])))
                drip_casts(2)
                k += 1
                if k >= NT and st['T'] >= NT and all(d == 8 for d in dsa_done) and all(d == 8 for d in mla_done):
                    break
                assert k < NT + 64
            drip_casts(1000)
            tr.barrier()
            if 'stop_p2' in dbg:
                return

            def layer_norm(xt_, xtt_, gt, bt, stt, mv, sd):
                xv = xt_.rearrange("p (c n) -> p c n", c=4)
                for c in range(4):
                    op('dve', lambda e: e.bn_stats(out=stt[:, c, :], in_=xv[:, c, :]), reads=[xtt_], writes=['stt'])
                op('dve', lambda e: e.bn_aggr(out=mv, in_=stt), reads=['stt'], writes=['mv'])
                op('dve', lambda e: e.tensor_scalar(out=sd, in0=mv[:, 1:2], scalar1=LN_EPS, scalar2=None, op0=ALU.add), reads=['mv'], writes=['sd'])
                op('act', lambda e: e.activation(out=sd, in_=sd, func=AF.Sqrt), reads=['sd'], writes=['sd'])
                op('dve', lambda e: e.reciprocal(out=sd, in_=sd), reads=['sd'], writes=['sd'])
                op('dve', lambda e: e.tensor_scalar(out=xt_, in0=xt_, scalar1=mv[:, 0:1], scalar2=sd, op0=ALU.subtract, op1=ALU.mult),
                   reads=[xtt_, 'mv', 'sd'], writes=[xtt_])
                op('pool', lambda e: e.tensor_tensor(out=xt_, in0=xt_, in1=gt, op=ALU.mult), reads=[xtt_, 'lng'], writes=[xtt_])
                op('pool', lambda e: e.tensor_tensor(out=xt_, in0=xt_, in1=bt, op=ALU.add), reads=[xtt_, 'lnb'], writes=[xtt_])

            tr.reset()
            aT = [tr.alloc(16 * G, BF16).rearrange("p (k n) -> p k n", k=16) for _ in range(2)]
            wo = [tr.alloc(16 * 512, BF16).rearrange("p (k n) -> p k n", k=16) for _ in range(2)]
            xs = [tr.alloc(D, F32) for _ in range(8)]
            lng = tr.alloc(D, F32)
            lnb = tr.alloc(D, F32)
            stt = tr.alloc(4 * 6, F32).rearrange("p (c n) -> p c n", c=4)
            mv = tr.alloc(2, F32)
            sd = tr.alloc(1, F32)
            dma('sp', lng, ln_in['ln1_g'][l:l + 1, :].to_broadcast([128, D]), writes=['lng'])
            dma('sp', lnb, ln_in['ln1_b'][l:l + 1, :].to_broadcast([128, D]), writes=['lnb'])
            aoT_v = aoT.rearrange("(k p) s -> p k s", p=128)
            wc = 0
            bk = 0
            for g in range(NG):
                ab = g % 2
                dma('sp', aT[ab], aoT_v[:, :, g * G:(g + 1) * G], writes=['aT%d' % ab])
                for i in range(4):
                    xi = (g % 2) * 4 + i
                    r0 = g * G + i * 128
                    dma('sp', xs[xi], x_src[r0:r0 + 128, :], writes=['xs%d' % xi])
                for cb in range(4):
                    wbf = wc % 2
                    wc += 1
                    dma('sp', wo[wbf], W('wo', cb * 128, (cb + 1) * 128).rearrange("p (k n) -> p k n", k=16), writes=['wo%d' % wbf])
                    for i in range(4):
                        xi = (g % 2) * 4 + i
                        bank = bk % 6
                        bk += 1
                        mm_group(bank, 128, 512, lambda kc: aT[ab][:, kc, i * 128:(i + 1) * 128], lambda kc: wo[wbf][:, kc, :], 16,
                                 ['aT%d' % ab, 'wo%d' % wbf])
                        xsl = xs[xi][:, cb * 512:(cb + 1) * 512]
                        op('dve', lambda e: e.scalar_tensor_tensor(out=xsl, in0=xsl, scalar=ALPHA, in1=PS[bank], op0=ALU.mult, op1=ALU.add),
                           reads=['ps%d' % bank, 'xs%d' % xi], writes=['xs%d' % xi])
                for i in range(4):
                    xi = (g % 2) * 4 + i
                    r0 = g * G + i * 128
                    layer_norm(xs[xi], 'xs%d' % xi, lng, lnb, stt, mv, sd)
                    dma('pool', xa[r0:r0 + 128, :], xs[xi], reads=['xs%d' % xi])
            tr.barrier()
            if 'stop_p3' in dbg:
                return

            tr.reset()
            x1 = [tr.alloc(D, F32) for _ in range(4)]
            xbf = [tr.alloc(D, BF16) for _ in range(2)]
            x1T = tr.alloc(16 * G, BF16).rearrange("p (k n) -> p k n", k=16)
            hT = tr.alloc(NFC * G, BF16).rearrange("p (k n) -> p k n", k=NFC)
            wgb = [tr.alloc(16 * 128, BF16).rearrange("p (k n) -> p k n", k=16) for _ in range(2)]
            wub = [tr.alloc(16 * 128, BF16).rearrange("p (k n) -> p k n", k=16) for _ in range(2)]
            wdb = [tr.alloc(NFC * 256, BF16).rearrange("p (k n) -> p k n", k=NFC) for _ in range(2)]
            sg = [tr.alloc(G, F32) for _ in range(2)]
            lng = tr.alloc(D, F32)
            lnb = tr.alloc(D, F32)
            stt = tr.alloc(4 * 6, F32).rearrange("p (c n) -> p c n", c=4)
            mv = tr.alloc(2, F32)
            sd = tr.alloc(1, F32)
            dma('sp', lng, ln_in['ln2_g'][l:l + 1, :].to_broadcast([128, D]), writes=['lng'])
            dma('sp', lnb, ln_in['ln2_b'][l:l + 1, :].to_broadcast([128, D]), writes=['lnb'])
            wc = 0
            wdc = 0
            bk = 0
            for g in range(NG):
                for i in range(4):
                    r0 = g * G + i * 128
                    b = i % 2
                    dma('sp', x1[i], xa[r0:r0 + 128, :], writes=['x1%d' % i])
                    op('pool', lambda e: e.tensor_copy(out=xbf[b], in_=x1[i]), reads=['x1%d' % i], writes=['xbf%d' % b])
                    for hb in range(2):
                        for c in range(8):
                            kc = hb * 8 + c
                            op('pe', lambda e: e.transpose(out=PSB[6 + hb][:, c * 128:(c + 1) * 128], in_=xbf[b][:, kc * 128:(kc + 1) * 128],
                                                           identity=identb[:]),
                               reads=['xbf%d' % b, 'identb'], writes=['ps%d' % (6 + hb)], inc=(c == 7))
                        eng = 'act' if hb == 0 else 'dve'
                        op(eng, lambda e: (e.copy if eng == 'act' else e.tensor_copy)(
                            out=x1T[:, hb * 8:(hb + 1) * 8, i * 128:(i + 1) * 128],
                            in_=PSB[6 + hb].rearrange("p (c n) -> p c n", c=8)), reads=['ps%d' % (6 + hb)], writes=['x1T'])
                for fc in range(NFC):
                    wbf = wc % 2
                    wc += 1
                    dma('sp', wgb[wbf], W('wg', fc * 128, (fc + 1) * 128).rearrange("p (k n) -> p k n", k=16), writes=['wg%d' % wbf])
                    dma('sp', wub[wbf], W('wu', fc * 128, (fc + 1) * 128).rearrange("p (k n) -> p k n", k=16), writes=['wu%d' % wbf])
                    bg = (bk % 3) * 2
                    bu = bg + 1
                    bk += 1
                    mm_group(bg, 128, G, lambda kc: wgb[wbf][:, kc, :], lambda kc: x1T[:, kc, :], 16, ['wg%d' % wbf, 'x1T'])
                    mm_group(bu, 128, G, lambda kc: wub[wbf][:, kc, :], lambda kc: x1T[:, kc, :], 16, ['wu%d' % wbf, 'x1T'])
                    k = fc % 2
                    op('act', lambda e: e.activation(out=sg[k], in_=PS[bg], func=AF.Silu), reads=['ps%d' % bg], writes=['sg%d' % k])
                    op('dve', lambda e: e.tensor_tensor(out=hT[:, fc, :], in0=sg[k], in1=PS[bu], op=ALU.mult),
                       reads=['sg%d' % k, 'ps%d' % bu], writes=['hT'])
                for cbh in range(8):
                    wbf = wdc % 2
                    wdc += 1
                    dma('sp', wdb[wbf], W('wd', cbh * 128, (cbh + 1) * 128).rearrange("p (k n) -> p k n", k=NFC), writes=['wd%d' % wbf])
                    for i in range(4):
                        bank = bk % 6
                        bk += 1
                        mm_group(bank, 128, 256, lambda kc: hT[:, kc, i * 128:(i + 1) * 128], lambda kc: wdb[wbf][:, kc, :], NFC,
                                 ['hT', 'wd%d' % wbf])
                        xsl = x1[i][:, cbh * 256:(cbh + 1) * 256]
                        op('dve', lambda e: e.scalar_tensor_tensor(out=xsl, in0=xsl, scalar=ALPHA, in1=PS[bank][:, 0:256], op0=ALU.mult, op1=ALU.add),
                           reads=['ps%d' % bank, 'x1%d' % i], writes=['x1%d' % i])
                for i in range(4):
                    r0 = g * G + i * 128
                    layer_norm(x1[i], 'x1%d' % i, lng, lnb, stt, mv, sd)
                    dma('pool', x_dst[r0:r0 + 128, :], x1[i], reads=['x1%d' % i])
            tr.barrier()

        layer(0, x_in, xb_d)
        if not any(k.startswith('stop') for k in dbg):
            layer(1, xb_d, out_d)
        tr.barrier()
    return nc


_CACHE = {}


def _host_inputs(inputs):
    f = lambda a: np.asarray(a, dtype=np.float32)
    layers = [_prep_layer(f(inputs['w_in'][l]), f(inputs['w_uq'][l]), f(inputs['w_ukv'][l]), f(inputs['w_o'][l]),
                          f(inputs['w_gate'][l]), f(inputs['w_up'][l]), f(inputs['w_down'][l])) for l in range(DEPTH)]
    shared = {k: np.ascontiguousarray(np.concatenate([layers[l][k] for l in range(DEPTH)], axis=0)) for k in WSHAPES}
    shared.update(_consts())
    shared['g_cq_t'] = np.ascontiguousarray(f(inputs['g_cq']).reshape(DEPTH, 4, 128).transpose(0, 2, 1)).reshape(DEPTH * 128, 4)
    shared['g_ckv_t'] = np.ascontiguousarray(f(inputs['g_ckv']).reshape(DEPTH, 2, 128).transpose(0, 2, 1)).reshape(DEPTH * 128, 2)
    for k in ('ln1_g', 'ln1_b', 'ln2_g', 'ln2_b'):
        shared[k] = f(inputs[k])
    return shared


def kernel(**inputs):
    x = np.asarray(inputs['x'], dtype=np.float32)
    pos = np.asarray(inputs['positions']).astype(np.int32)
    shared = _host_inputs(inputs)
    if 'nc' not in _CACHE:
        _CACHE['nc'] = build_program()
    nc = _CACHE['nc']
    in_maps = []
    for b in range(4):
        m = dict(shared)
        m['x'] = np.ascontiguousarray(x[b])
        m['positions'] = np.ascontiguousarray(pos[b:b + 1])
        in_maps.append(m)
    res = run_bass_kernel_spmd(nc, in_maps, core_ids=list(range(4)))
    out = np.stack([np.asarray(res.results[b]['out'], dtype=np.float32) for b in range(4)], axis=0)
    return out
```

```python
import numpy as np
from contextlib import ExitStack
import concourse.bass as bass
import concourse.mybir as mybir
from concourse.bass_utils import run_bass_kernel_spmd

F32 = mybir.dt.float32
BF16 = mybir.dt.bfloat16
I32 = mybir.dt.int32
ALU = mybir.AluOpType
AF = mybir.ActivationFunctionType

S = 4096
D = 2048
DEPTH = 2
FF = 5632
NFC = FF // 128
G = 512
NG = S // G
ALPHA = float((2 * DEPTH) ** 0.25)
LN_EPS = 1e-5
RMS_EPS = 1e-6
NEG = -1.0e30
MBIG = 30000.0
NFM = 44
ARENA = 50 * 1024


def _dsize(dt):
    return 4 if dt in (F32, I32) else 2


class TR:
    NQ = 8

    def __init__(self, nc, es):
        self.nc = nc
        self.E = {'pe': nc.tensor, 'act': nc.scalar, 'dve': nc.vector, 'pool': nc.gpsimd, 'sp': nc.sync}
        self.sem = {k: es.enter_context(nc.semaphore('s_' + k)) for k in ('pe', 'act', 'dve', 'pool')}
        self.cnt = {k: 0 for k in self.sem}
        self.dq = {}
        for q in ('sp', 'pool', 'act'):
            self.dq[q] = {'sems': [es.enter_context(nc.semaphore('d_%s%d' % (q, i))) for i in range(self.NQ)], 'j': 0}
        self.seen = {e: {} for e in self.E}
        self.lastw = {}
        self.readers = {}
        self.arena = es.enter_context(nc.sbuf_tensor('arena', [128, ARENA], F32))
        self.off = 0
        self.ps = [es.enter_context(nc.psum_tensor('ps%d' % i, [128, 512], F32)) for i in range(8)]

    def reset(self):
        self.off = 0

    def alloc(self, n, dt):
        nb = n * _dsize(dt)
        n4 = (nb + 31) // 32 * 8
        ap = self.arena[:, self.off:self.off + n4]
        self.off += n4
        assert self.off <= ARENA, ('arena overflow', self.off)
        if dt != F32:
            ap = ap.bitcast(dt)
        return ap[:, 0:n]

    def _handle(self, key):
        if key[0] == 'c':
            return self.sem[key[1]]
        return self.dq[key[1]]['sems'][key[2]]

    def _wait(self, e, ev):
        if ev is None:
            return
        key, val = ev
        if key == ('c', e) and e == 'pe':
            return
        if self.seen[e].get(key, 0) >= val:
            return
        self.E[e].wait_ge(self._handle(key), val)
        self.seen[e][key] = val

    def _deps(self, reads, writes):
        out = []
        for t in reads:
            ev = self.lastw.get(t)
            if ev is not None:
                out.append(ev)
        for t in writes:
            ev = self.lastw.get(t)
            if ev is not None:
                out.append(ev)
            for k, v in self.readers.get(t, {}).items():
                out.append((k, v))
        return out

    def _record(self, ev, reads, writes):
        for t in reads:
            r = self.readers.setdefault(t, {})
            if r.get(ev[0], 0) < ev[1]:
                r[ev[0]] = ev[1]
        for t in writes:
            self.lastw[t] = ev
            self.readers[t] = {}

    def op(self, e, fn, reads=(), writes=(), inc=True):
        for ev in self._deps(reads, writes):
            self._wait(e, ev)
        ins = fn(self.E[e])
        if inc:
            self.cnt[e] += 1
            ins.then_inc(self.sem[e], 1)
            ev = (('c', e), self.cnt[e])
        else:
            ev = (('c', e), self.cnt[e] + 1)
        self._record(ev, reads, writes)

    def dma(self, q, out, in_, reads=(), writes=()):
        dq = self.dq[q]
        j = dq['j']
        i = j % self.NQ
        if j >= self.NQ:
            self._wait(q, (('d', q, i), 16 * (j // self.NQ)))
        for ev in self._deps(reads, writes):
            self._wait(q, ev)
        self.E[q].dma_start(out=out, in_=in_).then_inc(dq['sems'][i], 16)
        dq['j'] += 1
        ev = (('d', q, i), 16 * (j // self.NQ + 1))
        self._record(ev, reads, writes)

    def barrier(self):
        evs = [(('c', k), self.cnt[k]) for k in self.cnt if self.cnt[k] > 0]
        for q, dq in self.dq.items():
            for i in range(self.NQ):
                n = (dq['j'] - i + self.NQ - 1) // self.NQ
                if n > 0:
                    evs.append((('d', q, i), 16 * n))
        for e in self.E:
            for ev in evs:
                self._wait(e, ev)
        self.lastw = {}
        self.readers = {}


A_OFF, K_OFF, V_OFF, QI_OFF, WI_OFF, KI_OFF, CQ_OFF, CKV_OFF, KR_OFF = 0, 1024, 2048, 3072, 4096, 4112, 4176, 4688, 4944


def _fm_chunk_cols():
    chunks = []

    def rot_pair(base, hd, rot, heads):
        half = rot // 2
        a, b = [], []
        for h in heads:
            o = base + h * hd
            a += list(range(o, o + rot))
            b += list(range(o + half, o + rot)) + list(range(o, o + half))
        return a, b

    for base in (A_OFF, K_OFF):
        for c in range(2):
            a, b = rot_pair(base, 128, 32, range(4 * c, 4 * c + 4))
            chunks += [a, b]
        for h in range(8):
            chunks.append(list(range(base + h * 128 + 32, base + h * 128 + 128)))
    for c in range(2):
        a, b = rot_pair(QI_OFF, 64, 16, range(8 * c, 8 * c + 8))
        chunks += [a, b]
    for j in range(8):
        cols = []
        for h in (2 * j, 2 * j + 1):
            cols += list(range(QI_OFF + h * 64 + 16, QI_OFF + h * 64 + 64))
        chunks.append(cols)
    for c in range(4):
        chunks.append(list(range(CQ_OFF + c * 128, CQ_OFF + (c + 1) * 128)))
    for c in range(2):
        chunks.append(list(range(CKV_OFF + c * 128, CKV_OFF + (c + 1) * 128)))
    s1 = list(range(KR_OFF, KR_OFF + 64)) + list(range(KI_OFF, KI_OFF + 64))
    s2 = list(range(KR_OFF + 32, KR_OFF + 64)) + list(range(KR_OFF, KR_OFF + 32)) + \
        list(range(KI_OFF + 8, KI_OFF + 16)) + list(range(KI_OFF, KI_OFF + 8))
    chunks += [s1, s2]
    assert len(chunks) == NFM
    return chunks


def _lhsT_tiles(w, col_chunks):
    K = w.shape[0]
    KC = K // 128
    out = np.zeros((len(col_chunks), 128, KC, 128), np.float32)
    wr = w.reshape(KC, 128, w.shape[1])
    for c, cols in enumerate(col_chunks):
        cols = np.asarray(cols)
        out[c, :, :, :len(cols)] = wr[:, :, cols].transpose(1, 0, 2)
    return out.reshape(len(col_chunks) * 128, KC * 128)


def _rhs_tiles(w, blocks):
    K = w.shape[0]
    KC = K // 128
    wr = w.reshape(KC, 128, w.shape[1])
    outs = []
    for cols in blocks:
        cols = np.asarray(cols)
        outs.append(np.ascontiguousarray(wr[:, :, cols].transpose(1, 0, 2)).reshape(128, KC * len(cols)))
    return np.concatenate(outs, axis=0)


WSHAPES = {
    'win_fm': (NFM * 128, 2048),
    'win_tm': (128, 16 * 1040),
    'wuq': (16 * 128, 512),
    'wukv_k': (8 * 128, 256),
    'wukv_v': (128, 2048),
    'wo': (4 * 128, 16 * 512),
    'wg': (NFC * 128, 2048),
    'wu': (NFC * 128, 2048),
    'wd': (8 * 128, NFC * 256),
}


def _prep_layer(w_in, w_uq, w_ukv, w_o, w_gate, w_up, w_down):
    o = {}
    o['win_fm'] = _lhsT_tiles(w_in, _fm_chunk_cols())
    tm_cols = list(range(V_OFF, V_OFF + 1024)) + list(range(WI_OFF, WI_OFF + 16))
    o['win_tm'] = _rhs_tiles(w_in, [tm_cols])
    uq = []
    for h in range(8):
        uq.append(list(range(h * 192, h * 192 + 128)))
    for c in range(4):
        a, b = [], []
        for h in (2 * c, 2 * c + 1):
            a += list(range(h * 192 + 128, h * 192 + 192))
            b += list(range(h * 192 + 160, h * 192 + 192)) + list(range(h * 192 + 128, h * 192 + 160))
        uq += [a, b]
    o['wuq'] = _lhsT_tiles(w_uq, uq)
    o['wukv_k'] = _lhsT_tiles(w_ukv, [list(range(h * 256, h * 256 + 128)) for h in range(8)])
    vcols = []
    for h in range(8):
        vcols += list(range(h * 256 + 128, h * 256 + 256))
    o['wukv_v'] = _rhs_tiles(w_ukv, [vcols])
    o['wo'] = _rhs_tiles(w_o, [list(range(b * 512, (b + 1) * 512)) for b in range(4)])
    o['wg'] = _lhsT_tiles(w_gate, [list(range(c * 128, (c + 1) * 128)) for c in range(NFC)])
    o['wu'] = _lhsT_tiles(w_up, [list(range(c * 128, (c + 1) * 128)) for c in range(NFC)])
    o['wd'] = _rhs_tiles(w_down, [list(range(b * 256, (b + 1) * 256)) for b in range(8)])
    for k, v in o.items():
        assert v.shape == WSHAPES[k], (k, v.shape)
    return o


def _consts():
    p = np.arange(128)
    cst = np.zeros((128, 8), np.float32)
    j = (p % 32) % 16
    cst[:, 0] = (np.float32(500000.0) ** (-2.0 * j.astype(np.float32) / 32)).astype(np.float32)
    cst[:, 1] = np.where((p % 32) < 16, -1.0, 1.0)
    j = (p % 16) % 8
    cst[:, 2] = (np.float32(500000.0) ** (-2.0 * j.astype(np.float32) / 16)).astype(np.float32)
    cst[:, 3] = np.where((p % 16) < 8, -1.0, 1.0)
    j = (p % 64) % 32
    cst[:, 4] = (np.float32(10000.0) ** (-2.0 * j.astype(np.float32) / 64)).astype(np.float32)
    cst[:, 5] = np.where((p % 64) < 32, -1.0, 1.0)
    ident = np.eye(128, dtype=np.float32)
    t = np.arange(128)[:, None]
    s = np.arange(128)[None, :]
    tri = np.where(s <= t, 0.0, NEG).astype(np.float32)
    cm = np.zeros((128, 4, 512), np.float32)
    for jj in range(4):
        cm[:, jj, :] = ((128 * jj + np.arange(128)[:, None]) <= np.arange(512)[None, :]).astype(np.float32)
    return {'cst': cst, 'ident': ident, 'tri': tri, 'cmask': cm.reshape(128, 2048)}


def build_program(dbg=()):
    nc = bass.Bass("TRN2", target_bir_lowering=False)
    es = ExitStack()
    tr = TR(nc, es)
    op, dma = tr.op, tr.dma

    def din(name, shape, dt=F32):
        return nc.dram_tensor(name, list(shape), dt, kind="ExternalInput").ap()

    def dscr(name, shape, dt):
        kind = "ExternalOutput" if name in dbg else "Internal"
        return nc.dram_tensor(name, list(shape), dt, kind=kind).ap()

    x_in = din('x', [S, D])
    pos_in = din('positions', [1, S], I32)
    out_d = nc.dram_tensor('out', [S, D], F32, kind="ExternalOutput").ap()
    w_in32 = {k: din(k, [DEPTH * v[0], v[1]]) for k, v in WSHAPES.items()}
    gq_in = din('g_cq_t', [DEPTH * 128, 4])
    gkv_in = din('g_ckv_t', [DEPTH * 128, 2])
    ln_in = {k: din(k, [DEPTH, D]) for k in ('ln1_g', 'ln1_b', 'ln2_g', 'ln2_b')}
    cst_in = din('cst', [128, 8])
    ident_in = din('ident', [128, 128])
    tri_in = din('tri', [128, 128])
    cmask_in = din('cmask', [128, 2048])

    wb = {k: dscr('b_' + k, [DEPTH * v[0], v[1]], BF16) for k, v in WSHAPES.items()}
    tab = dscr('tab', [6 * 128, S], F32)
    xa = dscr('xa', [S, D], F32)
    xb_d = dscr('xb', [S, D], F32)
    qaT = dscr('qaT', [8 * 128, S], BF16)
    kaT = dscr('kaT', [8 * 128, S], BF16)
    va_hp = dscr('va_hp', [8 * 128, 32 * 128], BF16)
    qiT = dscr('qiT', [16 * 64, S], BF16)
    kiT = dscr('kiT', [64, S], BF16)
    wi_d = dscr('wi', [S, 16], F32)
    qnT = dscr('qnT', [8 * 128, S], BF16)
    qpT = dscr('qpT', [8 * 64, S], BF16)
    knT = dscr('knT', [8 * 128, S], BF16)
    kpT = dscr('kpT', [64, S], BF16)
    vb_hp = dscr('vb_hp', [8 * 128, 32 * 128], BF16)
    aoT = dscr('aoT', [D, S], BF16)
    if 'dbg_p2' in dbg:
        dacc = dscr('dacc', [256, S], F32)
        dwork = dscr('dwork', [256, S], F32)
        dm8 = dscr('dm8', [256, 8], F32)
        dthr = dscr('dthr', [256, 1], F32)
        dmask = dscr('dmask', [256, S], BF16)

    def sb(name, shape, dt):
        return es.enter_context(nc.sbuf_tensor(name, shape, dt))

    identb = sb('identb', [128, 128], BF16)
    onesb = sb('onesb', [128, 128], BF16)
    onesf = sb('onesf', [128, 128], F32)
    cst = sb('cstt', [128, 8], F32)
    trineg = sb('trineg', [128, 128], F32)
    cmaskb = sb('cmaskb', [128, 2048], BF16)
    negbig = sb('negbig', [128, 1], F32)
    negm = sb('negm', [128, 1], F32)

    PS = [p[:] for p in tr.ps]
    PSB = [p[:].bitcast(BF16) for p in tr.ps]

    with es:
        tr.reset()
        tmpf = tr.alloc(2048, F32)
        dma('sp', tmpf[:, 0:128], ident_in, writes=['tmpf'])
        op('dve', lambda e: e.tensor_copy(out=identb[:], in_=tmpf[:, 0:128]), reads=['tmpf'], writes=['identb'])
        dma('sp', tmpf, cmask_in, writes=['tmpf'])
        op('dve', lambda e: e.tensor_scalar(out=cmaskb[:], in0=tmpf, scalar1=MBIG, scalar2=-MBIG, op0=ALU.mult, op1=ALU.add),
           reads=['tmpf'], writes=['cmaskb'])
        dma('sp', cst[:], cst_in, writes=['cst'])
        dma('sp', trineg[:], tri_in, writes=['trineg'])
        op('dve', lambda e: e.memset(onesb[:], 1.0), writes=['onesb'])
        op('dve', lambda e: e.memset(onesf[:], 1.0), writes=['onesf'])
        op('dve', lambda e: e.memset(negbig[:], -1.0e29), writes=['negbig'])
        op('dve', lambda e: e.memset(negm[:], -MBIG), writes=['negm'])

        P1_W = ('win_fm', 'win_tm', 'wuq', 'wukv_k', 'wukv_v')
        cast_q = []

        def cast_weights(l, names, defer=False):
            for k in names:
                R, C = WSHAPES[k]
                rows = max(128, (1 << 21) // C // 128 * 128)
                r = 0
                while r < R:
                    n = min(rows, R - r)
                    args = (wb[k][l * R + r:l * R + r + n, :], w_in32[k][l * R + r:l * R + r + n, :])
                    if defer:
                        cast_q.append(args)
                    else:
                        dma('pool', args[0], args[1], writes=['W_%s_%d' % (k, l)])
                    r += n

        def drip_casts(n):
            while n > 0 and cast_q:
                a = cast_q.pop(0)
                dma('pool', a[0], a[1])
                n -= 1

        cast_weights(0, P1_W)

        posi = tr.alloc(S, I32)
        posf = tr.alloc(S, F32)
        ang = tr.alloc(S, F32)
        tmp = tr.alloc(S, F32)
        tmpi = tr.alloc(S, I32)
        res = [tr.alloc(S, F32), tr.alloc(S, F32)]
        dma('sp', posi, pos_in[0:1, :].to_broadcast([128, S]), writes=['posi'])
        op('dve', lambda e: e.tensor_copy(out=posf, in_=posi), reads=['posi'], writes=['posf'])
        PI = float(np.pi)
        for ty in range(3):
            for which in range(2):
                rb = res[(2 * ty + which) % 2]
                rt = 'res%d' % ((2 * ty + which) % 2)
                shift = PI / 2 if which == 0 else 0.0
                op('dve', lambda e: e.tensor_scalar(out=ang, in0=posf, scalar1=cst[:, 2 * ty:2 * ty + 1], scalar2=shift,
                                                    op0=ALU.mult, op1=ALU.add), reads=['posf', 'cst'], writes=['ang'])
                op('dve', lambda e: e.tensor_scalar(out=tmp, in0=ang, scalar1=float(1 / (2 * PI)), scalar2=None, op0=ALU.mult),
                   reads=['ang'], writes=['tmp'])
                op('dve', lambda e: e.tensor_copy(out=tmpi, in_=tmp), reads=['tmp'], writes=['tmpi'])
                op('dve', lambda e: e.tensor_copy(out=tmp, in_=tmpi), reads=['tmpi'], writes=['tmp'])
                op('dve', lambda e: e.scalar_tensor_tensor(out=ang, in0=tmp, scalar=float(-2 * PI), in1=ang, op0=ALU.mult, op1=ALU.add),
                   reads=['tmp', 'ang'], writes=['ang'])
                op('dve', lambda e: e.tensor_scalar(out=tmp, in0=ang, scalar1=PI, scalar2=float(-2 * PI), op0=ALU.is_gt, op1=ALU.mult),
                   reads=['ang'], writes=['tmp'])
                op('dve', lambda e: e.tensor_tensor(out=ang, in0=ang, in1=tmp, op=ALU.add), reads=['ang', 'tmp'], writes=['ang'])
                op('dve', lambda e: e.tensor_scalar(out=tmp, in0=ang, scalar1=-PI, scalar2=float(2 * PI), op0=ALU.is_lt, op1=ALU.mult),
                   reads=['ang'], writes=['tmp'])
                op('dve', lambda e: e.tensor_tensor(out=ang, in0=ang, in1=tmp, op=ALU.add), reads=['ang', 'tmp'], writes=['ang'])
                op('dve', lambda e: e.tensor_scalar(out=ang, in0=ang, scalar1=PI, scalar2=-PI, op0=ALU.min, op1=ALU.max),
                   reads=['ang'], writes=['ang'])
                op('act', lambda e: e.activation(out=rb, in_=ang, func=AF.Sin), reads=['ang'], writes=[rt])
                if which == 1:
                    op('dve', lambda e: e.tensor_scalar(out=rb, in0=rb, scalar1=cst[:, 2 * ty + 1:2 * ty + 2], scalar2=None, op0=ALU.mult),
                       reads=[rt, 'cst'], writes=[rt])
                k = 2 * ty + which
                dma('sp', tab[k * 128:(k + 1) * 128, :], rb, reads=[rt])
        tr.barrier()
        cast_weights(0, [k for k in WSHAPES if k not in P1_W], defer=True)
        cast_weights(1, list(WSHAPES), defer=True)

        def layer(l, x_src, x_dst):
            def W(k, r0, r1):
                R = WSHAPES[k][0]
                return wb[k][l * R + r0:l * R + r1, :]

            tr.reset()
            wtm = tr.alloc(16 * 1040, BF16).rearrange("p (k n) -> p k n", k=16)
            xf = [tr.alloc(D, F32) for _ in range(2)]
            xbf = [tr.alloc(D, BF16) for _ in range(2)]
            xT = [tr.alloc(16 * G, BF16).rearrange("p (k n) -> p k n", k=16) for _ in range(2)]
            wfm = [tr.alloc(16 * 128, BF16).rearrange("p (k n) -> p k n", k=16) for _ in range(3)]
            tabs = {k: tr.alloc(G, F32) for k in ('a_c', 'a_s', 'i_c', 'i_s', 'm_c', 'm_s', 's_c', 's_s')}
            t1 = [tr.alloc(G, F32) for _ in range(2)]
            t2 = [tr.alloc(G, F32) for _ in range(2)]
            ob = [tr.alloc(G, BF16) for _ in range(4)]
            cqf = tr.alloc(4 * G, F32).rearrange("p (k n) -> p k n", k=4)
            ckvf = tr.alloc(2 * G, F32).rearrange("p (k n) -> p k n", k=2)
            sq = [tr.alloc(G, F32) for _ in range(2)]
            rstd = {'q': tr.alloc(G, F32), 'kv': tr.alloc(G, F32)}
            cqn = tr.alloc(4 * G, BF16).rearrange("p (k n) -> p k n", k=4)
            ckvn = tr.alloc(2 * G, BF16).rearrange("p (k n) -> p k n", k=2)
            wuq = tr.alloc(16 * 512, BF16).rearrange("p (c k n) -> p c k n", c=16, k=4)
            wukk = tr.alloc(8 * 256, BF16).rearrange("p (c k n) -> p c k n", c=8, k=2)
            wukv = tr.alloc(2 * 1024, BF16).rearrange("p (k n) -> p k n", k=2)
            gq = tr.alloc(4, F32)
            gkv = tr.alloc(2, F32)
            vtm = [tr.alloc(1024, BF16) for _ in range(2)]
            wio = [tr.alloc(16, F32) for _ in range(2)]

            dma('sp', wtm, W('win_tm', 0, 128).rearrange("p (k n) -> p k n", k=16), reads=['W_win_tm_%d' % l], writes=['wtm'])
            for c in range(16):
                dma('sp', wuq[:, c], W('wuq', c * 128, (c + 1) * 128).rearrange("p (k n) -> p k n", k=4),
                    reads=['W_wuq_%d' % l], writes=['wuq'])
            for c in range(8):
                dma('sp', wukk[:, c], W('wukv_k', c * 128, (c + 1) * 128).rearrange("p (k n) -> p k n", k=2),
                    reads=['W_wukv_k_%d' % l], writes=['wukk'])
            dma('sp', wukv, W('wukv_v', 0, 128).rearrange("p (k n) -> p k n", k=2), reads=['W_wukv_v_%d' % l], writes=['wukv'])
            dma('sp', gq, gq_in[l * 128:(l + 1) * 128, :], writes=['gq'])
            dma('sp', gkv, gkv_in[l * 128:(l + 1) * 128, :], writes=['gkv'])
            op('pool', lambda e: e.memset(tabs['s_c'][64:128, :], 1.0), writes=['s_c'])
            op('pool', lambda e: e.memset(tabs['s_s'][64:128, :], 0.0), writes=['s_s'])

            bankrr = [0]

            def nbank(lo=2, n=5):
                b = lo + bankrr[0] % n
                bankrr[0] += 1
                return b

            wcnt = [0]

            def load_wfm(ci):
                b = wcnt[0] % 3
                wcnt[0] += 1
                dma('sp', wfm[b], W('win_fm', ci * 128, (ci + 1) * 128).rearrange("p (k n) -> p k n", k=16),
                    reads=['W_win_fm_%d' % l], writes=['wfm%d' % b])
                return b

            cnt = {'ob': 0, 't': 0, 'sq': 0, 'v': 0}

            def store_rows(src, srct, dsts, ts):
                for (p0, p1, dr) in dsts:
                    dma('pool', dr[:, ts], src[p0:p1, :], reads=[srct])

            def evac_copy(bank, M, dsts, ts):
                k = cnt['ob'] % 4
                cnt['ob'] += 1
                op('act', lambda e: e.copy(out=ob[k][0:M, :], in_=PS[bank][0:M, :]), reads=['ps%d' % bank], writes=['ob%d' % k])
                store_rows(ob[k], 'ob%d' % k, dsts, ts)

            def evac_rope(bankA, bankB, ty, dsts, ts):
                k = cnt['ob'] % 4
                cnt['ob'] += 1
                j = cnt['t'] % 2
                cnt['t'] += 1
                op('dve', lambda e: e.tensor_tensor(out=t1[j], in0=PS[bankA], in1=tabs[ty + '_c'], op=ALU.mult),
                   reads=['ps%d' % bankA, ty + '_c'], writes=['t1%d' % j])
                op('dve', lambda e: e.tensor_tensor(out=t2[j], in0=PS[bankB], in1=tabs[ty + '_s'], op=ALU.mult),
                   reads=['ps%d' % bankB, ty + '_s'], writes=['t2%d' % j])
                op('pool', lambda e: e.tensor_tensor(out=ob[k], in0=t1[j], in1=t2[j], op=ALU.add),
                   reads=['t1%d' % j, 't2%d' % j], writes=['ob%d' % k])
                store_rows(ob[k], 'ob%d' % k, dsts, ts)

            def mm_group(bank, M, N, lhs_fn, rhs_fn, KC, reads):
                for kc in range(KC):
                    last = kc == KC - 1
                    op('pe', lambda e: e.matmul(PS[bank][0:M, 0:N], lhsT=lhs_fn(kc), rhs=rhs_fn(kc), start=(kc == 0), stop=last),
                       reads=reads, writes=['ps%d' % bank], inc=last)

            def rows(t, r0, n):
                return t[r0:r0 + n, :]

            for g in range(NG):
                ts = slice(g * G, (g + 1) * G)
                xt = xT[g % 2]
                xtt = 'xT%d' % (g % 2)
                for nm, k in (('a_c', 0), ('a_s', 1), ('i_c', 2), ('i_s', 3), ('m_c', 4), ('m_s', 5)):
                    dma('sp', tabs[nm], tab[k * 128:(k + 1) * 128, ts], writes=[nm])
                dma('sp', tabs['s_c'][0:64, :], tab[4 * 128:4 * 128 + 64, ts], writes=['s_c'])
                dma('sp', tabs['s_s'][0:64, :], tab[5 * 128:5 * 128 + 64, ts], writes=['s_s'])
                dma('sp', tabs['s_c'][64:80, :], tab[2 * 128 + 64:2 * 128 + 80, ts], writes=['s_c'])
                dma('sp', tabs['s_s'][64:80, :], tab[3 * 128 + 64:3 * 128 + 80, ts], writes=['s_s'])
                for i in range(4):
                    b = i % 2
                    r0 = g * G + i * 128
                    dma('sp', xf[b], x_src[r0:r0 + 128, :], writes=['xf%d' % b])
                    op('pool', lambda e: e.tensor_copy(out=xbf[b], in_=xf[b]), reads=['xf%d' % b], writes=['xbf%d' % b])
                    for hb in range(2):
                        for c in range(8):
                            kc = hb * 8 + c
                            op('pe', lambda e: e.transpose(out=PSB[hb][:, c * 128:(c + 1) * 128], in_=xbf[b][:, kc * 128:(kc + 1) * 128],
                                                           identity=identb[:]),
                               reads=['xbf%d' % b, 'identb'], writes=['ps%d' % hb], inc=(c == 7))
                        eng = 'act' if hb == 0 else 'dve'
                        op(eng, lambda e: (e.copy if eng == 'act' else e.tensor_copy)(
                            out=xt[:, hb * 8:(hb + 1) * 8, i * 128:(i + 1) * 128],
                            in_=PSB[hb].rearrange("p (c n) -> p c n", c=8)), reads=['ps%d' % hb], writes=[xtt])
                for i in range(4):
                    r0 = g * G + i * 128
                    vb_ = cnt['v'] % 2
                    cnt['v'] += 1
                    for cb in range(2):
                        bank = nbank()
                        mm_group(bank, 128, 512, lambda kc: xt[:, kc, i * 128:(i + 1) * 128], lambda kc: wtm[:, kc, cb * 512:(cb + 1) * 512],
                                 16, [xtt, 'wtm'])
                        op('act', lambda e: e.copy(out=vtm[vb_][:, cb * 512:(cb + 1) * 512], in_=PS[bank]), reads=['ps%d' % bank],
                           writes=['vtm%d' % vb_])
                    blk = (g * G) // 128 + i
                    for h in range(8):
                        dma('pool', va_hp[h * 128:(h + 1) * 128, blk * 128:(blk + 1) * 128], vtm[vb_][:, h * 128:(h + 1) * 128],
                            reads=['vtm%d' % vb_])
                    bank = nbank()
                    mm_group(bank, 128, 16, lambda kc: xt[:, kc, i * 128:(i + 1) * 128], lambda kc: wtm[:, kc, 1024:1040], 16, [xtt, 'wtm'])
                    op('act', lambda e: e.activation(out=wio[vb_], in_=PS[bank][:, 0:16], func=AF.Copy, scale=1.0 / 32.0),
                       reads=['ps%d' % bank], writes=['wio%d' % vb_])
                    dma('pool', wi_d[r0:r0 + 128, :], wio[vb_], reads=['wio%d' % vb_])
                ci = 0
                nxt = load_wfm(0)

                def fm(M=128):
                    nonlocal ci, nxt
                    b = nxt
                    if ci + 1 < NFM:
                        nxt = load_wfm(ci + 1)
                    bank = nbank()
                    mm_group(bank, M, G, lambda kc: wfm[b][:, kc, 0:M], lambda kc: xt[:, kc, :], 16, ['wfm%d' % b, xtt])
                    ci += 1
                    return bank

                for (dst, nm) in ((qaT, 'a'), (kaT, 'a')):
                    for c in range(2):
                        ba = fm()
                        bb = fm()
                        evac_rope(ba, bb, 'a', [(32 * k, 32 * k + 32, rows(dst, (4 * c + k) * 128, 32)) for k in range(4)], ts)
                    for h in range(8):
                        ba = fm(96)
                        evac_copy(ba, 96, [(0, 96, rows(dst, h * 128 + 32, 96))], ts)
                for c in range(2):
                    ba = fm()
                    bb = fm()
                    evac_rope(ba, bb, 'i', [(16 * k, 16 * k + 16, rows(qiT, (8 * c + k) * 64, 16)) for k in range(8)], ts)
                for j in range(8):
                    ba = fm(96)
                    evac_copy(ba, 96, [(0, 48, rows(qiT, (2 * j) * 64 + 16, 48)), (48, 96, rows(qiT, (2 * j + 1) * 64 + 16, 48))], ts)
                for (nchunk, dstf, nm) in ((4, cqf, 'q'), (2, ckvf, 'kv')):
                    sbank = 7
                    for c in range(nchunk):
                        ba = fm()
                        op('act', lambda e: e.copy(out=dstf[:, c, :], in_=PS[ba]), reads=['ps%d' % ba], writes=['cf' + nm])
                        k = cnt['sq'] % 2
                        cnt['sq'] += 1
                        op('act', lambda e: e.activation(out=sq[k], in_=PS[ba], func=AF.Square), reads=['ps%d' % ba], writes=['sq%d' % k])
                        op('pe', lambda e: e.matmul(PS[sbank], lhsT=onesf[:], rhs=sq[k], start=(c == 0), stop=(c == nchunk - 1)),
                           reads=['sq%d' % k, 'onesf'], writes=['ps%d' % sbank])
                    rs = rstd[nm]
                    op('dve', lambda e: e.tensor_scalar(out=rs, in0=PS[sbank], scalar1=1.0 / (128 * nchunk), scalar2=RMS_EPS,
                                                        op0=ALU.mult, op1=ALU.add), reads=['ps%d' % sbank], writes=['rs' + nm])
                    op('act', lambda e: e.activation(out=rs, in_=rs, func=AF.Sqrt), reads=['rs' + nm], writes=['rs' + nm])
                    op('dve', lambda e: e.reciprocal(out=rs, in_=rs), reads=['rs' + nm], writes=['rs' + nm])
                    gg = gq if nm == 'q' else gkv
                    dn = cqn if nm == 'q' else ckvn
                    for c in range(nchunk):
                        op('dve', lambda e: e.scalar_tensor_tensor(out=dn[:, c, :], in0=dstf[:, c, :], scalar=gg[:, c:c + 1], in1=rs,
                                                                   op0=ALU.mult, op1=ALU.mult),
                           reads=['cf' + nm, 'rs' + nm, 'gq', 'gkv'], writes=['n' + nm])
                ba = fm()
                bb = fm()
                evac_rope(ba, bb, 's', [(0, 64, kpT), (64, 128, kiT)], ts)
                assert ci == NFM
                for h in range(8):
                    bank = nbank()
                    mm_group(bank, 128, G, lambda kc: wuq[:, h, kc, :], lambda kc: cqn[:, kc, :], 4, ['wuq', 'nq'])
                    evac_copy(bank, 128, [(0, 128, rows(qnT, h * 128, 128))], ts)
                for c in range(4):
                    ba = nbank()
                    mm_group(ba, 128, G, lambda kc: wuq[:, 8 + 2 * c, kc, :], lambda kc: cqn[:, kc, :], 4, ['wuq', 'nq'])
                    bb = nbank()
                    mm_group(bb, 128, G, lambda kc: wuq[:, 9 + 2 * c, kc, :], lambda kc: cqn[:, kc, :], 4, ['wuq', 'nq'])
                    evac_rope(ba, bb, 'm', [(0, 64, rows(qpT, (2 * c) * 64, 64)), (64, 128, rows(qpT, (2 * c + 1) * 64, 64))], ts)
                for h in range(8):
                    bank = nbank()
                    mm_group(bank, 128, G, lambda kc: wukk[:, h, kc, :], lambda kc: ckvn[:, kc, :], 2, ['wukk', 'nkv'])
                    evac_copy(bank, 128, [(0, 128, rows(knT, h * 128, 128))], ts)
                for i in range(4):
                    vb_ = cnt['v'] % 2
                    cnt['v'] += 1
                    for cb in range(2):
                        bank = nbank()
                        mm_group(bank, 128, 512, lambda kc: ckvn[:, kc, i * 128:(i + 1) * 128], lambda kc: wukv[:, kc, cb * 512:(cb + 1) * 512],
                                 2, ['nkv', 'wukv'])
                        op('act', lambda e: e.copy(out=vtm[vb_][:, cb * 512:(cb + 1) * 512], in_=PS[bank]), reads=['ps%d' % bank],
                           writes=['vtm%d' % vb_])
                    blk = (g * G) // 128 + i
                    for h in range(8):
                        dma('pool', vb_hp[h * 128:(h + 1) * 128, blk * 128:(blk + 1) * 128], vtm[vb_][:, h * 128:(h + 1) * 128],
                            reads=['vtm%d' % vb_])
            tr.barrier()
            if 'stop_p1' in dbg:
                return

            tr.reset()
            ki2 = tr.alloc(S, BF16)
            qit = [tr.alloc(8 * 128, BF16).rearrange("p (j n) -> p j n", j=8) for _ in range(2)]
            wit = [tr.alloc(16, F32) for _ in range(2)]
            acc = [tr.alloc(S, F32) for _ in range(2)]
            work = tr.alloc(S, F32)
            rl = [tr.alloc(512, BF16) for _ in range(4)]
            diag = [tr.alloc(16 * 128, BF16).rearrange("p (h n) -> p h n", h=16) for _ in range(2)]
            m8 = [tr.alloc(8, F32) for _ in range(2)]
            thr = [tr.alloc(1, F32) for _ in range(2)]
            maskb = [tr.alloc(S, BF16) for _ in range(4)]
            maskT = tr.alloc(32 * 512, BF16).rearrange("p (b n) -> p b n", b=32)
            kT = [tr.alloc(S, BF16) for _ in range(2)]
            vv = [tr.alloc(32 * 128, BF16).rearrange("p (b n) -> p b n", b=32) for _ in range(2)]
            qq = [tr.alloc(512, BF16) for _ in range(2)]
            kpe = tr.alloc(S, BF16)
            qpe = [tr.alloc(512, BF16) for _ in range(2)]
            Pb = [tr.alloc(512, BF16) for _ in range(3)]
            rc = [tr.alloc(512, F32) for _ in range(2)]
            oo = [tr.alloc(512, BF16) for _ in range(2)]
            pos_ = [tr.alloc(512, F32) for _ in range(2)]

            dma('sp', ki2[0:64, :], kiT, writes=['ki2'])
            dma('sp', ki2[64:128, :], kiT, writes=['ki2'])
            dma('sp', kpe[0:64, :], kpT, writes=['kpe'])
            c2 = {'q': 0, 'rl': 0, 'ib': 0, 'tb': 0, 'kv': 0, 'P': 0, 'qk': 0, 'o': 0, 'm8': 0}
            qi_v = qiT.rearrange("(j two d) s -> (two d) j s", two=2, d=64)
            cm4 = cmaskb[:].rearrange("p (j n) -> p j n", j=4)

            def score_prep(qt):
                b = qt % 2
                dma('sp', qit[b], qi_v[:, :, qt * 128:(qt + 1) * 128], writes=['qit%d' % b])
                dma('sp', wit[b], wi_d[qt * 128:(qt + 1) * 128, :], writes=['wit%d' % b])
                for h in range(16):
                    op('pool', lambda e: e.tensor_scalar(out=diag[b][:, h, :], in0=identb[:], scalar1=wit[b][:, h:h + 1], scalar2=None,
                                                         op0=ALU.mult),
                       reads=['identb', 'wit%d' % b], writes=['diag%d' % b])

            def score_tile(qt, q4):
                L = (qt + 1) * 128
                b = qt % 2
                a = acc[b]
                at = 'acc%d' % b
                dg = diag[b]
                nsb = (L + 511) // 512
                for sbk in range(nsb):
                    n = min(512, L - sbk * 512)
                    ks = []

                    def a_mm(h):
                        bank = (0, 1, 3)[c2['ib'] % 3]
                        c2['ib'] += 1
                        p0 = (h % 2) * 64
                        op('pe', lambda e: e.matmul(PS[bank][:, 0:n], lhsT=qit[b][p0:p0 + 64, h // 2, :], rhs=ki2[p0:p0 + 64, sbk * 512:sbk * 512 + n],
                                                    start=True, stop=True), reads=['qit%d' % b, 'ki2'], writes=['ps%d' % bank])
                        k = c2['rl'] % 4
                        c2['rl'] += 1
                        op('act', lambda e: e.activation(out=rl[k][:, 0:n], in_=PS[bank][:, 0:n], func=AF.Relu), reads=['ps%d' % bank],
                           writes=['rl%d' % k])
                        ks.append(k)

                    def d_mm(h):
                        k = ks[h]
                        op('pe', lambda e: e.matmul(PS[2][:, 0:n], lhsT=dg[:, h, :], rhs=rl[k][:, 0:n], start=(h == 0), stop=(h == 15)),
                           reads=['diag%d' % b, 'rl%d' % k], writes=['ps2'])

                    a_mm(0)
                    a_mm(1)
                    for h in range(16):
                        if h + 2 < 16:
                            a_mm(h + 2)
                        d_mm(h)
                    op('act', lambda e: e.copy(out=a[:, sbk * 512:sbk * 512 + n], in_=PS[2][:, 0:n]), reads=['ps2'], writes=[at])
                op('dve', lambda e: e.tensor_tensor(out=a[:, qt * 128:L], in0=a[:, qt * 128:L], in1=trineg[:], op=ALU.add),
                   reads=[at, 'trineg'], writes=[at])
                mk = maskb[q4]
                mt = 'maskb%d' % q4
                if L > 256:
                    for r in range(32):
                        mi = c2['m8'] % 2
                        c2['m8'] += 1
                        src = a[:, 0:L] if r == 0 else work[:, 0:L]
                        op('dve', lambda e: e.max(out=m8[mi], in_=src), reads=[at, 'work'], writes=['m8%d' % mi])
                        if r < 31:
                            op('dve', lambda e: e.match_replace(out=work[:, 0:L], in_to_replace=m8[mi], in_values=src, imm_value=NEG),
                               reads=[at, 'work', 'm8%d' % mi], writes=['work'])
                    tb_ = thr[b]
                    op('dve', lambda e: e.tensor_scalar(out=tb_, in0=m8[mi][:, 7:8], scalar1=-1.0e29, scalar2=None, op0=ALU.max),
                       reads=['m8%d' % mi], writes=['thr%d' % b])
                    op('dve', lambda e: e.tensor_scalar(out=mk[:, 0:L], in0=a[:, 0:L], scalar1=tb_, scalar2=None, op0=ALU.is_ge),
                       reads=[at, 'thr%d' % b], writes=[mt])
                else:
                    op('dve', lambda e: e.tensor_scalar(out=mk[:, 0:L], in0=a[:, 0:L], scalar1=negbig[:], scalar2=None, op0=ALU.is_ge),
                       reads=[at, 'negbig'], writes=[mt])
                if 'dbg_p2' in dbg and l == 0 and qt in (2, 9):
                    di = 0 if qt == 2 else 1
                    dma('sp', dacc[di * 128:(di + 1) * 128, 0:L], a[:, 0:L], reads=[at])
                    dma('sp', dwork[di * 128:(di + 1) * 128, 0:L], work[:, 0:L], reads=['work'])
                    dma('sp', dm8[di * 128:(di + 1) * 128, :], m8[mi], reads=['m8%d' % mi])
                    dma('sp', dthr[di * 128:(di + 1) * 128, :], thr[b], reads=['thr%d' % b])
                    dma('sp', dmask[di * 128:(di + 1) * 128, 0:L], mk[:, 0:L], reads=[mt])
                Lmax = (qt // 4 + 1) * 512
                if L < Lmax:
                    op('dve', lambda e: e.memset(mk[:, L:Lmax], 0.0), writes=[mt])

            def transposes_tile(j):
                tbk, q4 = j // 4, j % 4
                nblk = 4 * tbk + 4
                blk0 = 0
                while blk0 < nblk:
                    nb = min(8, nblk - blk0)
                    for i in range(nb):
                        blk = blk0 + i
                        op('pe', lambda e: e.transpose(out=PSB[3][:, i * 128:(i + 1) * 128], in_=maskb[q4][:, blk * 128:(blk + 1) * 128],
                                                       identity=identb[:]),
                           reads=['maskb%d' % q4, 'identb'], writes=['ps3'], inc=(i == nb - 1))
                    op('act', lambda e: e.activation(out=maskT[:, blk0:blk0 + nb, q4 * 128:(q4 + 1) * 128],
                                                     in_=PSB[3][:, 0:nb * 128].rearrange("p (b n) -> p b n", b=nb),
                                                     func=AF.Identity, scale=MBIG, bias=negm[:]),
                       reads=['ps3', 'negm'], writes=['maskT'])
                    blk0 += nb

            def attn_head(tbk, h, mla):
                nblk = 4 * tbk + 4
                Lmax = nblk * 128
                tsl = slice(tbk * 512, (tbk + 1) * 512)
                b = c2['kv'] % 2
                c2['kv'] += 1
                ksrc, vsrc, qsrc = (knT, vb_hp, qnT) if mla else (kaT, va_hp, qaT)
                dma('sp', kT[b][:, 0:Lmax], ksrc[h * 128:(h + 1) * 128, 0:Lmax], writes=['kT%d' % b])
                dma('sp', vv[b][:, 0:nblk, :], vsrc[h * 128:(h + 1) * 128, 0:Lmax].rearrange("p (b n) -> p b n", n=128), writes=['vv%d' % b])
                dma('sp', qq[b], qsrc[h * 128:(h + 1) * 128, tsl], writes=['qq%d' % b])
                if mla:
                    dma('sp', qpe[b][0:64, :], qpT[h * 64:(h + 1) * 64, tsl], writes=['qpe%d' % b])
                scale = float((192.0 if mla else 128.0) ** -0.5)
                qkb = {}

                def qk(sc):
                    bank = 4 + c2['qk'] % 2
                    c2['qk'] += 1
                    qkb[sc] = bank
                    diag_blk = mla and sc >= 4 * tbk
                    op('pe', lambda e: e.matmul(PS[bank], lhsT=kT[b][:, sc * 128:(sc + 1) * 128], rhs=qq[b], start=True, stop=False),
                       reads=['kT%d' % b, 'qq%d' % b], writes=['ps%d' % bank], inc=False)
                    if mla:
                        op('pe', lambda e: e.matmul(PS[bank], lhsT=kpe[0:64, sc * 128:(sc + 1) * 128], rhs=qpe[b][0:64, :], start=False,
                                                    stop=not diag_blk),
                           reads=['kpe', 'qpe%d' % b], writes=['ps%d' % bank], inc=not diag_blk)
                        if diag_blk:
                            op('pe', lambda e: e.matmul(PS[bank], lhsT=identb[:], rhs=cm4[:, sc - 4 * tbk, :], start=False, stop=True),
                               reads=['identb', 'cmaskb'], writes=['ps%d' % bank])
                    else:
                        op('pe', lambda e: e.matmul(PS[bank], lhsT=identb[:], rhs=maskT[:, sc, :], start=False, stop=True),
                           reads=['identb', 'maskT'], writes=['ps%d' % bank])

                qk(0)
                for sc in range(nblk):
                    if sc + 1 < nblk:
                        qk(sc + 1)
                    bank = qkb[sc]
                    k = c2['P'] % 3
                    c2['P'] += 1
                    op('act', lambda e: e.activation(out=Pb[k], in_=PS[bank], func=AF.Exp, scale=scale), reads=['ps%d' % bank], writes=['P%d' % k])
                    op('pe', lambda e: e.matmul(PS[6], lhsT=vv[b][:, sc, :], rhs=Pb[k], start=(sc == 0), stop=(sc == nblk - 1)),
                       reads=['vv%d' % b, 'P%d' % k], writes=['ps6'], inc=False)
                    op('pe', lambda e: e.matmul(PS[7], lhsT=onesb[:], rhs=Pb[k], start=(sc == 0), stop=(sc == nblk - 1)),
                       reads=['onesb', 'P%d' % k], writes=['ps7'])
                j = c2['o'] % 2
                c2['o'] += 1
                op('act', lambda e: e.activation(out=rc[j], in_=PS[7], func=AF.Ln), reads=['ps7'], writes=['rc%d' % j])
                op('act', lambda e: e.copy(out=pos_[j], in_=PS[6]), reads=['ps6'], writes=['pos%d' % j])
                op('act', lambda e: e.activation(out=rc[j], in_=rc[j], func=AF.Exp, scale=-1.0), reads=['rc%d' % j], writes=['rc%d' % j])
                op('pool', lambda e: e.tensor_tensor(out=oo[j], in0=pos_[j], in1=rc[j], op=ALU.mult),
                   reads=['pos%d' % j, 'rc%d' % j], writes=['oo%d' % j])
                r0 = (8 + h if mla else h) * 128
                dma('pool', aoT[r0:r0 + 128, tsl], oo[j], reads=['oo%d' % j])

            NTB = 3 if 'dbg_p2' in dbg else 8
            NT = 4 * NTB
            st = {'scored': 0, 'T': 0}
            dsa_done = [0] * NTB
            mla_done = [0] * NTB

            def ensure_dsa(tb_, n):
                ensure_T(4 * tb_ + 4)
                while dsa_done[tb_] < n:
                    attn_head(tb_, dsa_done[tb_], False)
                    dsa_done[tb_] += 1

            def ensure_T(upto):
                while st['T'] < upto:
                    j = st['T']
                    if j % 4 == 0 and j >= 4:
                        ensure_dsa(j // 4 - 1, 8)
                    assert st['scored'] > j
                    transposes_tile(j)
                    st['T'] += 1

            k = 0
            score_prep(0)
            while True:
                if k < NT:
                    if k >= 4:
                        ensure_T(k - 3)
                    if k + 1 < NT:
                        score_prep(k + 1)
                    score_tile(k, k % 4)
                    st['scored'] += 1
                njobs = 0
                while njobs < 4:
                    cand = [t for t in range(NTB) if dsa_done[t] < 8 and st['T'] >= 4 * t + 4]
                    if cand:
                        t = cand[0]
                        attn_head(t, dsa_done[t], False)
                        dsa_done[t] += 1
                        njobs += 1
                        continue
                    cand = [t for t in range(NTB) if mla_done[t] < 8 and t <= k // 4 + 2]
                    if cand:
                        t = cand[0]
                        attn_head(t, mla_done[t], True)
                        mla_done[t] += 1
                        njobs += 1
                        continue
                    break
                ensure_T(max(0, min(k - 1, NT, st['scored'])))
                drip_casts(2)
                k += 1
                if k >= NT and st['T'] >= NT and all(d == 8 for d in dsa_done) and all(d == 8 for d in mla_done):
                    break
                assert k < NT + 64
            drip_casts(1000)
            tr.barrier()
            if 'stop_p2' in dbg:
                return

            def layer_norm(xt_, xtt_, gt, bt, stt, mv, sd):
                xv = xt_.rearrange("p (c n) -> p c n", c=4)
                for c in range(4):
                    op('dve', lambda e: e.bn_stats(out=stt[:, c, :], in_=xv[:, c, :]), reads=[xtt_], writes=['stt'])
                op('dve', lambda e: e.bn_aggr(out=mv, in_=stt), reads=['stt'], writes=['mv'])
                op('dve', lambda e: e.tensor_scalar(out=sd, in0=mv[:, 1:2], scalar1=LN_EPS, scalar2=None, op0=ALU.add), reads=['mv'], writes=['sd'])
                op('act', lambda e: e.activation(out=sd, in_=sd, func=AF.Sqrt), reads=['sd'], writes=['sd'])
                op('dve', lambda e: e.reciprocal(out=sd, in_=sd), reads=['sd'], writes=['sd'])
                op('dve', lambda e: e.tensor_scalar(out=xt_, in0=xt_, scalar1=mv[:, 0:1], scalar2=sd, op0=ALU.subtract, op1=ALU.mult),
                   reads=[xtt_, 'mv', 'sd'], writes=[xtt_])
                op('pool', lambda e: e.tensor_tensor(out=xt_, in0=xt_, in1=gt, op=ALU.mult), reads=[xtt_, 'lng'], writes=[xtt_])
                op('pool', lambda e: e.tensor_tensor(out=xt_, in0=xt_, in1=bt, op=ALU.add), reads=[xtt_, 'lnb'], writes=[xtt_])

            tr.reset()
            aT = [tr.alloc(16 * G, BF16).rearrange("p (k n) -> p k n", k=16) for _ in range(2)]
            wo = [tr.alloc(16 * 512, BF16).rearrange("p (k n) -> p k n", k=16) for _ in range(2)]
            xs = [tr.alloc(D, F32) for _ in range(8)]
            lng = tr.alloc(D, F32)
            lnb = tr.alloc(D, F32)
            stt = tr.alloc(4 * 6, F32).rearrange("p (c n) -> p c n", c=4)
            mv = tr.alloc(2, F32)
            sd = tr.alloc(1, F32)
            dma('sp', lng, ln_in['ln1_g'][l:l + 1, :].to_broadcast([128, D]), writes=['lng'])
            dma('sp', lnb, ln_in['ln1_b'][l:l + 1, :].to_broadcast([128, D]), writes=['lnb'])
            aoT_v = aoT.rearrange("(k p) s -> p k s", p=128)
            wc = 0
            bk = 0
            for g in range(NG):
                ab = g % 2
                dma('sp', aT[ab], aoT_v[:, :, g * G:(g + 1) * G], writes=['aT%d' % ab])
                for i in range(4):
                    xi = (g % 2) * 4 + i
                    r0 = g * G + i * 128
                    dma('sp', xs[xi], x_src[r0:r0 + 128, :], writes=['xs%d' % xi])
                for cb in range(4):
                    wbf = wc % 2
                    wc += 1
                    dma('sp', wo[wbf], W('wo', cb * 128, (cb + 1) * 128).rearrange("p (k n) -> p k n", k=16), writes=['wo%d' % wbf])
                    for i in range(4):
                        xi = (g % 2) * 4 + i
                        bank = bk % 6
                        bk += 1
                        mm_group(bank, 128, 512, lambda kc: aT[ab][:, kc, i * 128:(i + 1) * 128], lambda kc: wo[wbf][:, kc, :], 16,
                                 ['aT%d' % ab, 'wo%d' % wbf])
                        xsl = xs[xi][:, cb * 512:(cb + 1) * 512]
                        op('dve', lambda e: e.scalar_tensor_tensor(out=xsl, in0=xsl, scalar=ALPHA, in1=PS[bank], op0=ALU.mult, op1=ALU.add),
                           reads=['ps%d' % bank, 'xs%d' % xi], writes=['xs%d' % xi])
                for i in range(4):
                    xi = (g % 2) * 4 + i
                    r0 = g * G + i * 128
                    layer_norm(xs[xi], 'xs%d' % xi, lng, lnb, stt, mv, sd)
                    dma('pool', xa[r0:r0 + 128, :], xs[xi], reads=['xs%d' % xi])
            tr.barrier()
            if 'stop_p3' in dbg:
                return

            tr.reset()
            x1 = [tr.alloc(D, F32) for _ in range(4)]
            xbf = [tr.alloc(D, BF16) for _ in range(2)]
            x1T = tr.alloc(16 * G, BF16).rearrange("p (k n) -> p k n", k=16)
            hT = tr.alloc(NFC * G, BF16).rearrange("p (k n) -> p k n", k=NFC)
            wgb = [tr.alloc(16 * 128, BF16).rearrange("p (k n) -> p k n", k=16) for _ in range(2)]
            wub = [tr.alloc(16 * 128, BF16).rearrange("p (k n) -> p k n", k=16) for _ in range(2)]
            wdb = [tr.alloc(NFC * 256, BF16).rearrange("p (k n) -> p k n", k=NFC) for _ in range(2)]
            sg = [tr.alloc(G, F32) for _ in range(2)]
            lng = tr.alloc(D, F32)
            lnb = tr.alloc(D, F32)
            stt = tr.alloc(4 * 6, F32).rearrange("p (c n) -> p c n", c=4)
            mv = tr.alloc(2, F32)
            sd = tr.alloc(1, F32)
            dma('sp', lng, ln_in['ln2_g'][l:l + 1, :].to_broadcast([128, D]), writes=['lng'])
            dma('sp', lnb, ln_in['ln2_b'][l:l + 1, :].to_broadcast([128, D]), writes=['lnb'])
            wc = 0
            wdc = 0
            bk = 0
            for g in range(NG):
                for i in range(4):
                    r0 = g * G + i * 128
                    b = i % 2
                    dma('sp', x1[i], xa[r0:r0 + 128, :], writes=['x1%d' % i])
                    op('pool', lambda e: e.tensor_copy(out=xbf[b], in_=x1[i]), reads=['x1%d' % i], writes=['xbf%d' % b])
                    for hb in range(2):
                        for c in range(8):
                            kc = hb * 8 + c
                            op('pe', lambda e: e.transpose(out=PSB[6 + hb][:, c * 128:(c + 1) * 128], in_=xbf[b][:, kc * 128:(kc + 1) * 128],
                                                           identity=identb[:]),
                               reads=['xbf%d' % b, 'identb'], writes=['ps%d' % (6 + hb)], inc=(c == 7))
                        eng = 'act' if hb == 0 else 'dve'
                        op(eng, lambda e: (e.copy if eng == 'act' else e.tensor_copy)(
                            out=x1T[:, hb * 8:(hb + 1) * 8, i * 128:(i + 1) * 128],
                            in_=PSB[6 + hb].rearrange("p (c n) -> p c n", c=8)), reads=['ps%d' % (6 + hb)], writes=['x1T'])
                for fc in range(NFC):
                    wbf = wc % 2
                    wc += 1
                    dma('sp', wgb[wbf], W('wg', fc * 128, (fc + 1) * 128).rearrange("p (k n) -> p k n", k=16), writes=['wg%d' % wbf])
                    dma('sp', wub[wbf], W('wu', fc * 128, (fc + 1) * 128).rearrange("p (k n) -> p k n", k=16), writes=['wu%d' % wbf])
                    bg = (bk % 3) * 2
                    bu = bg + 1
                    bk += 1
                    mm_group(bg, 128, G, lambda kc: wgb[wbf][:, kc, :], lambda kc: x1T[:, kc, :], 16, ['wg%d' % wbf, 'x1T'])
                    mm_group(bu, 128, G, lambda kc: wub[wbf][:, kc, :], lambda kc: x1T[:, kc, :], 16, ['wu%d' % wbf, 'x1T'])
                    k = fc % 2
                    op('act', lambda e: e.activation(out=sg[k], in_=PS[bg], func=AF.Silu), reads=['ps%d' % bg], writes=['sg%d' % k])
                    op('dve', lambda e: e.tensor_tensor(out=hT[:, fc, :], in0=sg[k], in1=PS[bu], op=ALU.mult),
                       reads=['sg%d' % k, 'ps%d' % bu], writes=['hT'])
                for cbh in range(8):
                    wbf = wdc % 2
                    wdc += 1
                    dma('sp', wdb[wbf], W('wd', cbh * 128, (cbh + 1) * 128).rearrange("p (k n) -> p k n", k=NFC), writes=['wd%d' % wbf])
                    for i in range(4):
                        bank = bk % 6
                        bk += 1
                        mm_group(bank, 128, 256, lambda kc: hT[:, kc, i * 128:(i + 1) * 128], lambda kc: wdb[wbf][:, kc, :], NFC,
                                 ['hT', 'wd%d' % wbf])
                        xsl = x1[i][:, cbh * 256:(cbh + 1) * 256]
                        op('dve', lambda e: e.scalar_tensor_tensor(out=xsl, in0=xsl, scalar=ALPHA, in1=PS[bank][:, 0:256], op0=ALU.mult, op1=ALU.add),
                           reads=['ps%d' % bank, 'x1%d' % i], writes=['x1%d' % i])
                for i in range(4):
                    r0 = g * G + i * 128
                    layer_norm(x1[i], 'x1%d' % i, lng, lnb, stt, mv, sd)
                    dma('pool', x_dst[r0:r0 + 128, :], x1[i], reads=['x1%d' % i])
            tr.barrier()

        layer(0, x_in, xb_d)
        if not any(k.startswith('stop') for k in dbg):
            layer(1, xb_d, out_d)
        tr.barrier()
    return nc


_CACHE = {}


def _host_inputs(inputs):
    f = lambda a: np.asarray(a, dtype=np.float32)
    layers = [_prep_layer(f(inputs['w_in'][l]), f(inputs['w_uq'][l]), f(inputs['w_ukv'][l]), f(inputs['w_o'][l]),
                          f(inputs['w_gate'][l]), f(inputs['w_up'][l]), f(inputs['w_down'][l])) for l in range(DEPTH)]
    shared = {k: np.ascontiguousarray(np.concatenate([layers[l][k] for l in range(DEPTH)], axis=0)) for k in WSHAPES}
    shared.update(_consts())
    shared['g_cq_t'] = np.ascontiguousarray(f(inputs['g_cq']).reshape(DEPTH, 4, 128).transpose(0, 2, 1)).reshape(DEPTH * 128, 4)
    shared['g_ckv_t'] = np.ascontiguousarray(f(inputs['g_ckv']).reshape(DEPTH, 2, 128).transpose(0, 2, 1)).reshape(DEPTH * 128, 2)
    for k in ('ln1_g', 'ln1_b', 'ln2_g', 'ln2_b'):
        shared[k] = f(inputs[k])
    return shared


def kernel(**inputs):
    x = np.asarray(inputs['x'], dtype=np.float32)
    pos = np.asarray(inputs['positions']).astype(np.int32)
    shared = _host_inputs(inputs)
    if 'nc' not in _CACHE:
        _CACHE['nc'] = build_program()
    nc = _CACHE['nc']
    in_maps = []
    for b in range(4):
        m = dict(shared)
        m['x'] = np.ascontiguousarray(x[b])
        m['positions'] = np.ascontiguousarray(pos[b:b + 1])
        in_maps.append(m)
    res = run_bass_kernel_spmd(nc, in_maps, core_ids=list(range(4)))
    out = np.stack([np.asarray(res.results[b]['out'], dtype=np.float32) for b in range(4)], axis=0)
    return out
```

```python
import numpy as np
from contextlib import ExitStack
import concourse.bass as bass
import concourse.mybir as mybir
from concourse.bass_utils import run_bass_kernel_spmd

F32 = mybir.dt.float32
BF16 = mybir.dt.bfloat16
I32 = mybir.dt.int32
ALU = mybir.AluOpType
AF = mybir.ActivationFunctionType

S = 4096
D = 2048
DEPTH = 2
FF = 5632
NFC = FF // 128
G = 512
NG = S // G
ALPHA = float((2 * DEPTH) ** 0.25)
LN_EPS = 1e-5
RMS_EPS = 1e-6
NEG = -1.0e30
MBIG = 30000.0
NFM = 44
ARENA = 50 * 1024


def _dsize(dt):
    return 4 if dt in (F32, I32) else 2


class TR:
    NQ = 8

    def __init__(self, nc, es):
        self.nc = nc
        self.E = {'pe': nc.tensor, 'act': nc.scalar, 'dve': nc.vector, 'pool': nc.gpsimd, 'sp': nc.sync}
        self.sem = {k: es.enter_context(nc.semaphore('s_' + k)) for k in ('pe', 'act', 'dve', 'pool')}
        self.cnt = {k: 0 for k in self.sem}
        self.dq = {}
        for q in ('sp', 'pool', 'act'):
            self.dq[q] = {'sems': [es.enter_context(nc.semaphore('d_%s%d' % (q, i))) for i in range(self.NQ)], 'j': 0}
        self.seen = {e: {} for e in self.E}
        self.lastw = {}
        self.readers = {}
        self.arena = es.enter_context(nc.sbuf_tensor('arena', [128, ARENA], F32))
        self.off = 0
        self.ps = [es.enter_context(nc.psum_tensor('ps%d' % i, [128, 512], F32)) for i in range(8)]

    def reset(self):
        self.off = 0

    def alloc(self, n, dt):
        nb = n * _dsize(dt)
        n4 = (nb + 31) // 32 * 8
        ap = self.arena[:, self.off:self.off + n4]
        self.off += n4
        assert self.off <= ARENA, ('arena overflow', self.off)
        if dt != F32:
            ap = ap.bitcast(dt)
        return ap[:, 0:n]

    def _handle(self, key):
        if key[0] == 'c':
            return self.sem[key[1]]
        return self.dq[key[1]]['sems'][key[2]]

    def _wait(self, e, ev):
        if ev is None:
            return
        key, val = ev
        if key == ('c', e) and e == 'pe':
            return
        if self.seen[e].get(key, 0) >= val:
            return
        self.E[e].wait_ge(self._handle(key), val)
        self.seen[e][key] = val

    def _deps(self, reads, writes):
        out = []
        for t in reads:
            ev = self.lastw.get(t)
            if ev is not None:
                out.append(ev)
        for t in writes:
            ev = self.lastw.get(t)
            if ev is not None:
                out.append(ev)
            for k, v in self.readers.get(t, {}).items():
                out.append((k, v))
        return out

    def _record(self, ev, reads, writes):
        for t in reads:
            r = self.readers.setdefault(t, {})
            if r.get(ev[0], 0) < ev[1]:
                r[ev[0]] = ev[1]
        for t in writes:
            self.lastw[t] = ev
            self.readers[t] = {}

    def op(self, e, fn, reads=(), writes=(), inc=True):
        for ev in self._deps(reads, writes):
            self._wait(e, ev)
        ins = fn(self.E[e])
        if inc:
            self.cnt[e] += 1
            ins.then_inc(self.sem[e], 1)
            ev = (('c', e), self.cnt[e])
        else:
            ev = (('c', e), self.cnt[e] + 1)
        self._record(ev, reads, writes)

    def dma(self, q, out, in_, reads=(), writes=()):
        dq = self.dq[q]
        j = dq['j']
        i = j % self.NQ
        if j >= self.NQ:
            self._wait(q, (('d', q, i), 16 * (j // self.NQ)))
        for ev in self._deps(reads, writes):
            self._wait(q, ev)
        self.E[q].dma_start(out=out, in_=in_).then_inc(dq['sems'][i], 16)
        dq['j'] += 1
        ev = (('d', q, i), 16 * (j // self.NQ + 1))
        self._record(ev, reads, writes)

    def barrier(self):
        evs = [(('c', k), self.cnt[k]) for k in self.cnt if self.cnt[k] > 0]
        for q, dq in self.dq.items():
            for i in range(self.NQ):
                n = (dq['j'] - i + self.NQ - 1) // self.NQ
                if n > 0:
                    evs.append((('d', q, i), 16 * n))
        for e in self.E:
            for ev in evs:
                self._wait(e, ev)
        self.lastw = {}
        self.readers = {}


A_OFF, K_OFF, V_OFF, QI_OFF, WI_OFF, KI_OFF, CQ_OFF, CKV_OFF, KR_OFF = 0, 1024, 2048, 3072, 4096, 4112, 4176, 4688, 4944


def _fm_chunk_cols():
    chunks = []

    def rot_pair(base, hd, rot, heads):
        half = rot // 2
        a, b = [], []
        for h in heads:
            o = base + h * hd
            a += list(range(o, o + rot))
            b += list(range(o + half, o + rot)) + list(range(o, o + half))
        return a, b

    for base in (A_OFF, K_OFF):
        for c in range(2):
            a, b = rot_pair(base, 128, 32, range(4 * c, 4 * c + 4))
            chunks += [a, b]
        for h in range(8):
            chunks.append(list(range(base + h * 128 + 32, base + h * 128 + 128)))
    for c in range(2):
        a, b = rot_pair(QI_OFF, 64, 16, range(8 * c, 8 * c + 8))
        chunks += [a, b]
    for j in range(8):
        cols = []
        for h in (2 * j, 2 * j + 1):
            cols += list(range(QI_OFF + h * 64 + 16, QI_OFF + h * 64 + 64))
        chunks.append(cols)
    for c in range(4):
        chunks.append(list(range(CQ_OFF + c * 128, CQ_OFF + (c + 1) * 128)))
    for c in range(2):
        chunks.append(list(range(CKV_OFF + c * 128, CKV_OFF + (c + 1) * 128)))
    s1 = list(range(KR_OFF, KR_OFF + 64)) + list(range(KI_OFF, KI_OFF + 64))
    s2 = list(range(KR_OFF + 32, KR_OFF + 64)) + list(range(KR_OFF, KR_OFF + 32)) + \
        list(range(KI_OFF + 8, KI_OFF + 16)) + list(range(KI_OFF, KI_OFF + 8))
    chunks += [s1, s2]
    assert len(chunks) == NFM
    return chunks


def _lhsT_tiles(w, col_chunks):
    K = w.shape[0]
    KC = K // 128
    out = np.zeros((len(col_chunks), 128, KC, 128), np.float32)
    wr = w.reshape(KC, 128, w.shape[1])
    for c, cols in enumerate(col_chunks):
        cols = np.asarray(cols)
        out[c, :, :, :len(cols)] = wr[:, :, cols].transpose(1, 0, 2)
    return out.reshape(len(col_chunks) * 128, KC * 128)


def _rhs_tiles(w, blocks):
    K = w.shape[0]
    KC = K // 128
    wr = w.reshape(KC, 128, w.shape[1])
    outs = []
    for cols in blocks:
        cols = np.asarray(cols)
        outs.append(np.ascontiguousarray(wr[:, :, cols].transpose(1, 0, 2)).reshape(128, KC * len(cols)))
    return np.concatenate(outs, axis=0)


WSHAPES = {
    'win_fm': (NFM * 128, 2048),
    'win_tm': (128, 16 * 1040),
    'wuq': (16 * 128, 512),
    'wukv_k': (8 * 128, 256),
    'wukv_v': (128, 2048),
    'wo': (4 * 128, 16 * 512),
    'wg': (NFC * 128, 2048),
    'wu': (NFC * 128, 2048),
    'wd': (8 * 128, NFC * 256),
}


def _prep_layer(w_in, w_uq, w_ukv, w_o, w_gate, w_up, w_down):
    o = {}
    o['win_fm'] = _lhsT_tiles(w_in, _fm_chunk_cols())
    tm_cols = list(range(V_OFF, V_OFF + 1024)) + list(range(WI_OFF, WI_OFF + 16))
    o['win_tm'] = _rhs_tiles(w_in, [tm_cols])
    uq = []
    for h in range(8):
        uq.append(list(range(h * 192, h * 192 + 128)))
    for c in range(4):
        a, b = [], []
        for h in (2 * c, 2 * c + 1):
            a += list(range(h * 192 + 128, h * 192 + 192))
            b += list(range(h * 192 + 160, h * 192 + 192)) + list(range(h * 192 + 128, h * 192 + 160))
        uq += [a, b]
    o['wuq'] = _lhsT_tiles(w_uq, uq)
    o['wukv_k'] = _lhsT_tiles(w_ukv, [list(range(h * 256, h * 256 + 128)) for h in range(8)])
    vcols = []
    for h in range(8):
        vcols += list(range(h * 256 + 128, h * 256 + 256))
    o['wukv_v'] = _rhs_tiles(w_ukv, [vcols])
    o['wo'] = _rhs_tiles(w_o, [list(range(b * 512, (b + 1) * 512)) for b in range(4)])
    o['wg'] = _lhsT_tiles(w_gate, [list(range(c * 128, (c + 1) * 128)) for c in range(NFC)])
    o['wu'] = _lhsT_tiles(w_up, [list(range(c * 128, (c + 1) * 128)) for c in range(NFC)])
    o['wd'] = _rhs_tiles(w_down, [list(range(b * 256, (b + 1) * 256)) for b in range(8)])
    for k, v in o.items():
        assert v.shape == WSHAPES[k], (k, v.shape)
    return o


def _consts():
    p = np.arange(128)
    cst = np.zeros((128, 8), np.float32)
    j = (p % 32) % 16
    cst[:, 0] = (np.float32(500000.0) ** (-2.0 * j.astype(np.float32) / 32)).astype(np.float32)
    cst[:, 1] = np.where((p % 32) < 16, -1.0, 1.0)
    j = (p % 16) % 8
    cst[:, 2] = (np.float32(500000.0) ** (-2.0 * j.astype(np.float32) / 16)).astype(np.float32)
    cst[:, 3] = np.where((p % 16) < 8, -1.0, 1.0)
    j = (p % 64) % 32
    cst[:, 4] = (np.float32(10000.0) ** (-2.0 * j.astype(np.float32) / 64)).astype(np.float32)
    cst[:, 5] = np.where((p % 64) < 32, -1.0, 1.0)
    ident = np.eye(128, dtype=np.float32)
    t = np.arange(128)[:, None]
    s = np.arange(128)[None, :]
    tri = np.where(s <= t, 0.0, NEG).astype(np.float32)
    cm = np.zeros((128, 4, 512), np.float32)
    for jj in range(4):
        cm[:, jj, :] = ((128 * jj + np.arange(128)[:, None]) <= np.arange(512)[None, :]).astype(np.float32)
    return {'cst': cst, 'ident': ident, 'tri': tri, 'cmask': cm.reshape(128, 2048)}


def build_program(dbg=()):
    nc = bass.Bass("TRN2", target_bir_lowering=False)
    es = ExitStack()
    tr = TR(nc, es)
    op, dma = tr.op, tr.dma

    def din(name, shape, dt=F32):
        return nc.dram_tensor(name, list(shape), dt, kind="ExternalInput").ap()

    def dscr(name, shape, dt):
        kind = "ExternalOutput" if name in dbg else "Internal"
        return nc.dram_tensor(name, list(shape), dt, kind=kind).ap()

    x_in = din('x', [S, D])
    pos_in = din('positions', [1, S], I32)
    out_d = nc.dram_tensor('out', [S, D], F32, kind="ExternalOutput").ap()
    w_in32 = {k: din(k, [DEPTH * v[0], v[1]]) for k, v in WSHAPES.items()}
    gq_in = din('g_cq_t', [DEPTH * 128, 4])
    gkv_in = din('g_ckv_t', [DEPTH * 128, 2])
    ln_in = {k: din(k, [DEPTH, D]) for k in ('ln1_g', 'ln1_b', 'ln2_g', 'ln2_b')}
    cst_in = din('cst', [128, 8])
    ident_in = din('ident', [128, 128])
    tri_in = din('tri', [128, 128])
    cmask_in = din('cmask', [128, 2048])

    wb = {k: dscr('b_' + k, [DEPTH * v[0], v[1]], BF16) for k, v in WSHAPES.items()}
    tab = dscr('tab', [6 * 128, S], F32)
    xa = dscr('xa', [S, D], F32)
    xb_d = dscr('xb', [S, D], F32)
    qaT = dscr('qaT', [8 * 128, S], BF16)
    kaT = dscr('kaT', [8 * 128, S], BF16)
    va_hp = dscr('va_hp', [8 * 128, 32 * 128], BF16)
    qiT = dscr('qiT', [16 * 64, S], BF16)
    kiT = dscr('kiT', [64, S], BF16)
    wi_d = dscr('wi', [S, 16], F32)
    qnT = dscr('qnT', [8 * 128, S], BF16)
    qpT = dscr('qpT', [8 * 64, S], BF16)
    knT = dscr('knT', [8 * 128, S], BF16)
    kpT = dscr('kpT', [64, S], BF16)
    vb_hp = dscr('vb_hp', [8 * 128, 32 * 128], BF16)
    aoT = dscr('aoT', [D, S], BF16)
    if 'dbg_p2' in dbg:
        dacc = dscr('dacc', [256, S], F32)
        dwork = dscr('dwork', [256, S], F32)
        dm8 = dscr('dm8', [256, 8], F32)
        dthr = dscr('dthr', [256, 1], F32)
        dmask = dscr('dmask', [256, S], BF16)

    def sb(name, shape, dt):
        return es.enter_context(nc.sbuf_tensor(name, shape, dt))

    identb = sb('identb', [128, 128], BF16)
    onesb = sb('onesb', [128, 128], BF16)
    onesf = sb('onesf', [128, 128], F32)
    cst = sb('cstt', [128, 8], F32)
    trineg = sb('trineg', [128, 128], F32)
    cmaskb = sb('cmaskb', [128, 2048], BF16)
    negbig = sb('negbig', [128, 1], F32)
    negm = sb('negm', [128, 1], F32)

    PS = [p[:] for p in tr.ps]
    PSB = [p[:].bitcast(BF16) for p in tr.ps]

    with es:
        tr.reset()
        tmpf = tr.alloc(2048, F32)
        dma('sp', tmpf[:, 0:128], ident_in, writes=['tmpf'])
        op('dve', lambda e: e.tensor_copy(out=identb[:], in_=tmpf[:, 0:128]), reads=['tmpf'], writes=['identb'])
        dma('sp', tmpf, cmask_in, writes=['tmpf'])
        op('dve', lambda e: e.tensor_scalar(out=cmaskb[:], in0=tmpf, scalar1=MBIG, scalar2=-MBIG, op0=ALU.mult, op1=ALU.add),
           reads=['tmpf'], writes=['cmaskb'])
        dma('sp', cst[:], cst_in, writes=['cst'])
        dma('sp', trineg[:], tri_in, writes=['trineg'])
        op('dve', lambda e: e.memset(onesb[:], 1.0), writes=['onesb'])
        op('dve', lambda e: e.memset(onesf[:], 1.0), writes=['onesf'])
        op('dve', lambda e: e.memset(negbig[:], -1.0e29), writes=['negbig'])
        op('dve', lambda e: e.memset(negm[:], -MBIG), writes=['negm'])

        P1_W = ('win_fm', 'win_tm', 'wuq', 'wukv_k', 'wukv_v')
        cast_q = []

        def cast_weights(l, names, defer=False):
            for k in names:
                R, C = WSHAPES[k]
                rows = max(128, (1 << 21) // C // 128 * 128)
                r = 0
                while r < R:
                    n = min(rows, R - r)
                    args = (wb[k][l * R + r:l * R + r + n, :], w_in32[k][l * R + r:l * R + r + n, :])
                    if defer:
                        cast_q.append(args)
                    else:
                        dma('pool', args[0], args[1], writes=['W_%s_%d' % (k, l)])
                    r += n

        def drip_casts(n):
            while n > 0 and cast_q:
                a = cast_q.pop(0)
                dma('pool', a[0], a[1])
                n -= 1

        cast_weights(0, P1_W)

        posi = tr.alloc(S, I32)
        posf = tr.alloc(S, F32)
        ang = tr.alloc(S, F32)
        tmp = tr.alloc(S, F32)
        tmpi = tr.alloc(S, I32)
        res = [tr.alloc(S, F32), tr.alloc(S, F32)]
        dma('sp', posi, pos_in[0:1, :].to_broadcast([128, S]), writes=['posi'])
        op('dve', lambda e: e.tensor_copy(out=posf, in_=posi), reads=['posi'], writes=['posf'])
        PI = float(np.pi)
        for ty in range(3):
            for which in range(2):
                rb = res[(2 * ty + which) % 2]
                rt = 'res%d' % ((2 * ty + which) % 2)
                shift = PI / 2 if which == 0 else 0.0
                op('dve', lambda e: e.tensor_scalar(out=ang, in0=posf, scalar1=cst[:, 2 * ty:2 * ty + 1], scalar2=shift,
                                                    op0=ALU.mult, op1=ALU.add), reads=['posf', 'cst'], writes=['ang'])
                op('dve', lambda e: e.tensor_scalar(out=tmp, in0=ang, scalar1=float(1 / (2 * PI)), scalar2=None, op0=ALU.mult),
                   reads=['ang'], writes=['tmp'])
                op('dve', lambda e: e.tensor_copy(out=tmpi, in_=tmp), reads=['tmp'], writes=['tmpi'])
                op('dve', lambda e: e.tensor_copy(out=tmp, in_=tmpi), reads=['tmpi'], writes=['tmp'])
                op('dve', lambda e: e.scalar_tensor_tensor(out=ang, in0=tmp, scalar=float(-2 * PI), in1=ang, op0=ALU.mult, op1=ALU.add),
                   reads=['tmp', 'ang'], writes=['ang'])
                op('dve', lambda e: e.tensor_scalar(out=tmp, in0=ang, scalar1=PI, scalar2=float(-2 * PI), op0=ALU.is_gt, op1=ALU.mult),
                   reads=['ang'], writes=['tmp'])
                op('dve', lambda e: e.tensor_tensor(out=ang, in0=ang, in1=tmp, op=ALU.add), reads=['ang', 'tmp'], writes=['ang'])
                op('dve', lambda e: e.tensor_scalar(out=tmp, in0=ang, scalar1=-PI, scalar2=float(2 * PI), op0=ALU.is_lt, op1=ALU.mult),
                   reads=['ang'], writes=['tmp'])
                op('dve', lambda e: e.tensor_tensor(out=ang, in0=ang, in1=tmp, op=ALU.add), reads=['ang', 'tmp'], writes=['ang'])
                op('dve', lambda e: e.tensor_scalar(out=ang, in0=ang, scalar1=PI, scalar2=-PI, op0=ALU.min, op1=ALU.max),
                   reads=['ang'], writes=['ang'])
                op('act', lambda e: e.activation(out=rb, in_=ang, func=AF.Sin), reads=['ang'], writes=[rt])
                if which == 1:
                    op('dve', lambda e: e.tensor_scalar(out=rb, in0=rb, scalar1=cst[:, 2 * ty + 1:2 * ty + 2], scalar2=None, op0=ALU.mult),
                       reads=[rt, 'cst'], writes=[rt])
                k = 2 * ty + which
                dma('sp', tab[k * 128:(k + 1) * 128, :], rb, reads=[rt])
        tr.barrier()
        cast_weights(0, [k for k in WSHAPES if k not in P1_W], defer=True)
        cast_weights(1, list(WSHAPES), defer=True)

        def layer(l, x_src, x_dst):
            def W(k, r0, r1):
                R = WSHAPES[k][0]
                return wb[k][l * R + r0:l * R + r1, :]

            tr.reset()
            wtm = tr.alloc(16 * 1040, BF16).rearrange("p (k n) -> p k n", k=16)
            xf = [tr.alloc(D, F32) for _ in range(2)]
            xbf = [tr.alloc(D, BF16) for _ in range(2)]
            xT = [tr.alloc(16 * G, BF16).rearrange("p (k n) -> p k n", k=16) for _ in range(2)]
            wfm = [tr.alloc(16 * 128, BF16).rearrange("p (k n) -> p k n", k=16) for _ in range(3)]
            tabs = {k: tr.alloc(G, F32) for k in ('a_c', 'a_s', 'i_c', 'i_s', 'm_c', 'm_s', 's_c', 's_s')}
            t1 = [tr.alloc(G, F32) for _ in range(2)]
            t2 = [tr.alloc(G, F32) for _ in range(2)]
            ob = [tr.alloc(G, BF16) for _ in range(4)]
            cqf = tr.alloc(4 * G, F32).rearrange("p (k n) -> p k n", k=4)
            ckvf = tr.alloc(2 * G, F32).rearrange("p (k n) -> p k n", k=2)
            sq = [tr.alloc(G, F32) for _ in range(2)]
            rstd = {'q': tr.alloc(G, F32), 'kv': tr.alloc(G, F32)}
            cqn = tr.alloc(4 * G, BF16).rearrange("p (k n) -> p k n", k=4)
            ckvn = tr.alloc(2 * G, BF16).rearrange("p (k n) -> p k n", k=2)
            wuq = tr.alloc(16 * 512, BF16).rearrange("p (c k n) -> p c k n", c=16, k=4)
            wukk = tr.alloc(8 * 256, BF16).rearrange("p (c k n) -> p c k n", c=8, k=2)
            wukv = tr.alloc(2 * 1024, BF16).rearrange("p (k n) -> p k n", k=2)
            gq = tr.alloc(4, F32)
            gkv = tr.alloc(2, F32)
            vtm = [tr.alloc(1024, BF16) for _ in range(2)]
            wio = [tr.alloc(16, F32) for _ in range(2)]

            dma('sp', wtm, W('win_tm', 0, 128).rearrange("p (k n) -> p k n", k=16), reads=['W_win_tm_%d' % l], writes=['wtm'])
            for c in range(16):
                dma('sp', wuq[:, c], W('wuq', c * 128, (c + 1) * 128).rearrange("p (k n) -> p k n", k=4),
                    reads=['W_wuq_%d' % l], writes=['wuq'])
            for c in range(8):
                dma('sp', wukk[:, c], W('wukv_k', c * 128, (c + 1) * 128).rearrange("p (k n) -> p k n", k=2),
                    reads=['W_wukv_k_%d' % l], writes=['wukk'])
            dma('sp', wukv, W('wukv_v', 0, 128).rearrange("p (k n) -> p k n", k=2), reads=['W_wukv_v_%d' % l], writes=['wukv'])
            dma('sp', gq, gq_in[l * 128:(l + 1) * 128, :], writes=['gq'])
            dma('sp', gkv, gkv_in[l * 128:(l + 1) * 128, :], writes=['gkv'])
            op('pool', lambda e: e.memset(tabs['s_c'][64:128, :], 1.0), writes=['s_c'])
            op('pool', lambda e: e.memset(tabs['s_s'][64:128, :], 0.0), writes=['s_s'])

            bankrr = [0]

            def nbank(lo=2, n=5):
                b = lo + bankrr[0] % n
                bankrr[0] += 1
                return b

            wcnt = [0]

            def load_wfm(ci):
                b = wcnt[0] % 3
                wcnt[0] += 1
                dma('sp', wfm[b], W('win_fm', ci * 128, (ci + 1) * 128).rearrange("p (k n) -> p k n", k=16),
                    reads=['W_win_fm_%d' % l], writes=['wfm%d' % b])
                return b

            cnt = {'ob': 0, 't': 0, 'sq': 0, 'v': 0}

            def store_rows(src, srct, dsts, ts):
                for (p0, p1, dr) in dsts:
                    dma('pool', dr[:, ts], src[p0:p1, :], reads=[srct])

            def evac_copy(bank, M, dsts, ts):
                k = cnt['ob'] % 4
                cnt['ob'] += 1
                op('act', lambda e: e.copy(out=ob[k][0:M, :], in_=PS[bank][0:M, :]), reads=['ps%d' % bank], writes=['ob%d' % k])
                store_rows(ob[k], 'ob%d' % k, dsts, ts)

            def evac_rope(bankA, bankB, ty, dsts, ts):
                k = cnt['ob'] % 4
                cnt['ob'] += 1
                j = cnt['t'] % 2
                cnt['t'] += 1
                op('dve', lambda e: e.tensor_tensor(out=t1[j], in0=PS[bankA], in1=tabs[ty + '_c'], op=ALU.mult),
                   reads=['ps%d' % bankA, ty + '_c'], writes=['t1%d' % j])
                op('dve', lambda e: e.tensor_tensor(out=t2[j], in0=PS[bankB], in1=tabs[ty + '_s'], op=ALU.mult),
                   reads=['ps%d' % bankB, ty + '_s'], writes=['t2%d' % j])
                op('pool', lambda e: e.tensor_tensor(out=ob[k], in0=t1[j], in1=t2[j], op=ALU.add),
                   reads=['t1%d' % j, 't2%d' % j], writes=['ob%d' % k])
                store_rows(ob[k], 'ob%d' % k, dsts, ts)

            def mm_group(bank, M, N, lhs_fn, rhs_fn, KC, reads):
                for kc in range(KC):
                    last = kc == KC - 1
                    op('pe', lambda e: e.matmul(PS[bank][0:M, 0:N], lhsT=lhs_fn(kc), rhs=rhs_fn(kc), start=(kc == 0), stop=last),
                       reads=reads, writes=['ps%d' % bank], inc=last)

            def rows(t, r0, n):
                return t[r0:r0 + n, :]

            for g in range(NG):
                ts = slice(g * G, (g + 1) * G)
                xt = xT[g % 2]
                xtt = 'xT%d' % (g % 2)
                for nm, k in (('a_c', 0), ('a_s', 1), ('i_c', 2), ('i_s', 3), ('m_c', 4), ('m_s', 5)):
                    dma('sp', tabs[nm], tab[k * 128:(k + 1) * 128, ts], writes=[nm])
                dma('sp', tabs['s_c'][0:64, :], tab[4 * 128:4 * 128 + 64, ts], writes=['s_c'])
                dma('sp', tabs['s_s'][0:64, :], tab[5 * 128:5 * 128 + 64, ts], writes=['s_s'])
                dma('sp', tabs['s_c'][64:80, :], tab[2 * 128 + 64:2 * 128 + 80, ts], writes=['s_c'])
                dma('sp', tabs['s_s'][64:80, :], tab[3 * 128 + 64:3 * 128 + 80, ts], writes=['s_s'])
                for i in range(4):
                    b = i % 2
                    r0 = g * G + i * 128
                    dma('sp', xf[b], x_src[r0:r0 + 128, :], writes=['xf%d' % b])
                    op('pool', lambda e: e.tensor_copy(out=xbf[b], in_=xf[b]), reads=['xf%d' % b], writes=['xbf%d' % b])
                    for hb in range(2):
                        for c in range(8):
                            kc = hb * 8 + c
                            op('pe', lambda e: e.transpose(out=PSB[hb][:, c * 128:(c + 1) * 128], in_=xbf[b][:, kc * 128:(kc + 1) * 128],
                                                           identity=identb[:]),
                               reads=['xbf%d' % b, 'identb'], writes=['ps%d' % hb], inc=(c == 7))
                        eng = 'act' if hb == 0 else 'dve'
                        op(eng, lambda e: (e.copy if eng == 'act' else e.tensor_copy)(
                            out=xt[:, hb * 8:(hb + 1) * 8, i * 128:(i + 1) * 128],
                            in_=PSB[hb].rearrange("p (c n) -> p c n", c=8)), reads=['ps%d' % hb], writes=[xtt])
                for i in range(4):
                    r0 = g * G + i * 128
                    vb_ = cnt['v'] % 2
                    cnt['v'] += 1
                    for cb in range(2):
                        bank = nbank()
                        mm_group(bank, 128, 512, lambda kc: xt[:, kc, i * 128:(i + 1) * 128], lambda kc: wtm[:, kc, cb * 512:(cb + 1) * 512],
                                 16, [xtt, 'wtm'])
                        op('act', lambda e: e.copy(out=vtm[vb_][:, cb * 512:(cb + 1) * 512], in_=PS[bank]), reads=['ps%d' % bank],
                           writes=['vtm%d' % vb_])
                    blk = (g * G) // 128 + i
                    for h in range(8):
                        dma('pool', va_hp[h * 128:(h + 1) * 128, blk * 128:(blk + 1) * 128], vtm[vb_][:, h * 128:(h + 1) * 128],
                            reads=['vtm%d' % vb_])
                    bank = nbank()
                    mm_group(bank, 128, 16, lambda kc: xt[:, kc, i * 128:(i + 1) * 128], lambda kc: wtm[:, kc, 1024:1040], 16, [xtt, 'wtm'])
                    op('act', lambda e: e.activation(out=wio[vb_], in_=PS[bank][:, 0:16], func=AF.Copy, scale=1.0 / 32.0),
                       reads=['ps%d' % bank], writes=['wio%d' % vb_])
                    dma('pool', wi_d[r0:r0 + 128, :], wio[vb_], reads=['wio%d' % vb_])
                ci = 0
                nxt = load_wfm(0)

                def fm(M=128):
                    nonlocal ci, nxt
                    b = nxt
                    if ci + 1 < NFM:
                        nxt = load_wfm(ci + 1)
                    bank = nbank()
                    mm_group(bank, M, G, lambda kc: wfm[b][:, kc, 0:M], lambda kc: xt[:, kc, :], 16, ['wfm%d' % b, xtt])
                    ci += 1
                    return bank

                for (dst, nm) in ((qaT, 'a'), (kaT, 'a')):
                    for c in range(2):
                        ba = fm()
                        bb = fm()
                        evac_rope(ba, bb, 'a', [(32 * k, 32 * k + 32, rows(dst, (4 * c + k) * 128, 32)) for k in range(4)], ts)
                    for h in range(8):
                        ba = fm(96)
                        evac_copy(ba, 96, [(0, 96, rows(dst, h * 128 + 32, 96))], ts)
                for c in range(2):
                    ba = fm()
                    bb = fm()
                    evac_rope(ba, bb, 'i', [(16 * k, 16 * k + 16, rows(qiT, (8 * c + k) * 64, 16)) for k in range(8)], ts)
                for j in range(8):
                    ba = fm(96)
                    evac_copy(ba, 96, [(0, 48, rows(qiT, (2 * j) * 64 + 16, 48)), (48, 96, rows(qiT, (2 * j + 1) * 64 + 16, 48))], ts)
                for (nchunk, dstf, nm) in ((4, cqf, 'q'), (2, ckvf, 'kv')):
                    sbank = 7
                    for c in range(nchunk):
                        ba = fm()
                        op('act', lambda e: e.copy(out=dstf[:, c, :], in_=PS[ba]), reads=['ps%d' % ba], writes=['cf' + nm])
                        k = cnt['sq'] % 2
                        cnt['sq'] += 1
                        op('act', lambda e: e.activation(out=sq[k], in_=PS[ba], func=AF.Square), reads=['ps%d' % ba], writes=['sq%d' % k])
                        op('pe', lambda e: e.matmul(PS[sbank], lhsT=onesf[:], rhs=sq[k], start=(c == 0), stop=(c == nchunk - 1)),
                           reads=['sq%d' % k, 'onesf'], writes=['ps%d' % sbank])
                    rs = rstd[nm]
                    op('dve', lambda e: e.tensor_scalar(out=rs, in0=PS[sbank], scalar1=1.0 / (128 * nchunk), scalar2=RMS_EPS,
                                                        op0=ALU.mult, op1=ALU.add), reads=['ps%d' % sbank], writes=['rs' + nm])
                    op('act', lambda e: e.activation(out=rs, in_=rs, func=AF.Sqrt), reads=['rs' + nm], writes=['rs' + nm])
                    op('dve', lambda e: e.reciprocal(out=rs, in_=rs), reads=['rs' + nm], writes=['rs' + nm])
                    gg = gq if nm == 'q' else gkv
                    dn = cqn if nm == 'q' else ckvn
                    for c in range(nchunk):
                        op('dve', lambda e: e.scalar_tensor_tensor(out=dn[:, c, :], in0=dstf[:, c, :], scalar=gg[:, c:c + 1], in1=rs,
                                                                   op0=ALU.mult, op1=ALU.mult),
                           reads=['cf' + nm, 'rs' + nm, 'gq', 'gkv'], writes=['n' + nm])
                ba = fm()
                bb = fm()
                evac_rope(ba, bb, 's', [(0, 64, kpT), (64, 128, kiT)], ts)
                assert ci == NFM
                for h in range(8):
                    bank = nbank()
                    mm_group(bank, 128, G, lambda kc: wuq[:, h, kc, :], lambda kc: cqn[:, kc, :], 4, ['wuq', 'nq'])
                    evac_copy(bank, 128, [(0, 128, rows(qnT, h * 128, 128))], ts)
                for c in range(4):
                    ba = nbank()
                    mm_group(ba, 128, G, lambda kc: wuq[:, 8 + 2 * c, kc, :], lambda kc: cqn[:, kc, :], 4, ['wuq', 'nq'])
                    bb = nbank()
                    mm_group(bb, 128, G, lambda kc: wuq[:, 9 + 2 * c, kc, :], lambda kc: cqn[:, kc, :], 4, ['wuq', 'nq'])
                    evac_rope(ba, bb, 'm', [(0, 64, rows(qpT, (2 * c) * 64, 64)), (64, 128, rows(qpT, (2 * c + 1) * 64, 64))], ts)
                for h in range(8):
                    bank = nbank()
                    mm_group(bank, 128, G, lambda kc: wukk[:, h, kc, :], lambda kc: ckvn[:, kc, :], 2, ['wukk', 'nkv'])
                    evac_copy(bank, 128, [(0, 128, rows(knT, h * 128, 128))], ts)
                for i in range(4):
                    vb_ = cnt['v'] % 2
                    cnt['v'] += 1
                    for cb in range(2):
                        bank = nbank()
                        mm_group(bank, 128, 512, lambda kc: ckvn[:, kc, i * 128:(i + 1) * 128], lambda kc: wukv[:, kc, cb * 512:(cb + 1) * 512],
                                 2, ['nkv', 'wukv'])
                        op('act', lambda e: e.copy(out=vtm[vb_][:, cb * 512:(cb + 1) * 512], in_=PS[bank]), reads=['ps%d' % bank],
                           writes=['vtm%d' % vb_])
                    blk = (g * G) // 128 + i
                    for h in range(8):
                        dma('pool', vb_hp[h * 128:(h + 1) * 128, blk * 128:(blk + 1) * 128], vtm[vb_][:, h * 128:(h + 1) * 128],
                            reads=['vtm%d' % vb_])
            tr.barrier()
            if 'stop_p1' in dbg:
                return

            tr.reset()
            ki2 = tr.alloc(S, BF16)
            qit = [tr.alloc(8 * 128, BF16).rearrange("p (j n) -> p j n", j=8) for _ in range(2)]
            wit = [tr.alloc(16, F32) for _ in range(2)]
            acc = [tr.alloc(S, F32) for _ in range(2)]
            work = tr.alloc(S, F32)
            rl = [tr.alloc(512, BF16) for _ in range(4)]
            diag = [tr.alloc(16 * 128, BF16).rearrange("p (h n) -> p h n", h=16) for _ in range(2)]
            m8 = [tr.alloc(8, F32) for _ in range(2)]
            thr = [tr.alloc(1, F32) for _ in range(2)]
            maskb = [tr.alloc(S, BF16) for _ in range(4)]
            maskT = tr.alloc(32 * 512, BF16).rearrange("p (b n) -> p b n", b=32)
            kT = [tr.alloc(S, BF16) for _ in range(2)]
            vv = [tr.alloc(32 * 128, BF16).rearrange("p (b n) -> p b n", b=32) for _ in range(2)]
            qq = [tr.alloc(512, BF16) for _ in range(2)]
            kpe = tr.alloc(S, BF16)
            qpe = [tr.alloc(512, BF16) for _ in range(2)]
            Pb = [tr.alloc(512, BF16) for _ in range(3)]
            rc = [tr.alloc(512, F32) for _ in range(2)]
            oo = [tr.alloc(512, BF16) for _ in range(2)]
            pos_ = [tr.alloc(512, F32) for _ in range(2)]

            dma('sp', ki2[0:64, :], kiT, writes=['ki2'])
            dma('sp', ki2[64:128, :], kiT, writes=['ki2'])
            dma('sp', kpe[0:64, :], kpT, writes=['kpe'])
            c2 = {'q': 0, 'rl': 0, 'ib': 0, 'tb': 0, 'kv': 0, 'P': 0, 'qk': 0, 'o': 0, 'm8': 0}
            qi_v = qiT.rearrange("(j two d) s -> (two d) j s", two=2, d=64)
            cm4 = cmaskb[:].rearrange("p (j n) -> p j n", j=4)

            def score_prep(qt):
                b = qt % 2
                dma('sp', qit[b], qi_v[:, :, qt * 128:(qt + 1) * 128], writes=['qit%d' % b])
                dma('sp', wit[b], wi_d[qt * 128:(qt + 1) * 128, :], writes=['wit%d' % b])
                for h in range(16):
                    op('pool', lambda e: e.tensor_scalar(out=diag[b][:, h, :], in0=identb[:], scalar1=wit[b][:, h:h + 1], scalar2=None,
                                                         op0=ALU.mult),
                       reads=['identb', 'wit%d' % b], writes=['diag%d' % b])

            def score_tile(qt, q4):
                L = (qt + 1) * 128
                b = qt % 2
                a = acc[b]
                at = 'acc%d' % b
                dg = diag[b]
                nsb = (L + 511) // 512
                for sbk in range(nsb):
                    n = min(512, L - sbk * 512)
                    ks = []

                    def a_mm(h):
                        bank = (0, 1, 3)[c2['ib'] % 3]
                        c2['ib'] += 1
                        p0 = (h % 2) * 64
                        op('pe', lambda e: e.matmul(PS[bank][:, 0:n], lhsT=qit[b][p0:p0 + 64, h // 2, :], rhs=ki2[p0:p0 + 64, sbk * 512:sbk * 512 + n],
                                                    start=True, stop=True), reads=['qit%d' % b, 'ki2'], writes=['ps%d' % bank])
                        k = c2['rl'] % 4
                        c2['rl'] += 1
                        op('act', lambda e: e.activation(out=rl[k][:, 0:n], in_=PS[bank][:, 0:n], func=AF.Relu), reads=['ps%d' % bank],
                           writes=['rl%d' % k])
                        ks.append(k)

                    def d_mm(h):
                        k = ks[h]
                        op('pe', lambda e: e.matmul(PS[2][:, 0:n], lhsT=dg[:, h, :], rhs=rl[k][:, 0:n], start=(h == 0), stop=(h == 15)),
                           reads=['diag%d' % b, 'rl%d' % k], writes=['ps2'])

                    a_mm(0)
                    a_mm(1)
                    for h in range(16):
                        if h + 2 < 16:
                            a_mm(h + 2)
                        d_mm(h)
                    op('act', lambda e: e.copy(out=a[:, sbk * 512:sbk * 512 + n], in_=PS[2][:, 0:n]), reads=['ps2'], writes=[at])
                op('dve', lambda e: e.tensor_tensor(out=a[:, qt * 128:L], in0=a[:, qt * 128:L], in1=trineg[:], op=ALU.add),
                   reads=[at, 'trineg'], writes=[at])
                mk = maskb[q4]
                mt = 'maskb%d' % q4
                if L > 256:
                    for r in range(32):
                        mi = c2['m8'] % 2
                        c2['m8'] += 1
                        src = a[:, 0:L] if r == 0 else work[:, 0:L]
                        op('dve', lambda e: e.max(out=m8[mi], in_=src), reads=[at, 'work'], writes=['m8%d' % mi])
                        if r < 31:
                            op('dve', lambda e: e.match_replace(out=work[:, 0:L], in_to_replace=m8[mi], in_values=src, imm_value=NEG),
                               reads=[at, 'work', 'm8%d' % mi], writes=['work'])
                    tb_ = thr[b]
                    op('dve', lambda e: e.tensor_scalar(out=tb_, in0=m8[mi][:, 7:8], scalar1=-1.0e29, scalar2=None, op0=ALU.max),
                       reads=['m8%d' % mi], writes=['thr%d' % b])
                    op('dve', lambda e: e.tensor_scalar(out=mk[:, 0:L], in0=a[:, 0:L], scalar1=tb_, scalar2=None, op0=ALU.is_ge),
                       reads=[at, 'thr%d' % b], writes=[mt])
                else:
                    op('dve', lambda e: e.tensor_scalar(out=mk[:, 0:L], in0=a[:, 0:L], scalar1=negbig[:], scalar2=None, op0=ALU.is_ge),
                       reads=[at, 'negbig'], writes=[mt])
                if 'dbg_p2' in dbg and l == 0 and qt in (2, 9):
                    di = 0 if qt == 2 else 1
                    dma('sp', dacc[di * 128:(di + 1) * 128, 0:L], a[:, 0:L], reads=[at])
                    dma('sp', dwork[di * 128:(di + 1) * 128, 0:L], work[:, 0:L], reads=['work'])
                    dma('sp', dm8[di * 128:(di + 1) * 128, :], m8[mi], reads=['m8%d' % mi])
                    dma('sp', dthr[di * 128:(di + 1) * 128, :], thr[b], reads=['thr%d' % b])
                    dma('sp', dmask[di * 128:(di + 1) * 128, 0:L], mk[:, 0:L], reads=[mt])
                Lmax = (qt // 4 + 1) * 512
                if L < Lmax:
                    op('dve', lambda e: e.memset(mk[:, L:Lmax], 0.0), writes=[mt])

            def transposes_tile(j):
                tbk, q4 = j // 4, j % 4
                nblk = 4 * tbk + 4
                blk0 = 0
                while blk0 < nblk:
                    nb = min(8, nblk - blk0)
                    for i in range(nb):
                        blk = blk0 + i
                        op('pe', lambda e: e.transpose(out=PSB[3][:, i * 128:(i + 1) * 128], in_=maskb[q4][:, blk * 128:(blk + 1) * 128],
                                                       identity=identb[:]),
                           reads=['maskb%d' % q4, 'identb'], writes=['ps3'], inc=(i == nb - 1))
                    op('act', lambda e: e.activation(out=maskT[:, blk0:blk0 + nb, q4 * 128:(q4 + 1) * 128],
                                                     in_=PSB[3][:, 0:nb * 128].rearrange("p (b n) -> p b n", b=nb),
                                                     func=AF.Identity, scale=MBIG, bias=negm[:]),
                       reads=['ps3', 'negm'], writes=['maskT'])
                    blk0 += nb

            def attn_head(tbk, h, mla):
                nblk = 4 * tbk + 4
                Lmax = nblk * 128
                tsl = slice(tbk * 512, (tbk + 1) * 512)
                b = c2['kv'] % 2
                c2['kv'] += 1
                ksrc, vsrc, qsrc = (knT, vb_hp, qnT) if mla else (kaT, va_hp, qaT)
                dma('sp', kT[b][:, 0:Lmax], ksrc[h * 128:(h + 1) * 128, 0:Lmax], writes=['kT%d' % b])
                dma('sp', vv[b][:, 0:nblk, :], vsrc[h * 128:(h + 1) * 128, 0:Lmax].rearrange("p (b n) -> p b n", n=128), writes=['vv%d' % b])
                dma('sp', qq[b], qsrc[h * 128:(h + 1) * 128, tsl], writes=['qq%d' % b])
                if mla:
                    dma('sp', qpe[b][0:64, :], qpT[h * 64:(h + 1) * 64, tsl], writes=['qpe%d' % b])
                scale = float((192.0 if mla else 128.0) ** -0.5)
                qkb = {}

                def qk(sc):
                    bank = 4 + c2['qk'] % 2
                    c2['qk'] += 1
                    qkb[sc] = bank
                    diag_blk = mla and sc >= 4 * tbk
                    op('pe', lambda e: e.matmul(PS[bank], lhsT=kT[b][:, sc * 128:(sc + 1) * 128], rhs=qq[b], start=True, stop=False),
                       reads=['kT%d' % b, 'qq%d' % b], writes=['ps%d' % bank], inc=False)
                    if mla:
                        op('pe', lambda e: e.matmul(PS[bank], lhsT=kpe[0:64, sc * 128:(sc + 1) * 128], rhs=qpe[b][0:64, :], start=False,
                                                    stop=not diag_blk),
                           reads=['kpe', 'qpe%d' % b], writes=['ps%d' % bank], inc=not diag_blk)
                        if diag_blk:
                            op('pe', lambda e: e.matmul(PS[bank], lhsT=identb[:], rhs=cm4[:, sc - 4 * tbk, :], start=False, stop=True),
                               reads=['identb', 'cmaskb'], writes=['ps%d' % bank])
                    else:
                        op('pe', lambda e: e.matmul(PS[bank], lhsT=identb[:], rhs=maskT[:, sc, :], start=False, stop=True),
                           reads=['identb', 'maskT'], writes=['ps%d' % bank])

                qk(0)
                for sc in range(nblk):
                    if sc + 1 < nblk:
                        qk(sc + 1)
                    bank = qkb[sc]
                    k = c2['P'] % 3
                    c2['P'] += 1
                    op('act', lambda e: e.activation(out=Pb[k], in_=PS[bank], func=AF.Exp, scale=scale), reads=['ps%d' % bank], writes=['P%d' % k])
                    op('pe', lambda e: e.matmul(PS[6], lhsT=vv[b][:, sc, :], rhs=Pb[k], start=(sc == 0), stop=(sc == nblk - 1)),
                       reads=['vv%d' % b, 'P%d' % k], writes=['ps6'], inc=False)
                    op('pe', lambda e: e.matmul(PS[7], lhsT=onesb[:], rhs=Pb[k], start=(sc == 0), stop=(sc == nblk - 1)),
                       reads=['onesb', 'P%d' % k], writes=['ps7'])
                j = c2['o'] % 2
                c2['o'] += 1
                op('act', lambda e: e.activation(out=rc[j], in_=PS[7], func=AF.Ln), reads=['ps7'], writes=['rc%d' % j])
                op('act', lambda e: e.copy(out=pos_[j], in_=PS[6]), reads=['ps6'], writes=['pos%d' % j])
                op('act', lambda e: e.activation(out=rc[j], in_=rc[j], func=AF.Exp, scale=-1.0), reads=['rc%d' % j], writes=['rc%d' % j])
                op('pool', lambda e: e.tensor_tensor(out=oo[j], in0=pos_[j], in1=rc[j], op=ALU.mult),
                   reads=['pos%d' % j, 'rc%d' % j], writes=['oo%d' % j])
                r0 = (8 + h if mla else h) * 128
                dma('pool', aoT[r0:r0 + 128, tsl], oo[j], reads=['oo%d' % j])

            NTB = 3 if 'dbg_p2' in dbg else 8
            NT = 4 * NTB
            st = {'scored': 0, 'T': 0}
            dsa_done = [0] * NTB
            mla_done = [0] * NTB

            def ensure_dsa(tb_, n):
                ensure_T(4 * tb_ + 4)
                while dsa_done[tb_] < n:
                    attn_head(tb_, dsa_done[tb_], False)
                    dsa_done[tb_] += 1

            def ensure_T(upto):
                while st['T'] < upto:
                    j = st['T']
                    if j % 4 == 0 and j >= 4:
                        ensure_dsa(j // 4 - 1, 8)
                    assert st['scored'] > j
                    transposes_tile(j)
                    st['T'] += 1

            k = 0
            score_prep(0)
            while True:
                if k < NT:
                    if k >= 4:
                        ensure_T(k - 3)
                    if k + 1 < NT:
                        score_prep(k + 1)
                    score_tile(k, k % 4)
                    st['scored'] += 1
                njobs = 0
                while njobs < 4:
                    cand = [t for t in range(NTB) if dsa_done[t] < 8 and st['T'] >= 4 * t + 4]
                    if cand:
                        t = cand[0]
                        attn_head(t, dsa_done[t], False)
                        dsa_done[t] += 1
                        njobs += 1
                        continue
                    cand = [t for t in range(NTB) if mla_done[t] < 8 and t <= (k - 4) // 4]
                    if cand:
                        t = cand[0]
                        attn_head(t, mla_done[t], True)
                        mla_done[t] += 1
                        njobs += 1
                        continue
                    break
                ensure_T(max(0, min(k - 1, NT, st['scored'])))
                drip_casts(2)
                k += 1
                if k >= NT and st['T'] >= NT and all(d == 8 for d in dsa_done) and all(d == 8 for d in mla_done):
                    break
                assert k < NT + 64
            drip_casts(1000)
            tr.barrier()
            if 'stop_p2' in dbg:
                return

            def layer_norm(xt_, xtt_, gt, bt, stt, mv, sd):
                xv = xt_.rearrange("p (c n) -> p c n", c=4)
                for c in range(4):
                    op('dve', lambda e: e.bn_stats(out=stt[:, c, :], in_=xv[:, c, :]), reads=[xtt_], writes=['stt'])
                op('dve', lambda e: e.bn_aggr(out=mv, in_=stt), reads=['stt'], writes=['mv'])
                op('dve', lambda e: e.tensor_scalar(out=sd, in0=mv[:, 1:2], scalar1=LN_EPS, scalar2=None, op0=ALU.add), reads=['mv'], writes=['sd'])
                op('act', lambda e: e.activation(out=sd, in_=sd, func=AF.Sqrt), reads=['sd'], writes=['sd'])
                op('dve', lambda e: e.reciprocal(out=sd, in_=sd), reads=['sd'], writes=['sd'])
                op('dve', lambda e: e.tensor_scalar(out=xt_, in0=xt_, scalar1=mv[:, 0:1], scalar2=sd, op0=ALU.subtract, op1=ALU.mult),
                   reads=[xtt_, 'mv', 'sd'], writes=[xtt_])
                op('pool', lambda e: e.tensor_tensor(out=xt_, in0=xt_, in1=gt, op=ALU.mult), reads=[xtt_, 'lng'], writes=[xtt_])
                op('pool', lambda e: e.tensor_tensor(out=xt_, in0=xt_, in1=bt, op=ALU.add), reads=[xtt_, 'lnb'], writes=[xtt_])

            tr.reset()
            aT = [tr.alloc(16 * G, BF16).rearrange("p (k n) -> p k n", k=16) for _ in range(2)]
            wo = [tr.alloc(16 * 512, BF16).rearrange("p (k n) -> p k n", k=16) for _ in range(2)]
            xs = [tr.alloc(D, F32) for _ in range(8)]
            lng = tr.alloc(D, F32)
            lnb = tr.alloc(D, F32)
            stt = tr.alloc(4 * 6, F32).rearrange("p (c n) -> p c n", c=4)
            mv = tr.alloc(2, F32)
            sd = tr.alloc(1, F32)
            dma('sp', lng, ln_in['ln1_g'][l:l + 1, :].to_broadcast([128, D]), writes=['lng'])
            dma('sp', lnb, ln_in['ln1_b'][l:l + 1, :].to_broadcast([128, D]), writes=['lnb'])
            aoT_v = aoT.rearrange("(k p) s -> p k s", p=128)
            wc = 0
            bk = 0
            for g in range(NG):
                ab = g % 2
                dma('sp', aT[ab], aoT_v[:, :, g * G:(g + 1) * G], writes=['aT%d' % ab])
                for i in range(4):
                    xi = (g % 2) * 4 + i
                    r0 = g * G + i * 128
                    dma('sp', xs[xi], x_src[r0:r0 + 128, :], writes=['xs%d' % xi])
                for cb in range(4):
                    wbf = wc % 2
                    wc += 1
                    dma('sp', wo[wbf], W('wo', cb * 128, (cb + 1) * 128).rearrange("p (k n) -> p k n", k=16), writes=['wo%d' % wbf])
                    for i in range(4):
                        xi = (g % 2) * 4 + i
                        bank = bk % 6
                        bk += 1
                        mm_group(bank, 128, 512, lambda kc: aT[ab][:, kc, i * 128:(i + 1) * 128], lambda kc: wo[wbf][:, kc, :], 16,
                                 ['aT%d' % ab, 'wo%d' % wbf])
                        xsl = xs[xi][:, cb * 512:(cb + 1) * 512]
                        op('dve', lambda e: e.scalar_tensor_tensor(out=xsl, in0=xsl, scalar=ALPHA, in1=PS[bank], op0=ALU.mult, op1=ALU.add),
                           reads=['ps%d' % bank, 'xs%d' % xi], writes=['xs%d' % xi])
                for i in range(4):
                    xi = (g % 2) * 4 + i
                    r0 = g * G + i * 128
                    layer_norm(xs[xi], 'xs%d' % xi, lng, lnb, stt, mv, sd)
                    dma('pool', xa[r0:r0 + 128, :], xs[xi], reads=['xs%d' % xi])
            tr.barrier()
            if 'stop_p3' in dbg:
                return

            tr.reset()
            x1 = [tr.alloc(D, F32) for _ in range(4)]
            xbf = [tr.alloc(D, BF16) for _ in range(2)]
            x1T = tr.alloc(16 * G, BF16).rearrange("p (k n) -> p k n", k=16)
            hT = tr.alloc(NFC * G, BF16).rearrange("p (k n) -> p k n", k=NFC)
            wgb = [tr.alloc(16 * 128, BF16).rearrange("p (k n) -> p k n", k=16) for _ in range(2)]
            wub = [tr.alloc(16 * 128, BF16).rearrange("p (k n) -> p k n", k=16) for _ in range(2)]
            wdb = [tr.alloc(NFC * 256, BF16).rearrange("p (k n) -> p k n", k=NFC) for _ in range(2)]
            sg = [tr.alloc(G, F32) for _ in range(2)]
            lng = tr.alloc(D, F32)
            lnb = tr.alloc(D, F32)
            stt = tr.alloc(4 * 6, F32).rearrange("p (c n) -> p c n", c=4)
            mv = tr.alloc(2, F32)
            sd = tr.alloc(1, F32)
            dma('sp', lng, ln_in['ln2_g'][l:l + 1, :].to_broadcast([128, D]), writes=['lng'])
            dma('sp', lnb, ln_in['ln2_b'][l:l + 1, :].to_broadcast([128, D]), writes=['lnb'])
            wc = 0
            wdc = 0
            bk = 0
            for g in range(NG):
                for i in range(4):
                    r0 = g * G + i * 128
                    b = i % 2
                    dma('sp', x1[i], xa[r0:r0 + 128, :], writes=['x1%d' % i])
                    op('pool', lambda e: e.tensor_copy(out=xbf[b], in_=x1[i]), reads=['x1%d' % i], writes=['xbf%d' % b])
                    for hb in range(2):
                        for c in range(8):
                            kc = hb * 8 + c
                            op('pe', lambda e: e.transpose(out=PSB[6 + hb][:, c * 128:(c + 1) * 128], in_=xbf[b][:, kc * 128:(kc + 1) * 128],
                                                           identity=identb[:]),
                               reads=['xbf%d' % b, 'identb'], writes=['ps%d' % (6 + hb)], inc=(c == 7))
                        eng = 'act' if hb == 0 else 'dve'
                        op(eng, lambda e: (e.copy if eng == 'act' else e.tensor_copy)(
                            out=x1T[:, hb * 8:(hb + 1) * 8, i * 128:(i + 1) * 128],
                            in_=PSB[6 + hb].rearrange("p (c n) -> p c n", c=8)), reads=['ps%d' % (6 + hb)], writes=['x1T'])
                for fc in range(NFC):
                    wbf = wc % 2
                    wc += 1
                    dma('sp', wgb[wbf], W('wg', fc * 128, (fc + 1) * 128).rearrange("p (k n) -> p k n", k=16), writes=['wg%d' % wbf])
                    dma('sp', wub[wbf], W('wu', fc * 128, (fc + 1) * 128).rearrange("p (k n) -> p k n", k=16), writes=['wu%d' % wbf])
                    bg = (bk % 3) * 2
                    bu = bg + 1
                    bk += 1
                    mm_group(bg, 128, G, lambda kc: wgb[wbf][:, kc, :], lambda kc: x1T[:, kc, :], 16, ['wg%d' % wbf, 'x1T'])
                    mm_group(bu, 128, G, lambda kc: wub[wbf][:, kc, :], lambda kc: x1T[:, kc, :], 16, ['wu%d' % wbf, 'x1T'])
                    k = fc % 2
                    op('act', lambda e: e.activation(out=sg[k], in_=PS[bg], func=AF.Silu), reads=['ps%d' % bg], writes=['sg%d' % k])
                    op('dve', lambda e: e.tensor_tensor(out=hT[:, fc, :], in0=sg[k], in1=PS[bu], op=ALU.mult),
                       reads=['sg%d' % k, 'ps%d' % bu], writes=['hT'])
                for cbh in range(8):
                    wbf = wdc % 2
                    wdc += 1
                    dma('sp', wdb[wbf], W('wd', cbh * 128, (cbh + 1) * 128).rearrange("p (k n) -> p k n", k=NFC), writes=['wd%d' % wbf])
                    for i in range(4):
                        bank = bk % 6
                        bk += 1
                        mm_group(bank, 128, 256, lambda kc: hT[:, kc, i * 128:(i + 1) * 128], lambda kc: wdb[wbf][:, kc, :], NFC,
                                 ['hT', 'wd%d' % wbf])
                        xsl = x1[i][:, cbh * 256:(cbh + 1) * 256]
                        op('dve', lambda e: e.scalar_tensor_tensor(out=xsl, in0=xsl, scalar=ALPHA, in1=PS[bank][:, 0:256], op0=ALU.mult, op1=ALU.add),
                           reads=['ps%d' % bank, 'x1%d' % i], writes=['x1%d' % i])
                for i in range(4):
                    r0 = g * G + i * 128
                    layer_norm(x1[i], 'x1%d' % i, lng, lnb, stt, mv, sd)
                    dma('pool', x_dst[r0:r0 + 128, :], x1[i], reads=['x1%d' % i])
            tr.barrier()

        layer(0, x_in, xb_d)
        if not any(k.startswith('stop') for k in dbg):
            layer(1, xb_d, out_d)
        tr.barrier()
    return nc


_CACHE = {}


def _host_inputs(inputs):
    f = lambda a: np.asarray(a, dtype=np.float32)
    layers = [_prep_layer(f(inputs['w_in'][l]), f(inputs['w_uq'][l]), f(inputs['w_ukv'][l]), f(inputs['w_o'][l]),
                          f(inputs['w_gate'][l]), f(inputs['w_up'][l]), f(inputs['w_down'][l])) for l in range(DEPTH)]
    shared = {k: np.ascontiguousarray(np.concatenate([layers[l][k] for l in range(DEPTH)], axis=0)) for k in WSHAPES}
    shared.update(_consts())
    shared['g_cq_t'] = np.ascontiguousarray(f(inputs['g_cq']).reshape(DEPTH, 4, 128).transpose(0, 2, 1)).reshape(DEPTH * 128, 4)
    shared['g_ckv_t'] = np.ascontiguousarray(f(inputs['g_ckv']).reshape(DEPTH, 2, 128).transpose(0, 2, 1)).reshape(DEPTH * 128, 2)
    for k in ('ln1_g', 'ln1_b', 'ln2_g', 'ln2_b'):
        shared[k] = f(inputs[k])
    return shared


def kernel(**inputs):
    x = np.asarray(inputs['x'], dtype=np.float32)
    pos = np.asarray(inputs['positions']).astype(np.int32)
    shared = _host_inputs(inputs)
    if 'nc' not in _CACHE:
        _CACHE['nc'] = build_program()
    nc = _CACHE['nc']
    in_maps = []
    for b in range(4):
        m = dict(shared)
        m['x'] = np.ascontiguousarray(x[b])
        m['positions'] = np.ascontiguousarray(pos[b:b + 1])
        in_maps.append(m)
    res = run_bass_kernel_spmd(nc, in_maps, core_ids=list(range(4)))
    out = np.stack([np.asarray(res.results[b]['out'], dtype=np.float32) for b in range(4)], axis=0)
    return out
```
